# Optimizing a Trainium2 kernel written in Bass

```python
import math
import jax
import jax.numpy as jnp
from jax import lax
import numpy as np

D_MODEL = 2048
BATCH = 32
SEQ = 256
DEPTH = 4
DEC_BATCH = 8
DEC_SEQ = 4096
PAST_LEN = 512

GRID_W = 64
CHUNK = 64
N_DIR = 2
N_EVEN = (DEPTH + 1) // 2
N_ODD = DEPTH // 2
MIX_W = D_MODEL
GLA_W = MIX_W // 2
GLA_HEADS = 4
GLA_DK = GLA_W // (2 * GLA_HEADS)
GLA_DV = GLA_W // GLA_HEADS
GLA_LORA = 16
GLA_TAU = 16.0
RWKV_W = MIX_W - GLA_W
RWKV_N = 64
RWKV_HEADS = RWKV_W // RWKV_N
RWKV_LORA = 64
GDN_HEADS = 16
GDN_DK = 128
GDN_DV = MIX_W // GDN_HEADS
CONV_K = 3
EPS = 1e-6
GN_EPS = 64e-5

EVEN_SIZES = (GLA_HEADS * GLA_DK, GLA_HEADS * GLA_DK, GLA_W, GLA_W, N_DIR * GLA_LORA,
              RWKV_W, RWKV_W, RWKV_W, RWKV_W, N_DIR * RWKV_LORA, N_DIR * RWKV_LORA)
ODD_SIZES = (GDN_HEADS * GDN_DK, GDN_HEADS * GDN_DK, MIX_W, MIX_W, N_DIR * GDN_HEADS, N_DIR * GDN_HEADS)
EVEN_IN = sum(EVEN_SIZES)
ODD_IN = sum(ODD_SIZES)
GDN_QKV = 2 * GDN_HEADS * GDN_DK + MIX_W

kernel_name = 'bidir_gla_rwkv7_gdn_prefix_diffusion_step'


def _split(y, sizes):
    return jnp.split(y, [int(i) for i in np.cumsum(sizes)[:-1]], axis=-1)


def _rms(x, g):
    xf = x.astype(jnp.float32)
    return xf * lax.rsqrt(jnp.mean(xf * xf, axis=-1, keepdims=True) + EPS) * g.astype(jnp.float32)


def _l2n(x):
    xf = x.astype(jnp.float32)
    return xf * lax.rsqrt(jnp.sum(xf * xf, axis=-1, keepdims=True) + EPS)


def _modulate(x, g, w_ada, b_ada, cvec):
    mod = (jax.nn.silu(cvec) @ w_ada + b_ada).astype(jnp.float32)
    shift, scale, gate = jnp.split(mod, 3, axis=-1)
    h = _rms(x, g) * (1.0 + scale[:, None]) + shift[:, None]
    return h.astype(x.dtype), gate[:, None]


def _neighbours(x, n_seg):
    b, t, ch = x.shape
    xs = jnp.pad(x.reshape(b, n_seg, t // n_seg, ch), ((0, 0), (0, 0), (1, 1), (0, 0)))
    return xs[:, :, :-2].reshape(b, t, ch), xs[:, :, 2:].reshape(b, t, ch)


def _to_cols(x, rows):
    b, t, ch = x.shape
    return x.reshape(b, rows, GRID_W, ch).transpose(0, 2, 1, 3).reshape(b, t, ch)


def _to_rows(x, rows):
    b, t, ch = x.shape
    return x.reshape(b, GRID_W, rows, ch).transpose(0, 2, 1, 3).reshape(b, t, ch)


def _chunks(a):
    b, t, h = a.shape[:3]
    a = a.reshape((b, t // CHUNK, CHUNK, h) + a.shape[3:])
    return jnp.transpose(a, (1, 0, 3, 2) + tuple(range(4, a.ndim)))


def _unchunks(a):
    a = jnp.transpose(a, (1, 0, 3, 2) + tuple(range(4, a.ndim)))
    b, n, l, h = a.shape[:4]
    return a.reshape((b, n * l, h) + a.shape[4:])


def _run_dir(fn, ins, s0, d):
    if d == 1:
        ins = tuple(jnp.flip(a, axis=1) for a in ins)
    o, s = fn(*ins, s0)
    return (jnp.flip(o, axis=1) if d == 1 else o), s


def _gla_chunked(q, k, v, log_a, s0):
    lower = jnp.tril(jnp.ones((CHUNK, CHUNK), bool))

    def step(s, xs):
        qc, kc, vc, lc = xs
        b = jnp.cumsum(lc, axis=2)
        bl = b[:, :, -1:]
        qd = qc * jnp.exp(b)
        att = jnp.where(lower, jnp.einsum('bhlk,bhsk->bhls', qd, kc * jnp.exp(-b)), 0.0)
        o = jnp.einsum('bhls,bhsv->bhlv', att, vc) + jnp.einsum('bhlk,bhkv->bhlv', qd, s)
        s = jnp.exp(bl)[:, :, 0, :, None] * s + jnp.einsum('bhlk,bhlv->bhkv', kc * jnp.exp(bl - b), vc)
        return s, o

    s_fin, o = lax.scan(step, s0, tuple(_chunks(a) for a in (q, k, v, log_a)))
    return _unchunks(o), s_fin


def _gdn_chunked(q, k, v, g, beta, s0):
    lower = jnp.tril(jnp.ones((CHUNK, CHUNK), bool))
    strict = jnp.tril(jnp.ones((CHUNK, CHUNK), bool), -1)
    eye = jnp.eye(CHUNK, dtype=jnp.float32)
    dv = v.shape[-1]

    def step(s, xs):
        qc, kc, vc, gch, bc = xs
        gc = jnp.cumsum(gch, axis=-1)
        gl = gc[..., -1:]
        dec = jnp.exp(jnp.where(lower, gc[..., :, None] - gc[..., None, :], -jnp.inf))
        kk = jnp.einsum('bhlk,bhsk->bhls', kc, kc)
        lmat = eye + jnp.where(strict, bc[..., :, None] * dec * kk, 0.0)
        rhs = jnp.concatenate([vc * bc[..., None], kc * (bc * jnp.exp(gc))[..., None]], axis=-1)
        sol = lax.linalg.triangular_solve(lmat, rhs, left_side=True, lower=True, unit_diagonal=True)
        u = sol[..., :dv] - jnp.einsum('bhlk,bhkv->bhlv', sol[..., dv:], s)
        att = dec * jnp.einsum('bhlk,bhsk->bhls', qc, kc)
        o = jnp.einsum('bhlk,bhkv->bhlv', qc * jnp.exp(gc)[..., None], s) + jnp.einsum('bhls,bhsv->bhlv', att, u)
        s = jnp.exp(gl)[..., None] * s + jnp.einsum('bhlk,bhlv->bhkv', kc * jnp.exp(gl - gc)[..., None], u)
        return s, o

    s_fin, o = lax.scan(step, s0, tuple(_chunks(a) for a in (q, k, v, g, beta)))
    return _unchunks(o), s_fin


def _rwkv_scan(r, w, kk, a, k, v, s0):
    def step(s, xs):
        rt, wt, kkt, at, kt, vt = xs
        sa = jnp.einsum('bhij,bhj->bhi', s, kkt)
        s = s * wt[:, :, None, :] - sa[..., None] * (kkt * at)[:, :, None, :] + vt[..., None] * kt[:, :, None, :]
        return s, jnp.einsum('bhij,bhj->bhi', s, rt)

    s_fin, y = lax.scan(step, s0, tuple(jnp.moveaxis(x, 1, 0) for x in (r, w, kk, a, k, v)))
    return jnp.moveaxis(y, 0, 1), s_fin


def _even_mixer(h, w_in, gla_w2, gla_b, gla_g, rwkv_mu, rwkv_w0, rwkv_w2, rwkv_a0, rwkv_a2,
                rwkv_k_k, rwkv_k_a, rwkv_r_k, rwkv_gn_g, rwkv_gn_b, s_gla, s_rwkv, n_seg):
    f32 = jnp.float32
    bsz, t, _ = h.shape
    gq, gk, gv, gz, glora, rr, rk, rv, rz, rwl, ral = _split(h @ w_in, EVEN_SIZES)
    q = gq.astype(f32).reshape(bsz, t, GLA_HEADS, GLA_DK) * GLA_DK ** -0.5
    k = gk.astype(f32).reshape(bsz, t, GLA_HEADS, GLA_DK)
    v = gv.astype(f32).reshape(bsz, t, GLA_HEADS, GLA_DV)
    glora = glora.astype(f32).reshape(bsz, t, N_DIR, GLA_LORA)
    o_gla, st_gla = 0.0, []
    for d in range(N_DIR):
        logit = glora[:, :, d] @ gla_w2[d].astype(f32) + gla_b[d].astype(f32)
        log_a = (jax.nn.log_sigmoid(logit) / GLA_TAU).reshape(bsz, t, GLA_HEADS, GLA_DK)
        o, s = _run_dir(_gla_chunked, (q, k, v, log_a), s_gla[:, d].astype(f32), d)
        o_gla = o_gla + o
        st_gla.append(s)
    o_gla = _rms(o_gla, gla_g).reshape(bsz, t, GLA_W) * jax.nn.silu(gz.astype(f32))
    def shift(x, i):
        x = x.astype(f32)
        prev, nxt = _neighbours(x, n_seg)
        return x + rwkv_mu[i, 0] * (prev - x) + rwkv_mu[i, 1] * (nxt - x)
    hd = (bsz, t, RWKV_HEADS, RWKV_N)
    r = shift(rr, 0).reshape(hd)
    kb = shift(rk, 1)
    v = shift(rv, 2).reshape(hd)
    kk = _l2n((kb * rwkv_k_k).reshape(hd))
    rwl = jnp.tanh(rwl.astype(f32)).reshape(bsz, t, N_DIR, RWKV_LORA)
    ral = ral.astype(f32).reshape(bsz, t, N_DIR, RWKV_LORA)
    y_sum, bonus, st_rwkv = 0.0, 0.0, []
    for d in range(N_DIR):
        w = -jax.nn.softplus(-(rwkv_w0[d] + rwl[:, :, d] @ rwkv_w2[d])) - 0.5
        decay = jnp.exp(-jnp.exp(w)).reshape(hd)
        a = jax.nn.sigmoid(rwkv_a0[d] + ral[:, :, d] @ rwkv_a2[d])
        kd = (kb * (1.0 + (a - 1.0) * rwkv_k_a)).reshape(hd)
        y, s = _run_dir(_rwkv_scan, (r, decay, kk, a.reshape(hd), kd, v), s_rwkv[:, d].astype(f32), d)
        y_sum = y_sum + y
        bonus = bonus + jnp.sum(r * kd * rwkv_r_k, axis=-1, keepdims=True) * v
        st_rwkv.append(s)
    mu = jnp.mean(y_sum, axis=-1, keepdims=True)
    var = jnp.mean(jnp.square(y_sum - mu), axis=-1, keepdims=True)
    yn = ((y_sum - mu) * lax.rsqrt(var + GN_EPS)).reshape(bsz, t, RWKV_W) * rwkv_gn_g + rwkv_gn_b
    o_rwkv = (yn + bonus.reshape(bsz, t, RWKV_W)) * jax.nn.silu(rz.astype(f32))
    mix = jnp.concatenate([o_gla, o_rwkv], axis=-1).astype(h.dtype)
    return mix, jnp.stack(st_gla, axis=1), jnp.stack(st_rwkv, axis=1)


def _odd_mixer(h, w_in, gdn_conv_w, gdn_A_log, gdn_dt_bias, gdn_g, s_gdn, n_seg):
    f32 = jnp.float32
    bsz, t, _ = h.shape
    q, k, v, z, bl, al = _split(h @ w_in, ODD_SIZES)
    qkv = jnp.concatenate([q, k, v], axis=-1).astype(f32)
    prev, nxt = _neighbours(qkv, n_seg)
    cw = gdn_conv_w.astype(f32)
    qkv = jax.nn.silu(cw[0] * prev + cw[1] * qkv + cw[2] * nxt)
    q, k, v = _split(qkv, (GDN_HEADS * GDN_DK, GDN_HEADS * GDN_DK, MIX_W))
    q = _l2n(q.reshape(bsz, t, GDN_HEADS, GDN_DK)) * GDN_DK ** -0.5
    k = _l2n(k.reshape(bsz, t, GDN_HEADS, GDN_DK))
    v = v.reshape(bsz, t, GDN_HEADS, GDN_DV)
    bl = bl.astype(f32).reshape(bsz, t, N_DIR, GDN_HEADS)
    al = al.astype(f32).reshape(bsz, t, N_DIR, GDN_HEADS)
    o_sum, st = 0.0, []
    for d in range(N_DIR):
        beta = jax.nn.sigmoid(bl[:, :, d])
        g = -jnp.exp(gdn_A_log[d].astype(f32)) * jax.nn.softplus(al[:, :, d] + gdn_dt_bias[d])
        o, s = _run_dir(_gdn_chunked, (q, k, v, g, beta), s_gdn[:, d].astype(f32), d)
        o_sum = o_sum + o
        st.append(s)
    o = _rms(o_sum, gdn_g).reshape(bsz, t, MIX_W) * jax.nn.silu(z.astype(f32))
    return o.astype(h.dtype), jnp.stack(st, axis=1)


def setup_inputs(seed: int = 0) -> dict:
    key = jax.random.key(seed)
    keys = iter(jax.random.split(key, 40))

    def nrm(shape, scale):
        return jax.random.normal(next(keys), shape, jnp.float32) * scale

    def unif(shape, lo, hi):
        return jax.random.uniform(next(keys), shape, jnp.float32, lo, hi)

    dt = jnp.exp(unif((N_ODD, N_DIR, GDN_HEADS), math.log(1e-3), math.log(1e-1)))
    return {
        'x_prompt': nrm((BATCH, SEQ, D_MODEL), 1.0),
        'x_sample': nrm((DEC_BATCH, DEC_SEQ, D_MODEL), 1.0),
        'state_gla': nrm((DEC_BATCH, N_EVEN, N_DIR, GLA_HEADS, GLA_DK, GLA_DV), 1.0),
        'state_rwkv': nrm((DEC_BATCH, N_EVEN, N_DIR, RWKV_HEADS, RWKV_N, RWKV_N), 0.5),
        'state_gdn': nrm((DEC_BATCH, N_ODD, N_DIR, GDN_HEADS, GDN_DK, GDN_DV), 0.1),
        'c': nrm((DEC_BATCH, D_MODEL), 1.0),
        'c_ctx': nrm((D_MODEL,), 1.0),
        'norm_g': 1.0 + nrm((DEPTH, D_MODEL), 0.02),
        'w_ada': nrm((DEPTH, D_MODEL, 3 * D_MODEL), 0.5 * D_MODEL ** -0.5),
        'b_ada': nrm((DEPTH, 3 * D_MODEL), 0.02),
        'w_in_even': nrm((N_EVEN, D_MODEL, EVEN_IN), D_MODEL ** -0.5),
        'w_in_odd': nrm((N_ODD, D_MODEL, ODD_IN), D_MODEL ** -0.5),
        'w_out': nrm((DEPTH, MIX_W, D_MODEL), MIX_W ** -0.5),
        'gla_w2': nrm((N_EVEN, N_DIR, GLA_LORA, GLA_HEADS * GLA_DK), GLA_LORA ** -0.5),
        'gla_b': nrm((N_EVEN, N_DIR, GLA_HEADS * GLA_DK), 1.0),
        'gla_g': 1.0 + nrm((N_EVEN, GLA_DV), 0.02),
        'rwkv_mu': unif((N_EVEN, 3, 2, RWKV_W), 0.0, 0.5),
        'rwkv_w0': unif((N_EVEN, N_DIR, RWKV_W), -6.0, 1.0),
        'rwkv_w2': nrm((N_EVEN, N_DIR, RWKV_LORA, RWKV_W), 0.1),
        'rwkv_a0': nrm((N_EVEN, N_DIR, RWKV_W), 0.5),
        'rwkv_a2': nrm((N_EVEN, N_DIR, RWKV_LORA, RWKV_W), 0.1),
        'rwkv_k_k': 0.85 + nrm((N_EVEN, RWKV_W), 0.02),
        'rwkv_k_a': 1.0 + nrm((N_EVEN, RWKV_W), 0.02),
        'rwkv_r_k': nrm((N_EVEN, RWKV_HEADS, RWKV_N), 0.1),
        'rwkv_gn_g': 1.0 + nrm((N_EVEN, RWKV_W), 0.02),
        'rwkv_gn_b': nrm((N_EVEN, RWKV_W), 0.02),
        'gdn_conv_w': nrm((N_ODD, CONV_K, GDN_QKV), CONV_K ** -0.5),
        'gdn_A_log': jnp.log(unif((N_ODD, N_DIR, GDN_HEADS), 1.0, 16.0)),
        'gdn_dt_bias': dt + jnp.log(-jnp.expm1(-dt)),
        'gdn_g': 1.0 + nrm((N_ODD, GDN_DV), 0.02),
        'final_g': 1.0 + nrm((D_MODEL,), 0.02),
    }


def reference(x_prompt, x_sample, state_gla, state_rwkv, state_gdn, c, c_ctx, norm_g, w_ada, b_ada,
              w_in_even, w_in_odd, w_out, gla_w2, gla_b, gla_g, rwkv_mu, rwkv_w0, rwkv_w2, rwkv_a0,
              rwkv_a2, rwkv_k_k, rwkv_k_a, rwkv_r_k, rwkv_gn_g, rwkv_gn_b, gdn_conv_w, gdn_A_log,
              gdn_dt_bias, gdn_g, final_g):
    f32 = jnp.float32
    bp = x_prompt.shape[0]
    rows = x_sample.shape[1] // GRID_W
    xp, xs = x_prompt, x_sample
    new_gla, new_rwkv, new_gdn = [], [], []
    for i in range(DEPTH):
        j = i // 2
        hp, gate_p = _modulate(xp, norm_g[i], w_ada[i], b_ada[i], c_ctx[None, :])
        hs, gate_s = _modulate(xs, norm_g[i], w_ada[i], b_ada[i], c)
        if i % 2 == 0:
            prm = (w_in_even[j], gla_w2[j], gla_b[j], gla_g[j], rwkv_mu[j], rwkv_w0[j], rwkv_w2[j],
                   rwkv_a0[j], rwkv_a2[j], rwkv_k_k[j], rwkv_k_a[j], rwkv_r_k[j], rwkv_gn_g[j], rwkv_gn_b[j])
            zg = jnp.zeros((bp,) + state_gla.shape[2:], f32)
            zr = jnp.zeros((bp,) + state_rwkv.shape[2:], f32)
            mp, sg, sr = _even_mixer(hp, *prm, zg, zr, 1)
            ms, _, _ = _even_mixer(hs, *prm, state_gla[:, j], state_rwkv[:, j], rows)
            new_gla.append(sg)
            new_rwkv.append(sr)
        else:
            prm = (w_in_odd[j], gdn_conv_w[j], gdn_A_log[j], gdn_dt_bias[j], gdn_g[j])
            zd = jnp.zeros((bp,) + state_gdn.shape[2:], f32)
            mp, sd = _odd_mixer(hp, *prm, zd, 1)
            ms, _ = _odd_mixer(_to_cols(hs, rows), *prm, state_gdn[:, j], GRID_W)
            ms = _to_rows(ms, rows)
            new_gdn.append(sd)
        xp = xp + (gate_p * (mp @ w_out[i])).astype(xp.dtype)
        xs = xs + (gate_s * (ms @ w_out[i])).astype(xs.dtype)
    y_prompt = _rms(xp, final_g).astype(x_prompt.dtype)
    y_sample = _rms(xs, final_g).astype(x_sample.dtype)
    new_state_gla = jnp.stack(new_gla, axis=1).astype(x_prompt.dtype)
    new_state_rwkv = jnp.stack(new_rwkv, axis=1).astype(x_prompt.dtype)
    new_state_gdn = jnp.stack(new_gdn, axis=1).astype(x_prompt.dtype)
    return (y_prompt, y_sample, new_state_gla, new_state_rwkv, new_state_gdn)
```

```python
from contextlib import ExitStack
import numpy as np
import concourse.bass as bass
import concourse.mybir as mybir
from concourse.bass_utils import run_bass_kernel_spmd

F32 = mybir.dt.float32
BF16 = mybir.dt.bfloat16
AF = mybir.ActivationFunctionType
ALU = mybir.AluOpType
AX = mybir.AxisListType

import os
DBG = int(os.environ.get("KDBG", "0"))
SELF_SYNC = True
NDMASEM = 12


class T:
    __slots__ = ("ap", "w", "r", "war", "name", "const", "fw")

    def __init__(self, ap, name="", const=False):
        self.ap = ap
        self.w = []
        self.r = []
        self.war = []
        self.name = name
        self.const = const
        self.fw = None

    def __getitem__(self, idx):
        return self.ap[idx]


class TV:
    def __init__(self, base, ap):
        object.__setattr__(self, "base", base)
        object.__setattr__(self, "ap", ap)

    def __getattr__(self, nm):
        return getattr(self.base, nm)

    def __setattr__(self, nm, v):
        setattr(self.base, nm, v)

    def __getitem__(self, idx):
        return self.ap[idx]


class Op:
    __slots__ = ("eng", "fn", "deps", "signal", "is_dma", "sem", "val", "pos", "prewait")

    def __init__(self, eng, fn, is_dma):
        self.eng = eng
        self.fn = fn
        self.is_dma = is_dma
        self.deps = []
        self.signal = is_dma
        self.sem = None
        self.val = 0
        self.pos = 0
        self.prewait = None


class Sched:
    ENGS = ("pe", "dve", "act", "pool", "sp")

    def __init__(self, nc):
        self.nc = nc
        self.q = {e: [] for e in self.ENGS}
        self.last = {e: None for e in self.ENGS}
        self.dmas_since_barrier = []
        self.nops = 0

    def op(self, eng, fn, reads=(), writes=(), pwrites=(), dma=False):
        o = Op(eng, fn, dma)
        deps = o.deps
        for t in reads:
            if t.w:
                deps.extend(t.w)
        for t in writes:
            deps.extend(t.w)
            deps.extend(t.r)
            deps.extend(t.war)
        for t in pwrites:
            if t.r:
                t.war = t.r + t.w
                t.r = []
                t.w = []
            deps.extend(t.war)
            if t.fw is not None:
                deps.append(t.fw)
        for t in reads:
            if not t.const:
                t.r.append(o)
        for t in writes:
            t.war = t.r + t.w
            t.w = [o]
            t.r = []
            t.fw = o
        for t in pwrites:
            t.w.append(o)
        o.pos = len(self.q[eng])
        self.q[eng].append(o)
        self.last[eng] = o
        if dma:
            self.dmas_since_barrier.append(o)
        self.nops += 1
        return o

    def dma(self, eng, out_t, out_ap, in_t, in_ap, partial=False, **kw):
        def fn(e):
            return e.dma_start(out=out_ap, in_=in_ap, **kw)
        if partial:
            return self.op(eng, fn, reads=(in_t,), pwrites=(out_t,), dma=True)
        return self.op(eng, fn, reads=(in_t,), writes=(out_t,), dma=True)

    def barrier(self):
        lasts = [self.last[e] for e in self.ENGS if self.last[e] is not None]
        dm = list(self.dmas_since_barrier)
        self.dmas_since_barrier = []
        for e in self.ENGS:
            o = Op(e, None, False)
            o.deps = [x for x in lasts if x.eng != e and not x.is_dma] + dm
            o.pos = len(self.q[e])
            self.q[e].append(o)

    def emit(self):
        nc = self.nc
        for e in self.ENGS:
            for o in self.q[e]:
                for d in o.deps:
                    if d.is_dma:
                        continue
                    if d.eng != o.eng:
                        d.signal = True
                    elif o.eng != "pe" and (SELF_SYNC or o.is_dma):
                        d.signal = True
        esem = {e: nc.alloc_semaphore("sem_" + e) for e in self.ENGS}
        dsem = {e: [nc.alloc_semaphore("dsem_%s_%d" % (e, i)) for i in range(NDMASEM)]
                for e in self.ENGS if any(o.is_dma for o in self.q[e])}
        for e in self.ENGS:
            cnt = 0
            nd = 0
            for o in self.q[e]:
                if o.is_dma:
                    o.sem = dsem[e][nd % NDMASEM]
                    o.val = 16 * (nd // NDMASEM + 1)
                    if nd >= NDMASEM:
                        o.prewait = (o.sem, 16 * (nd // NDMASEM))
                    nd += 1
                elif o.signal:
                    cnt += 1
                    o.sem = esem[e]
                    o.val = cnt
        engobj = {"pe": "tensor", "dve": "vector", "act": "scalar", "pool": "gpsimd", "sp": "sync"}
        with nc.Block() as block:
            for e in self.ENGS:
                ops = self.q[e]
                if not ops:
                    continue

                def body(eng, ops=ops, e=e):
                    waited = {}
                    for o in ops:
                        need = {}
                        if o.prewait is not None:
                            need[id(o.prewait[0])] = (o.prewait[0], o.prewait[1])
                        for d in o.deps:
                            if (not d.is_dma) and d.eng == e:
                                if e == "pe" or not d.signal:
                                    continue
                            k = id(d.sem)
                            if k not in need or need[k][1] < d.val:
                                need[k] = (d.sem, d.val)
                        for k, (sem, val) in need.items():
                            if waited.get(k, 0) < val:
                                eng.wait_ge(sem, val)
                                waited[k] = val
                        if o.fn is None:
                            continue
                        ins = o.fn(eng)
                        if o.is_dma:
                            ins.then_inc(o.sem, 16)
                        elif o.signal:
                            ins.then_inc(o.sem, 1)
                    fin = {}
                    for o in ops:
                        if o.is_dma:
                            fin[id(o.sem)] = (o.sem, o.val)
                    for k, (sem, val) in fin.items():
                        if waited.get(k, 0) < val:
                            eng.wait_ge(sem, val)
                            waited[k] = val

                getattr(block, engobj[e])(body)


D = 2048
GRID_W = 64
CH = 64
EVEN_SIZES = (512, 512, 1024, 1024, 32, 1024, 1024, 1024, 1024, 128, 128)
ODD_SIZES = (2048, 2048, 2048, 2048, 32, 32)
EVEN_IN = sum(EVEN_SIZES)
ODD_IN = sum(ODD_SIZES)
EPS = 1e-6
GN_EPS = 64e-5


class Cfg:
    def __init__(self, depth=4, ts=4096, np_=4, tp=256):
        self.depth = depth
        self.ts = ts
        self.np = np_
        self.tp = tp
        self.ntok = ts + np_ * tp
        assert self.ntok % 128 == 0 and ts % 128 == 0
        self.ntile = self.ntok // 128
        self.n_even = (depth + 1) // 2
        self.n_odd = depth // 2


def make_consts():
    c = {}
    c["ident"] = np.eye(128, dtype=np.float32)
    sel = np.zeros((2, 2, 128), np.float32)
    sel[0, 0, :] = 1.0
    sel[1, 1, :] = 1.0
    c["sel"] = sel
    i = np.arange(64)
    le0 = (i[:, None] <= i[None, :]).astype(np.float32)
    lt0 = (i[:, None] < i[None, :]).astype(np.float32)
    c["masks"] = np.stack([le0, le0.T.copy(), lt0, lt0.T.copy()]).astype(np.float32)
    c["zeros"] = np.zeros((1, ODD_IN), np.float32)
    blk = lambda n: ((i[:, None] // n) == (i[None, :] // n)).astype(np.float32)
    c["bmask"] = np.stack([blk(8), blk(16) - blk(8), blk(32) - blk(16), 1.0 - blk(32)]).astype(np.float32)
    return c


class K:
    def __init__(self, cfg):
        self.cfg = cfg
        self.nc = bass.Bass("TRN2", target_bir_lowering=False)
        self.s = Sched(self.nc)
        self.es = ExitStack()
        self.uid = 0

    def dram(self, name, shape, kind, dt=F32):
        h = self.nc.dram_tensor(name, list(shape), dt, kind=kind)
        return T(h.ap(), name)

    def sb(self, stack, shape, dt=F32, name=None, const=False):
        self.uid += 1
        nm = "%s_%d" % (name or "t", self.uid)
        ap = stack.enter_context(self.nc.sbuf_tensor(nm, list(shape), dt))
        return T(ap, nm, const=const)

    def ps(self, stack, shape, dt=F32, name=None):
        self.uid += 1
        nm = "%s_%d" % (name or "p", self.uid)
        ap = stack.enter_context(self.nc.psum_tensor(nm, list(shape), dt))
        return T(ap, nm)

    def mm(self, out_t, out_ap, l_t, l_ap, r_t, r_ap, start=True, stop=True, acc=False):
        def fn(e):
            return e.matmul(out_ap, l_ap, r_ap, start=start, stop=stop)
        return self.s.op("pe", fn, reads=(l_t, r_t), writes=(out_t,)) if not acc else \
            self.s.op("pe", fn, reads=(l_t, r_t), pwrites=(out_t,))

    def tr(self, out_t, out_ap, in_t, in_ap, id_t, id_ap, partial=True):
        def fn(e):
            return e.transpose(out_ap, in_ap, id_ap)
        if partial:
            return self.s.op("pe", fn, reads=(in_t, id_t), pwrites=(out_t,))
        return self.s.op("pe", fn, reads=(in_t, id_t), writes=(out_t,))

    def act(self, out_t, out_ap, in_t, in_ap, func, bias=None, scale=None, accum=None, extra_reads=(),
            partial=False, eng="act"):
        kw = {}
        if bias is not None:
            kw["bias"] = bias
        if scale is not None:
            kw["scale"] = scale
        wr = [out_t]
        if accum is not None:
            kw["accum_out"] = accum[1]
            wr.append(accum[0])

        def fn(e):
            return e.activation(out_ap, in_ap, func, **kw)
        if partial:
            return self.s.op(eng, fn, reads=(in_t,) + tuple(extra_reads), pwrites=tuple(wr))
        return self.s.op(eng, fn, reads=(in_t,) + tuple(extra_reads), writes=tuple(wr))

    def tsc(self, eng, out_t, out_ap, in_t, in_ap, s1, s2, op0, op1=None, extra_reads=(), partial=False):
        def fn(e):
            if op1 is None:
                return e.tensor_scalar(out_ap, in_ap, s1, None, op0)
            return e.tensor_scalar(out_ap, in_ap, s1, s2, op0, op1)
        if partial:
            return self.s.op(eng, fn, reads=(in_t,) + tuple(extra_reads), pwrites=(out_t,))
        return self.s.op(eng, fn, reads=(in_t,) + tuple(extra_reads), writes=(out_t,))

    def tt(self, eng, out_t, out_ap, a_t, a_ap, b_t, b_ap, op, partial=False):
        def fn(e):
            return e.tensor_tensor(out_ap, a_ap, b_ap, op)
        if partial:
            return self.s.op(eng, fn, reads=(a_t, b_t), pwrites=(out_t,))
        return self.s.op(eng, fn, reads=(a_t, b_t), writes=(out_t,))

    def stt(self, out_t, out_ap, a_t, a_ap, scalar, b_t, b_ap, op0, op1, extra_reads=(), partial=False):
        def fn(e):
            return e.scalar_tensor_tensor(out_ap, a_ap, scalar, b_ap, op0, op1)
        if partial:
            return self.s.op("dve", fn, reads=(a_t, b_t) + tuple(extra_reads), pwrites=(out_t,))
        return self.s.op("dve", fn, reads=(a_t, b_t) + tuple(extra_reads), writes=(out_t,))

    def cp(self, eng, out_t, out_ap, in_t, in_ap, partial=False):
        if eng == "act":
            def fn(e):
                return e.copy(out_ap, in_ap)
        else:
            def fn(e):
                return e.tensor_copy(out_ap, in_ap)
        if partial:
            return self.s.op(eng, fn, reads=(in_t,), pwrites=(out_t,))
        return self.s.op(eng, fn, reads=(in_t,), writes=(out_t,))

    def memset(self, eng, t, ap, val):
        def fn(e):
            return e.memset(ap, val)
        return self.s.op(eng, fn, writes=(t,))

    def ld(self, out_t, out_ap, in_t, in_ap, eng="sp", partial=False, **kw):
        return self.s.dma(eng, out_t, out_ap, in_t, in_ap, partial=partial, **kw)


class StopBuild(Exception):
    pass


_CKPT = [0]
KSTOPG = int(os.environ.get("KSTOPG", "0"))


def ckpt():
    _CKPT[0] += 1
    if KSTOPG and _CKPT[0] >= KSTOPG:
        raise StopBuild()


def build(cfg, mixer_mode="full", stop=None):
    k = K(cfg)
    try:
        _build(k, cfg, mixer_mode, stop)
    except StopBuild:
        pass
    k.s.emit()
    return k


def _build(k, cfg, mixer_mode, stop):
    def stage(name):
        if stop == name:
            raise StopBuild()
    nc = k.nc
    s = k.s
    NT = cfg.ntile
    depth = cfg.depth
    xin = k.dram("xin", [cfg.ntok, D], "ExternalInput")
    cvec = k.dram("cvec", [2, D], "ExternalInput")
    norm_g = k.dram("norm_g", [depth, D], "ExternalInput")
    w_ada = k.dram("w_ada", [depth, D, 3 * D], "ExternalInput")
    b_ada = k.dram("b_ada", [depth, 3 * D], "ExternalInput")
    w_in_even = k.dram("w_in_even", [cfg.n_even, D, EVEN_IN], "ExternalInput")
    w_in_odd = k.dram("w_in_odd", [max(cfg.n_odd, 1), D, ODD_IN], "ExternalInput")
    w_out = k.dram("w_out", [depth, D, D], "ExternalInput")
    final_g = k.dram("final_g", [1, D], "ExternalInput")
    ident_d = k.dram("ident", [128, 128], "ExternalInput")
    sel_d = k.dram("sel", [2, 2, 128], "ExternalInput")
    masks_d = k.dram("masks", [4, 64, 64], "ExternalInput")
    bmask_d = k.dram("bmask", [4, 64, 64], "ExternalInput")
    zeros = k.dram("zeros", [1, ODD_IN], "ExternalInput")
    ne, no_ = cfg.n_even, max(cfg.n_odd, 1)
    P_ = {}
    for nm, shp in (("state_gla", [ne, 2, 4, 128, 256]), ("state_rwkv", [ne, 2, 16, 64, 64]),
                    ("state_gdn", [no_, 2, 16, 128, 128]),
                    ("gla_w2", [ne, 2, 16, 512]), ("gla_b", [ne, 2, 512]), ("gla_g", [ne, 256]),
                    ("rwkv_mu", [ne, 3, 2, 1024]), ("rwkv_w0", [ne, 2, 1024]), ("rwkv_w2", [ne, 2, 64, 1024]),
                    ("rwkv_a0", [ne, 2, 1024]), ("rwkv_a2", [ne, 2, 64, 1024]), ("rwkv_k_k", [ne, 1024]),
                    ("rwkv_k_a", [ne, 1024]), ("rwkv_r_k", [ne, 1024]), ("rwkv_gn_g", [ne, 1024]),
                    ("rwkv_gn_b", [ne, 1024]), ("gdn_conv_w", [no_, 3, 6144]), ("gdn_A_log", [no_, 2, 16]),
                    ("gdn_dt_bias", [no_, 2, 16]), ("gdn_g", [no_, 128])):
        P_[nm] = k.dram(nm, shp, "ExternalInput")
    P_["ns_gla"] = k.dram("ns_gla", [cfg.np, ne, 2, 4, 128, 256], "ExternalOutput")
    P_["ns_rwkv"] = k.dram("ns_rwkv", [cfg.np, ne, 2, 16, 64, 64], "ExternalOutput")
    P_["ns_gdn"] = k.dram("ns_gdn", [cfg.np, no_, 2, 16, 128, 128], "ExternalOutput")
    P_["ogla"] = k.dram("ogla", [cfg.ntok, 1024], "Internal")
    P_["orw"] = k.dram("orw", [cfg.ntok, 1024], "Internal")
    P_["ogd"] = k.dram("ogd", [cfg.ntok, 2048], "Internal")
    P_["bsum"] = k.dram("bsum", [cfg.ntok, 16], "Internal")
    P_["zeros"] = zeros
    yout = k.dram("y", [cfg.ntok, D], "ExternalOutput")
    xcur = [k.dram("xcur%d" % i, [cfg.ntok, D], "Internal") for i in range(2)]
    proj = k.dram("proj", [cfg.ntok, ODD_IN], "Internal")
    mixd = k.dram("mixd", [cfg.ntok, D], "Internal", BF16)
    def rowtiles(t, n):
        return [T(t.ap, "%s_r%d" % (t.name, i)) for i in range(n)]
    xcur_rt = [rowtiles(x, NT) for x in xcur]
    proj_rt = rowtiles(proj, NT)
    mix_rt = rowtiles(mixd, NT)
    yout_rt = rowtiles(yout, NT)

    def cond_of_tile(tt):
        return 0 if tt * 128 < cfg.ts else 1

    with ExitStack() as glob:
        ident = k.sb(glob, [128, 128], F32, "ident", const=True)
        k.ld(ident, ident[:, :], ident_d, ident_d[:, :])
        gT = k.sb(glob, [128, depth * 16], F32, "gT")
        bT = k.sb(glob, [128, depth * 48], F32, "bT")
        sT = k.sb(glob, [128, 16, 2], F32, "sT")
        fg_bc = k.sb(glob, [128, D], F32, "fg_bc")
        k.ld(fg_bc, fg_bc[:, :], final_g, final_g.ap[0:1, :].to_broadcast([128, D]))
        sel = k.sb(glob, [2, 2, 128], F32, "sel", const=True)
        k.ld(sel, sel[:, :, :], sel_d, sel_d[:, :, :])
        identb = k.sb(glob, [128, 128], BF16, "identb", const=True)
        k.cp("dve", identb, identb[:, :], ident, ident[:, :])
        maskf = k.sb(glob, [64, 4, 64], F32, "maskf", const=True)
        k.ld(maskf, maskf[:, :, :], masks_d, masks_d.ap.rearrange("m s t -> s m t"))
        maskb = k.sb(glob, [64, 4, 64], BF16, "maskb", const=True)
        k.cp("dve", maskb, maskb[:, :, :], maskf, maskf[:, :, :])
        bmaskf = k.sb(glob, [64, 4, 64], F32, "bmaskf", const=True)
        k.ld(bmaskf, bmaskf[:, :, :], bmask_d, bmask_d.ap.rearrange("m s t -> s m t"))
        masks = {}
        masks["bmask"] = [T(bmaskf.ap[:, mi, :], "bm%d" % mi, const=True) for mi in range(4)]
        masks["identf64"] = T(ident.ap[0:64, 0:64], "identf64", const=True)
        for mi, mn in enumerate(("LE0", "LE1", "LT0", "LT1")):
            masks[mn] = T(maskb.ap[:, mi, :], mn, const=True)
            masks[mn[0:2] + "f" + mn[2]] = T(maskf.ap[:, mi, :], mn + "f", const=True)
        with ExitStack() as st0:
            rows = k.sb(st0, [64, 128], F32, "rows")
            pst = k.ps(st0, [128, 512], F32, "pst")
            n_r = depth * 16
            k.ld(rows, rows[0:n_r, :], norm_g, norm_g.ap.rearrange("l (c p) -> (l c) p", p=128))
            k.tr(pst, pst[:, 0:n_r], rows, rows[0:n_r, :], ident, ident[0:n_r, 0:n_r], partial=False)
            k.cp("dve", gT, gT[:, :], pst, pst[:, 0:n_r])
            for l in range(depth):
                k.ld(rows, rows[0:48, :], b_ada, b_ada.ap[l:l + 1, :].rearrange("o (c p) -> (o c) p", p=128))
                k.tr(pst, pst[:, 0:48], rows, rows[0:48, :], ident, ident[0:48, 0:48], partial=False)
                k.cp("dve", bT, bT[:, l * 48:(l + 1) * 48], pst, pst[:, 0:48], partial=True)
            cv = k.sb(st0, [2, D], F32, "cv")
            sg = k.sb(st0, [2, D], F32, "sg")
            k.ld(cv, cv[:, :], cvec, cvec[:, :])
            k.act(sg, sg[:, :], cv, cv[:, :], AF.Sigmoid)
            k.tt("dve", sg, sg[:, :], sg, sg[:, :], cv, cv[:, :], ALU.mult)
            for c in range(16):
                k.tr(pst, pst[:, 64 + 2 * c:64 + 2 * c + 2], sg, sg[0:2, c * 128:(c + 1) * 128], ident, ident[0:2, 0:2],
                     partial=(c > 0))
            k.cp("dve", sT, sT[:, :, :], pst, pst[:, 64:96].rearrange("p (c t) -> p c t", t=2))
        s.barrier()
        stage("consts")

        for layer in range(depth + 1):
            last = layer == depth
            with ExitStack() as L:
                gate_bc = None
                A_T = B_T = None
                if not last:
                    A_T = k.sb(L, [128, 2, 16], F32, "A_T")
                    B_T = k.sb(L, [128, 2, 16], F32, "B_T")
                if layer > 0:
                    gate_bc = k.sb(L, [128, 2, D], F32, "gate_bc")
                with ExitStack() as M:
                    slab = [k.sb(M, [128, 16, 512], F32, "adaslab") for _ in range(2)]
                    psm = k.ps(M, [128, 512], F32, "psm")
                    psg = k.ps(M, [128, 512], F32, "psg")
                    nsl = 0
                    if not last:
                        modT = k.sb(M, [128, 32, 2], F32, "modT")
                        for sl in range(8):
                            sb_ = slab[nsl % 2]
                            nsl += 1
                            k.ld(sb_, sb_[:, :, :], w_ada,
                                 w_ada.ap[layer, :, sl * 512:(sl + 1) * 512].rearrange("(c p) n -> p c n", p=128))
                            for j in range(4):
                                blk = sl * 4 + j
                                for kc in range(16):
                                    k.mm(psm, psm[:, 2 * blk:2 * blk + 2], sb_, sb_[:, kc, j * 128:(j + 1) * 128],
                                         sT, sT[:, kc, :], start=(kc == 0), stop=(kc == 15),
                                         acc=not (blk == 0 and kc == 0))
                        k.cp("dve", modT, modT[:, :, :], psm, psm[:, 0:64].rearrange("p (b t) -> p b t", t=2))
                        for cnd in range(2):
                            k.tt("dve", B_T, B_T[:, cnd, :], modT, modT[:, 0:16, cnd], bT,
                                 bT[:, layer * 48:layer * 48 + 16], ALU.add, partial=(cnd > 0))
                            k.tt("dve", A_T, A_T[:, cnd, :], modT, modT[:, 16:32, cnd], bT,
                                 bT[:, layer * 48 + 16:layer * 48 + 32], ALU.add, partial=(cnd > 0))
                        for cnd in range(2):
                            k.stt(A_T, A_T[:, cnd, :], A_T, A_T[:, cnd, :], 1.0, gT, gT[:, layer * 16:(layer + 1) * 16],
                                  ALU.add, ALU.mult, partial=True)
                    if layer > 0:
                        pl = layer - 1
                        grow = k.sb(M, [2, D], F32, "grow")
                        brow = k.sb(M, [2, D], F32, "brow")
                        for r in range(2):
                            k.ld(brow, brow[r:r + 1, :], b_ada, b_ada.ap[pl:pl + 1, 2 * D:3 * D], partial=(r > 0))
                        for sl in range(4):
                            sb_ = slab[nsl % 2]
                            nsl += 1
                            k.ld(sb_, sb_[:, :, :], w_ada,
                                 w_ada.ap[pl, :, 2 * D + sl * 512:2 * D + (sl + 1) * 512].rearrange("(c p) n -> p c n", p=128))
                            for kc in range(16):
                                k.mm(psg, psg[0:2, :], sT, sT[:, kc, :], sb_, sb_[:, kc, :],
                                     start=(kc == 0), stop=(kc == 15), acc=(kc > 0))
                            k.tt("dve", grow, grow[:, sl * 512:(sl + 1) * 512], psg, psg[0:2, :], brow,
                                 brow[:, sl * 512:(sl + 1) * 512], ALU.add, partial=(sl > 0))
                        for cnd in range(2):
                            for sl in range(4):
                                k.mm(psg, psg[:, :], sel, sel[:, cnd, :], grow, grow[:, sl * 512:(sl + 1) * 512])
                                k.cp("dve", gate_bc, gate_bc[:, cnd, sl * 512:(sl + 1) * 512], psg, psg[:, :],
                                     partial=not (cnd == 0 and sl == 0))
                s.barrier()
                stage("mod%d" % layer)
                lin_phase(k, cfg, layer, L, dict(
                    xin=xin, xcur=xcur, xcur_rt=xcur_rt, proj=proj, proj_rt=proj_rt, mixd=mixd, mix_rt=mix_rt,
                    yout=yout, yout_rt=yout_rt, w_in_even=w_in_even, w_in_odd=w_in_odd, w_out=w_out,
                    ident=ident, identb=identb, A_T=A_T, B_T=B_T, gate_bc=gate_bc, fg_bc=fg_bc, cond_of_tile=cond_of_tile))
                s.barrier()
                stage("lin%d" % layer)
            if True:
                if not last:
                    RR = dict(proj=proj, proj_rt=proj_rt, mixd=mixd, mix_rt=mix_rt, ident=ident, identb=identb,
                              masks=masks)
                    RR.update(P_)
                    mixer_phase(k, cfg, layer, RR, mixer_mode)
                    s.barrier()


def lin_phase(k, cfg, layer, L, R):
    s = k.s
    depth = cfg.depth
    last = layer == depth
    first = layer == 0
    NT = cfg.ntile
    ident = R["ident"]
    identb = R["identb"]
    x_src = R["xin"] if layer <= 1 else R["xcur"][(layer - 1) % 2]
    x_src_rt = None if layer <= 1 else R["xcur_rt"][(layer - 1) % 2]
    x_dst = R["xcur"][layer % 2]
    x_dst_rt = R["xcur_rt"][layer % 2]
    if not last:
        even = layer % 2 == 0
        win = R["w_in_even"] if even else R["w_in_odd"]
        NOUT = EVEN_IN if even else ODD_IN
        nslab = (NOUT + 511) // 512
    GT = 8
    groups = []
    t0 = 0
    nts = cfg.ts // 128
    while t0 < nts:
        g = min(GT, nts - t0)
        groups.append((t0, g))
        t0 += g
    while t0 < NT:
        g = min(GT, NT - t0)
        groups.append((t0, g))
        t0 += g
    with ExitStack() as P:
        hT = k.sb(P, [128, 16, GT * 128], BF16, "hT") if not last else None
        xt = [k.sb(P, [128, D], F32, "xt") for _ in range(2)]
        xn = [k.sb(P, [128, 512], F32, "xn") for _ in range(2)]
        ss = [k.sb(P, [128, 1], F32, "ss") for _ in range(2)]
        rstd = [k.sb(P, [128, 1], F32, "rstd") for _ in range(2)]
        junk = k.sb(P, [128, D], BF16, "junk")
        pt = [k.ps(P, [128, 512], F32, "pt") for _ in range(2)]
        pm = [k.ps(P, [128, 512], F32, "pm") for _ in range(4)]
        if not last:
            wsl = [k.sb(P, [128, 16, 512], BF16, "wsl") for _ in range(2)]
            po = [k.sb(P, [128, 512], F32, "po") for _ in range(2)]
        if last:
            yb = [k.sb(P, [128, D], F32, "yb") for _ in range(2)]
        if not first:
            ptb = [k.ps(P, [128, 8, 128], BF16, "ptb") for _ in range(2)]
            mt = [k.sb(P, [128, D], BF16, "mt") for _ in range(2)]
            mixT = [k.sb(P, [128, 16, 128], BF16, "mixT") for _ in range(2)]
            tmp = [k.sb(P, [128, 512], F32, "tmp") for _ in range(2)]
            wout = [k.sb(P, [128, 16, 512], BF16, "wout") for _ in range(4)]
            for oc in range(4):
                k.ld(wout[oc], wout[oc][:, :, :], R["w_out"],
                     R["w_out"].ap[layer - 1, :, oc * 512:(oc + 1) * 512].rearrange("(c p) n -> p c n", p=128),
                     eng="pool")
        nws = 0
        npm = 0
        npo = 0
        ntl = 0
        nxn = 0
        for (g0, gn) in groups:
            for ti in range(gn):
                tt = g0 + ti
                cnd = R["cond_of_tile"](tt)
                b = ntl % 2
                ntl += 1
                X = xt[b]
                rows = slice(tt * 128, (tt + 1) * 128)
                if first:
                    k.ld(X, X[:, :], x_src, x_src.ap[rows, :])
                else:
                    k.ld(X, X[:, :], x_src_rt[tt] if x_src_rt is not None else x_src, x_src.ap[rows, :])
                    M_ = mt[b]
                    k.ld(M_, M_[:, :], R["mix_rt"][tt], R["mixd"].ap[rows, :])
                    MT = mixT[b]
                    for h2 in range(2):
                        p_ = ptb[h2]
                        for j in range(8):
                            c = h2 * 8 + j
                            k.tr(p_, p_[:, j, :], M_, M_[:, c * 128:(c + 1) * 128], identb, identb[:, :],
                                 partial=(j > 0))
                        k.cp("act", MT, MT[:, h2 * 8:(h2 + 1) * 8, :], p_, p_[:, :, :], partial=(h2 > 0))
                    for oc in range(4):
                        w = wout[oc]
                        p_ = pm[npm % 4]
                        npm += 1
                        for kc in range(16):
                            k.mm(p_, p_[:, :], MT, MT[:, kc, :], w, w[:, kc, :], start=(kc == 0), stop=(kc == 15),
                                 acc=(kc > 0))
                        tm = tmp[oc % 2]
                        k.tt("dve", tm, tm[:, :], p_, p_[:, :], R["gate_bc"], R["gate_bc"][:, cnd, oc * 512:(oc + 1) * 512],
                             ALU.mult)
                        k.tt("pool", X, X[:, oc * 512:(oc + 1) * 512], tm, tm[:, :], X, X[:, oc * 512:(oc + 1) * 512],
                             ALU.add, partial=True)
                    if not last:
                        k.ld(x_dst_rt[tt], x_dst.ap[rows, :], X, X[:, :], eng="sp")
                k.act(junk, junk[:, :], X, X[:, :], AF.Square, accum=(ss[b], ss[b][:, :]))
                k.tsc("dve", rstd[b], rstd[b][:, :], ss[b], ss[b][:, :], 1.0 / D, EPS, ALU.mult, ALU.add)
                k.act(rstd[b], rstd[b][:, :], rstd[b], rstd[b][:, :], AF.Sqrt)

                def rec(e, o=rstd[b][:, :]):
                    return e.reciprocal(o, o)
                s.op("dve", rec, reads=(rstd[b],), writes=(rstd[b],))
                if last:
                    k.stt(yb[b], yb[b][:, :], X, X[:, :], rstd[b][:, 0:1], R["fg_bc"], R["fg_bc"][:, :], ALU.mult,
                          ALU.mult, extra_reads=(rstd[b],))
                    k.ld(R["yout_rt"][tt], R["yout"].ap[rows, :], yb[b], yb[b][:, :], eng="sp")
                    continue
                if DBG & 4:
                    continue
                for q4 in range(4):
                    xq = xn[nxn % 2]
                    nxn += 1
                    k.tsc("dve" if (q4 % 2 == 0 or DBG & 1) else "pool", xq, xq[:, :], X, X[:, q4 * 512:(q4 + 1) * 512],
                          rstd[b][:, 0:1], None, ALU.mult, extra_reads=(rstd[b],))
                    p_ = pt[q4 % 2]
                    for j in range(4):
                        k.tr(p_, p_[:, j * 128:(j + 1) * 128], xq, xq[:, j * 128:(j + 1) * 128], ident, ident[:, :],
                             partial=(j > 0))
                    for j in range(4):
                        c = q4 * 4 + j
                        eng = "act" if (j % 2 == 0 and DBG & 8) else "dve"
                        if DBG & 16:
                            continue
                        if eng == "act":
                            k.act(hT, hT[:, c, ti * 128:(ti + 1) * 128], p_, p_[:, j * 128:(j + 1) * 128], AF.Identity,
                                  bias=R["B_T"][:, cnd, c:c + 1], scale=R["A_T"][:, cnd, c:c + 1],
                                  extra_reads=(R["A_T"], R["B_T"]), partial=True)
                        else:
                            k.tsc("dve", hT, hT[:, c, ti * 128:(ti + 1) * 128], p_, p_[:, j * 128:(j + 1) * 128],
                                  R["A_T"][:, cnd, c:c + 1], R["B_T"][:, cnd, c:c + 1], ALU.mult, ALU.add,
                                  extra_reads=(R["A_T"], R["B_T"]), partial=True)
            if last or DBG & 2:
                continue
            for sl in range(nslab):
                c0 = sl * 512
                cw = min(512, NOUT - c0)
                w = wsl[nws % 2]
                nws += 1
                k.ld(w, w[:, :, 0:cw], win, win.ap[layer // 2, :, c0:c0 + cw].rearrange("(c p) n -> p c n", p=128),
                     eng="pool")
                for ti in range(gn):
                    tt = g0 + ti
                    p_ = pm[npm % 4]
                    npm += 1
                    for kc in range(16):
                        k.mm(p_, p_[:, 0:cw], hT, hT[:, kc, ti * 128:(ti + 1) * 128], w, w[:, kc, 0:cw],
                             start=(kc == 0), stop=(kc == 15), acc=(kc > 0))
                    o_ = po[npo % 2]
                    k.cp("act" if npo % 2 == 0 else "dve", o_, o_[:, 0:cw], p_, p_[:, 0:cw])
                    npo += 1
                    k.ld(R["proj_rt"][tt], R["proj"].ap[tt * 128:(tt + 1) * 128, c0:c0 + cw], o_, o_[:, 0:cw],
                         eng="sp", partial=True)


def seq_list(cfg, layer):
    even = layer % 2 == 0
    seqs = []
    nch = cfg.ts // 64
    R = cfg.ts // 64
    chunks = []
    for j in range(nch):
        if even:
            chunks.append([(0, 64, j * 64, 1, True, True)])
        elif R >= 64:
            assert R == 64
            chunks.append([(0, 64, j, 64, True, True)])
        else:
            cpc = 64 // R
            chunks.append([(m * R, R, j * cpc + m, 64, True, True) for m in range(cpc)])
    seqs.append(dict(kind="sample", idx=0, chunks=chunks))
    ncp = cfg.tp // 64
    for q in range(cfg.np):
        b = cfg.ts + q * cfg.tp
        seqs.append(dict(kind="prompt", idx=q,
                         chunks=[[(0, 64, b + j * 64, 1, j == 0, j == ncp - 1)] for j in range(ncp)]))
    return seqs


class MX:
    pass


def load_rows(k, m, dst_t, rowfn, src_ap, c0, c1, runs, shift=0):
    for (p0, n, r0, st, s0, s1) in runs:
        if shift == 0:
            k.ld(dst_t, rowfn(p0, p0 + n), m.dr, src_ap[r0:r0 + st * (n - 1) + 1:st, c0:c1], partial=True)
        elif shift == -1:
            if s0:
                k.ld(dst_t, rowfn(p0, p0 + 1), m.dr, m.zeros.ap[0:1, 0:c1 - c0], partial=True)
                if n > 1:
                    k.ld(dst_t, rowfn(p0 + 1, p0 + n), m.dr, src_ap[r0:r0 + st * (n - 2) + 1:st, c0:c1], partial=True)
            else:
                k.ld(dst_t, rowfn(p0, p0 + n), m.dr, src_ap[r0 - st:r0 - st + st * (n - 1) + 1:st, c0:c1], partial=True)
        else:
            if s1:
                k.ld(dst_t, rowfn(p0 + n - 1, p0 + n), m.dr, m.zeros.ap[0:1, 0:c1 - c0], partial=True)
                if n > 1:
                    k.ld(dst_t, rowfn(p0, p0 + n - 1), m.dr, src_ap[r0 + st:r0 + st + st * (n - 2) + 1:st, c0:c1],
                         partial=True)
            else:
                k.ld(dst_t, rowfn(p0, p0 + n), m.dr, src_ap[r0 + st:r0 + st + st * (n - 1) + 1:st, c0:c1], partial=True)


def store_rows(k, m, dst_ap, c0, c1, runs, src_t, rowfn):
    for (p0, n, r0, st, s0, s1) in runs:
        k.ld(m.dw, dst_ap[r0:r0 + st * (n - 1) + 1:st, c0:c1], src_t, rowfn(p0, p0 + n), partial=True)


class ScanCore:
    def __init__(self, k, m, P, K, V, H, has_u, has_q, has_k2, post_scale, name):
        self.k, self.m = k, m
        self.K, self.V, self.H = K, V, H
        self.has_u, self.has_q, self.has_k2, self.post_scale = has_u, has_q, has_k2, post_scale
        self.Z = k.sb(P, [K, H, V], F32, name + "Z")
        self.Zb = k.sb(P, [K, H, V], BF16, name + "Zb")
        self.o = [k.sb(P, [64, H * V], F32, name + "o") for _ in range(1)]
        self.no = 0
        if has_u:
            self.iv = {nm: k.sb(P, [64, 8, 64], F32, name + nm) for nm in
                       ("N0", "NT0", "E0", "E1", "E2", "Xa", "Xb", "XTa", "XTb", "N2", "NT2", "N4")}
            self.iv["Y"] = self.iv["N2"]
            self.Xf = k.sb(P, [64, H, 64], BF16, name + "Xf")
            self.BhT = k.sb(P, [K, H, 64], BF16, name + "BhT")
            self.U0 = k.sb(P, [64, H * V], F32, name + "U0")
            self.U = k.sb(P, [64, H * V], BF16, name + "U")
        self.ztmp = k.sb(P, [K, H, V], F32, name + "zt")

    def init_state(self, src_t=None, src_ap=None):
        k = self.k
        if src_ap is None:
            k.memset("pool", self.Z, self.Z[:, :, :], 0.0)
        else:
            k.ld(self.Z, self.Z[:, :, :], src_t, src_ap)
        k.cp("act", self.Zb, self.Zb[:, :, :], self.Z, self.Z[:, :, :])

    def store_state(self, dst_t, dst_ap):
        self.k.ld(dst_t, dst_ap, self.Z, self.Z[:, :, :], partial=True)

    def _banks(self, width):
        return (width + 511) // 512

    def step(self, I):
        k, m = self.k, self.m
        K, V, H = self.K, self.V, self.H
        HV = H * V
        nb = self._banks(HV)
        hpb = 512 // V
        pd = m.pd
        evn = [0]

        def evac_eng():
            evn[0] += 1
            return "act" if evn[0] % 2 == 0 else "dve"

        if self.has_u:
            NN, NNT = I["NN"], I["NNT"]
            ident = m.identf64
            iv = self.iv
            for hg in range(0, H, 8):
                nh = min(8, H - hg)
                hs = slice(hg, hg + nh)

                def bcm(mt):
                    return mt[0:64, 0:64].unsqueeze(1).to_broadcast([64, nh, 64])

                def v(t_):
                    return t_[:, 0:nh, :]
                k.tt("pool", iv["N0"], v(iv["N0"]), NN, NN[:, hs, :], m.bmask[0], bcm(m.bmask[0]), ALU.mult)
                k.tt("pool", iv["NT0"], v(iv["NT0"]), NNT, NNT[:, hs, :], m.bmask[0], bcm(m.bmask[0]), ALU.mult)
                for l in range(3):
                    k.tt("pool", iv["E%d" % l], v(iv["E%d" % l]), NN, NN[:, hs, :], m.bmask[l + 1], bcm(m.bmask[l + 1]),
                         ALU.mult)
                X, XT = iv["Xa"], iv["XTa"]
                Xn, XTn = iv["Xb"], iv["XTb"]
                k.tt("pool", X, v(X), iv["N0"], v(iv["N0"]), ident, bcm(ident), ALU.add)
                k.tt("pool", XT, v(XT), iv["NT0"], v(iv["NT0"]), ident, bcm(ident), ALU.add)
                pc = [0]

                def mmh(L_, R_):
                    p_ = m.pd[pc[0] % 2]
                    pc[0] += 1
                    for hh in range(nh):
                        k.mm(p_, p_[0:64, hh * 64:(hh + 1) * 64], L_, L_[:, hh, :], R_, R_[:, hh, :], acc=(hh > 0))
                    return p_, p_[0:64, 0:nh * 64].rearrange("p (h t) -> p h t", t=64)

                def evc(dst, L_, R_):
                    p_, pap = mmh(L_, R_)
                    k.cp("act", dst, v(dst), p_, pap)

                def eva(dst, base, L_, R_):
                    p_, pap = mmh(L_, R_)
                    k.tt("dve", dst, v(dst), p_, pap, base, v(base), ALU.add)
                evc(iv["N2"], iv["NT0"], iv["N0"])
                evc(iv["NT2"], iv["N0"], iv["NT0"])
                eva(Xn, X, XT, iv["N2"])
                eva(XTn, XT, iv["N2"], XT)
                X, XT, Xn, XTn = Xn, XTn, X, XT
                evc(iv["N4"], iv["NT2"], iv["N2"])
                eva(Xn, X, XT, iv["N4"])
                eva(XTn, XT, iv["N4"], XT)
                X, XT, Xn, XTn = Xn, XTn, X, XT
                for l in range(3):
                    evc(iv["Y"], iv["E%d" % l], XT)
                    eva(Xn, X, iv["Y"], X)
                    if l < 2:
                        eva(XTn, XT, X, iv["Y"])
                    X, XT, Xn, XTn = Xn, XTn, X, XT
                k.cp("pool", self.Xf, self.Xf[:, hs, :], X, v(X), partial=(hg > 0))
            X = self.Xf
            Bop = I["Bop"]
            kpb = 512 // 64
            for g in range(0, H, kpb):
                p_ = pd[(g // kpb) % 2]
                for hh in range(g, min(H, g + kpb)):
                    k.mm(p_, p_[0:K, (hh - g) * 64:(hh - g + 1) * 64], Bop, Bop[:, hh * K:(hh + 1) * K], X, X[:, hh, :],
                         acc=(hh > g))
                nh = min(H, g + kpb) - g
                k.cp("dve", self.BhT, self.BhT[:, g:g + nh, :], p_,
                     p_[0:K, 0:nh * 64].rearrange("p (h t) -> p h t", t=64), partial=(g > 0))
            U0r = I["U0rhs"]
            for b in range(nb):
                p_ = pd[b % 2]
                for hh in range(b * hpb, min(H, (b + 1) * hpb)):
                    k.mm(p_, p_[0:64, (hh - b * hpb) * V:(hh - b * hpb + 1) * V], X, X[:, hh, :], U0r,
                         U0r[:, hh * V:(hh + 1) * V], acc=(hh > b * hpb))
                w = min(HV, (b + 1) * 512) - b * 512
                k.cp(evac_eng(), self.U0, self.U0[:, b * 512:b * 512 + w], p_, p_[0:64, 0:w], partial=(b > 0))
            for b in range(nb):
                p_ = m.pU[b]
                for hh in range(b * hpb, min(H, (b + 1) * hpb)):
                    k.mm(p_, p_[0:64, (hh - b * hpb) * V:(hh - b * hpb + 1) * V], self.BhT, self.BhT[:, hh, :], self.Zb,
                         self.Zb[:, hh, :], acc=(hh > b * hpb))
                w = min(HV, (b + 1) * 512) - b * 512
                k.tt("dve", self.U, self.U[:, b * 512:b * 512 + w], p_, p_[0:64, 0:w], self.U0,
                     self.U0[:, b * 512:b * 512 + w], ALU.add, partial=(b > 0))
        RopT = I["RopT"]
        o_t = self.o[0]
        self.no += 1
        for b in range(nb):
            p_ = m.pO[b]
            for hh in range(b * hpb, min(H, (b + 1) * hpb)):
                oap = p_[0:64, (hh - b * hpb) * V:(hh - b * hpb + 1) * V]
                terms = [(RopT, RopT[:, hh, :], self.Zb, self.Zb[:, hh, :])]
                if self.has_u:
                    terms.append((I["PmT"], I["PmT"][:, hh, :], self.U, self.U[:, hh * V:(hh + 1) * V]))
                if self.has_q:
                    terms.append((I["QmT"], I["QmT"][:, hh, :], I["Vtok"], I["Vtok"][:, hh * V:(hh + 1) * V]))
                for ti, (lt, lap, rt, rap) in enumerate(terms):
                    k.mm(p_, oap, lt, lap, rt, rap, start=(ti == 0), stop=(ti == len(terms) - 1),
                         acc=not (hh == b * hpb and ti == 0))
            w = min(HV, (b + 1) * 512) - b * 512
            k.cp(evac_eng(), o_t, o_t[:, b * 512:b * 512 + w], p_, p_[0:64, 0:w], partial=(b > 0))
        zs = I["zs"]
        for b in range(nb):
            p_ = m.pZ[b]
            for hh in range(b * hpb, min(H, (b + 1) * hpb)):
                zap = p_[0:K, (hh - b * hpb) * V:(hh - b * hpb + 1) * V]
                terms = []
                if self.has_u:
                    terms.append((I["Aop"], I["Aop"][:, hh * K:(hh + 1) * K], self.U, self.U[:, hh * V:(hh + 1) * V]))
                if self.has_k2:
                    terms.append((I["Kop"], I["Kop"][:, hh * K:(hh + 1) * K], I["Vtok"], I["Vtok"][:, hh * V:(hh + 1) * V]))
                for ti, (lt, lap, rt, rap) in enumerate(terms):
                    k.mm(p_, zap, lt, lap, rt, rap, start=(ti == 0), stop=(ti == len(terms) - 1),
                         acc=not (hh == b * hpb and ti == 0))
            h0 = b * hpb
            nh = min(H, (b + 1) * hpb) - h0
            zsb = zs[0:K, h0:h0 + nh].unsqueeze(2).to_broadcast([K, nh, V])
            pz = p_[0:K, 0:nh * V].rearrange("p (h v) -> p h v", v=V)
            if self.post_scale:
                k.tt("dve", self.ztmp, self.ztmp[:, h0:h0 + nh, :], p_, pz, self.Z, self.Z[:, h0:h0 + nh, :], ALU.add,
                     partial=(b > 0))
                k.tt("pool", self.Z, self.Z[:, h0:h0 + nh, :], self.ztmp, self.ztmp[:, h0:h0 + nh, :], zs, zsb, ALU.mult,
                     partial=(b > 0))
            else:
                k.tt("pool", self.ztmp, self.ztmp[:, h0:h0 + nh, :], self.Z, self.Z[:, h0:h0 + nh, :], zs, zsb, ALU.mult,
                     partial=(b > 0))
                k.tt("dve", self.Z, self.Z[:, h0:h0 + nh, :], p_, pz, self.ztmp, self.ztmp[:, h0:h0 + nh, :], ALU.add,
                     partial=(b > 0))
            k.cp("act", self.Zb, self.Zb[:, h0:h0 + nh, :], self.Z, self.Z[:, h0:h0 + nh, :], partial=(b > 0))
        return o_t


def mixer_phase(k, cfg, layer, R, mode):
    m = MX()
    m.cfg = cfg
    m.layer = layer
    m.R = R
    m.dr = T(None, "dram_read", const=True)
    m.dw = T(None, "dram_write")
    m.zeros = R["zeros"]
    m.ident = R["ident"]
    m.identb = R["identb"]
    m.masks = R["masks"]
    m.bmask = R["masks"]["bmask"]
    m.identf64 = R["masks"]["identf64"]
    seqs = seq_list(cfg, layer)
    with ExitStack() as P:
        m.pd = [k.ps(P, [128, 512], F32, "pd") for _ in range(2)]
        m.pdb = k.ps(P, [128, 16, 64], BF16, "pdb")
        m.pU = [k.ps(P, [128, 512], F32, "pU") for _ in range(2)]
        m.pO = [k.ps(P, [128, 512], F32, "pO") for _ in range(2)]
        m.pZ = m.pU
        if layer % 2 == 0:
            if mode in ("full", "gla"):
                gla_mixer(k, m, seqs)
            if mode in ("full", "rwkv"):
                rwkv_mixer(k, m, seqs)
            if mode in ("gla", "rwkv"):
                zt = k.sb(P, [128, 1024], BF16, "zt")
                k.memset("dve", zt, zt[:, :], 0.0)
                c0 = 1024 if mode == "gla" else 0
                for tt in range(cfg.ntile):
                    k.ld(m.dw, R["mixd"].ap[tt * 128:(tt + 1) * 128, c0:c0 + 1024], zt, zt[:, :], partial=True)
        else:
            gdn_mixer(k, m, seqs)
    return


def gla_mixer(k, m, seqs):
    cfg, layer, R = m.cfg, m.layer, m.R
    j = layer // 2
    proj = R["proj"].ap
    with ExitStack() as P:
        w2 = k.sb(P, [16, 2, 512], F32, "gw2", const=True)
        k.ld(w2, w2[:, :, :], R["gla_w2"], R["gla_w2"].ap[j].rearrange("d l n -> l d n"))
        gb = k.sb(P, [1, 2, 512], F32, "gb", const=True)
        k.ld(gb, gb[:, :, :], R["gla_b"], R["gla_b"].ap[j:j + 1, :, :])
        gg = k.sb(P, [64, 256], F32, "gg", const=True)
        k.ld(gg, gg[:, :], R["gla_g"], R["gla_g"].ap[j:j + 1, :].to_broadcast([64, 256]))
        ones_f = k.sb(P, [64, 64], F32, "ones_f", const=True)
        k.memset("dve", ones_f, ones_f[:, :], 1.0)
        core = ScanCore(k, m, P, 128, 256, 4, False, True, True, True, "gla")
        NB = 2
        gq = [k.sb(P, [64, 512], F32, "gq") for _ in range(NB)]
        gk = [k.sb(P, [64, 512], F32, "gk") for _ in range(NB)]
        gv = [k.sb(P, [64, 1024], F32, "gv") for _ in range(NB)]
        gl = [k.sb(P, [64, 16], F32, "gl") for _ in range(NB)]
        glT = [k.sb(P, [16, 64], F32, "glT") for _ in range(NB)]
        la = [k.sb(P, [64, 512], F32, "la") for _ in range(NB)]
        eW = [k.sb(P, [64, 512], F32, "eW") for _ in range(NB)]
        eWi = [k.sb(P, [64, 512], F32, "eWi") for _ in range(NB)]
        qtb = [k.sb(P, [64, 512], BF16, "qtb") for _ in range(NB)]
        ktb = [k.sb(P, [64, 512], BF16, "ktb") for _ in range(NB)]
        vt = [k.sb(P, [64, 1024], BF16, "vt") for _ in range(NB)]
        qT = [k.sb(P, [128, 4, 64], BF16, "qT") for _ in range(NB)]
        kT = [k.sb(P, [128, 4, 64], BF16, "kT") for _ in range(NB)]
        QmT = [k.sb(P, [64, 4, 64], BF16, "QmT") for _ in range(NB)]
        zs = [k.sb(P, [128, 4], F32, "zs") for _ in range(NB)]
        ofw = [k.sb(P, [64, 1024], F32, "ofw") for _ in range(NB)]
        gz = [k.sb(P, [64, 1024], F32, "gz") for _ in range(NB)]
        sq = k.sb(P, [64, 1024], F32, "sq")
        ssq = [k.sb(P, [64, 4], F32, "ssq") for _ in range(NB)]
        mo = [k.sb(P, [64, 1024], BF16, "mo") for _ in range(NB)]
        n = 0
        for d in range(2):
            LE = m.masks["LE%d" % d]
            LEf = m.masks["LEf%d" % d]
            for sq_ in seqs:
                if sq_["kind"] == "sample":
                    core.init_state(R["state_gla"], R["state_gla"].ap[j, d].rearrange("h k v -> k h v"))
                else:
                    core.init_state()
                chs = sq_["chunks"] if d == 0 else sq_["chunks"][::-1]
                for runs in chs:
                    b = n % NB
                    n += 1
                    load_rows(k, m, gq[b], lambda a, c, t=gq[b]: t[a:c, :], proj, 0, 512, runs)
                    load_rows(k, m, gk[b], lambda a, c, t=gk[b]: t[a:c, :], proj, 512, 1024, runs)
                    load_rows(k, m, gv[b], lambda a, c, t=gv[b]: t[a:c, :], proj, 1024, 2048, runs)
                    load_rows(k, m, gl[b], lambda a, c, t=gl[b]: t[a:c, :], proj, 3072 + d * 16, 3088 + d * 16, runs)
                    p0 = m.pd[0]
                    k.tr(p0, p0[0:16, 0:64], gl[b], gl[b][:, :], m.ident, m.ident[0:64, 0:64], partial=False)
                    k.cp("act", glT[b], glT[b][:, :], p0, p0[0:16, 0:64])
                    p1 = m.pd[1]
                    k.mm(p1, p1[0:64, :], glT[b], glT[b][:, :], w2, w2[:, d, :], start=True, stop=False)
                    k.mm(p1, p1[0:64, :], ones_f, ones_f[0:1, 0:64], gb, gb[0:1, d, :], start=False, stop=True, acc=True)
                    ckpt()
                    k.act(la[b], la[b][:, :], p1, p1[0:64, :], AF.Exp, scale=-1.0)
                    k.act(la[b], la[b][:, :], la[b], la[b][:, :], AF.Ln, bias=1.0)
                    k.tsc("pool", la[b], la[b][:, :], la[b], la[b][:, :], -1.0 / 16.0, None, ALU.mult)
                    ckpt()
                    k.mm(p0, p0[0:64, :], LEf, LEf[:, :], la[b], la[b][:, :])
                    k.act(eW[b], eW[b][:, :], p0, p0[0:64, :], AF.Exp)
                    k.act(eWi[b], eWi[b][:, :], p0, p0[0:64, :], AF.Exp, scale=-1.0)
                    for hh in range(4):
                        k.mm(p1, p1[:, 2 * hh:2 * hh + 2], la[b], la[b][:, hh * 128:(hh + 1) * 128], ones_f, ones_f[:, 0:2],
                             acc=(hh > 0))
                    k.act(zs[b], zs[b][:, :], p1, p1[:, 0:8].rearrange("p (h t) -> p h t", t=2)[:, :, 0], AF.Exp)
                    ckpt()
                    k.stt(qtb[b], qtb[b][0:64, :], gq[b], gq[b][:, :], 128.0 ** -0.5, eW[b], eW[b][:, :], ALU.mult, ALU.mult)
                    k.tt("dve" if DBG & 64 else "pool", ktb[b], ktb[b][0:64, :], gk[b], gk[b][:, :], eWi[b], eWi[b][:, :], ALU.mult)
                    k.cp("act", vt[b], vt[b][:, :], gv[b], gv[b][:, :])
                    ckpt()
                    pb = m.pO[0] if DBG & 128 else m.pd[1]
                    for hh in range(4):
                        k.mm(pb, pb[:, hh * 64:(hh + 1) * 64], qtb[b], qtb[b][:, hh * 128:(hh + 1) * 128], m.identb,
                             m.identb[0:64, 0:64], acc=(hh > 0))
                    for hh in range(4 if not DBG & 32 else 0):
                        k.mm(pb, pb[:, (4 + hh) * 64:(5 + hh) * 64], ktb[b], ktb[b][:, hh * 128:(hh + 1) * 128], m.identb,
                             m.identb[0:64, 0:64], acc=True)
                    if DBG & 256:
                        ckpt()
                    k.cp("dve", qT[b], qT[b][:, :, :], pb, pb[:, 0:256].rearrange("p (h t) -> p h t", t=64))
                    if DBG & 512:
                        ckpt()
                    k.cp("dve", kT[b], kT[b][:, :, :], pb, pb[:, 256:512].rearrange("p (h t) -> p h t", t=64))
                    ckpt()
                    for hh in range(4):
                        k.mm(p0, p0[0:64, hh * 64:(hh + 1) * 64], kT[b], kT[b][:, hh, :], qT[b], qT[b][:, hh, :], acc=(hh > 0))
                    k.tt("dve", QmT[b], QmT[b][:, :, :], p0, p0[0:64, 0:256].rearrange("p (h t) -> p h t", t=64), LE,
                         LE[0:64, 0:64].unsqueeze(1).to_broadcast([64, 4, 64]), ALU.mult)
                    ckpt()
                    o_t = core.step(dict(RopT=qT[b], QmT=QmT[b], Vtok=vt[b], Kop=ktb[b], zs=zs[b]))
                    ckpt()
                    if d == 0:
                        store_rows(k, m, R["ogla"].ap, 0, 1024, runs, o_t, lambda a, c, t=o_t: t[a:c, :])
                    else:
                        load_rows(k, m, ofw[b], lambda a, c, t=ofw[b]: t[a:c, :], R["ogla"].ap, 0, 1024, runs)
                        load_rows(k, m, gz[b], lambda a, c, t=gz[b]: t[a:c, :], proj, 2048, 3072, runs)
                        k.tt("dve", ofw[b], ofw[b][:, :], ofw[b], ofw[b][:, :], o_t, o_t[:, :], ALU.add)
                        k.tt("pool", sq, sq[:, :], ofw[b], ofw[b][:, :], ofw[b], ofw[b][:, :], ALU.mult)

                        def red(e, o=ssq[b][:, :], i=sq[:, :].rearrange("p (h v) -> p h v", v=256)):
                            return e.tensor_reduce(o, i, AX.X, ALU.add)
                        k.s.op("dve", red, reads=(sq,), writes=(ssq[b],))
                        k.tsc("dve", ssq[b], ssq[b][:, :], ssq[b], ssq[b][:, :], 1.0 / 256.0, EPS, ALU.mult, ALU.add)
                        k.act(ssq[b], ssq[b][:, :], ssq[b], ssq[b][:, :], AF.Sqrt)

                        def rec(e, o=ssq[b][:, :]):
                            return e.reciprocal(o, o)
                        k.s.op("dve", rec, reads=(ssq[b],), writes=(ssq[b],))
                        o3 = ofw[b][:, :].rearrange("p (h v) -> p h v", v=256)
                        k.tt("dve", ofw[b], o3, ofw[b], o3, ssq[b], ssq[b][:, :].unsqueeze(2).to_broadcast([64, 4, 256]), ALU.mult)
                        k.tt("pool", ofw[b], o3, ofw[b], o3, gg, gg[:, :].unsqueeze(1).to_broadcast([64, 4, 256]), ALU.mult)
                        k.act(sq, sq[:, :], gz[b], gz[b][:, :], AF.Sigmoid)
                        k.tt("pool", sq, sq[:, :], sq, sq[:, :], gz[b], gz[b][:, :], ALU.mult)
                        k.tt("dve", mo[b], mo[b][:, :], ofw[b], ofw[b][:, :], sq, sq[:, :], ALU.mult)
                        store_rows(k, m, R["mixd"].ap, 0, 1024, runs, mo[b], lambda a, c, t=mo[b]: t[a:c, :])
                if sq_["kind"] == "prompt":
                    core.store_state(R["ns_gla"], R["ns_gla"].ap[sq_["idx"], j, d].rearrange("h k v -> k h v"))
            k.s.barrier()


def core_inputs(inp, core, cfg, consts):
    f = lambda a: np.ascontiguousarray(np.asarray(a, dtype=np.float32))
    ne, no_ = cfg.n_even, max(cfg.n_odd, 1)
    m = {}
    xs = f(inp["x_sample"])[core]
    xp = f(inp["x_prompt"])[core * cfg.np:(core + 1) * cfg.np].reshape(-1, D)
    m["xin"] = np.ascontiguousarray(np.concatenate([xs, xp], axis=0))
    m["cvec"] = np.ascontiguousarray(np.stack([f(inp["c"])[core], f(inp["c_ctx"])]))
    m["state_gla"] = f(inp["state_gla"])[core]
    m["state_rwkv"] = f(inp["state_rwkv"])[core]
    sg = f(inp["state_gdn"])[core]
    m["state_gdn"] = sg if cfg.n_odd > 0 else np.zeros((1, 2, 16, 128, 128), np.float32)
    for nm in ("norm_g", "w_ada", "b_ada", "w_in_even", "w_out", "gla_w2", "gla_b", "gla_g", "rwkv_mu", "rwkv_w0",
               "rwkv_w2", "rwkv_a0", "rwkv_a2", "rwkv_k_k", "rwkv_k_a", "rwkv_gn_g", "rwkv_gn_b"):
        m[nm] = f(inp[nm])
    m["rwkv_r_k"] = f(inp["rwkv_r_k"]).reshape(ne, 1024)
    m["final_g"] = f(inp["final_g"]).reshape(1, D)
    if cfg.n_odd > 0:
        for nm in ("w_in_odd", "gdn_conv_w", "gdn_A_log", "gdn_dt_bias", "gdn_g"):
            m[nm] = f(inp[nm])
    else:
        m["w_in_odd"] = np.zeros((1, D, ODD_IN), np.float32)
        m["gdn_conv_w"] = np.zeros((1, 3, 6144), np.float32)
        m["gdn_A_log"] = np.zeros((1, 2, 16), np.float32)
        m["gdn_dt_bias"] = np.zeros((1, 2, 16), np.float32)
        m["gdn_g"] = np.zeros((1, 128), np.float32)
    m.update(consts)
    return m


C0 = 0.6065306597126334


def hilo_aug(k, P, name, w_t, w_ap64, row_t, row_ap):
    aug = k.sb(P, [128, 1024], BF16, name, const=True)
    k.memset("pool", aug, aug[:, :], 0.0)
    k.ld(aug, aug[64:128, :], w_t, w_ap64, eng="pool", partial=True)
    with ExitStack() as tmp:
        st = k.sb(tmp, [33, 1024], F32, name + "st")
        k.ld(st, st[0:1, :], row_t, row_ap)
        k.ld(st, st[32:33, :], row_t, row_ap, partial=True)
        k.cp("dve", aug, aug[0:1, :], st, st[0:1, :], partial=True)
        k.cp("dve", aug, aug[32:33, :], st, st[32:33, :], partial=True)
        k.tt("dve", aug, aug[32:33, :], st, st[32:33, :], aug, aug[32:33, :], ALU.subtract, partial=True)
    k.s.barrier()
    return aug


def rwkv_mixer(k, m, seqs):
    cfg, layer, R = m.cfg, m.layer, m.R
    j = layer // 2
    proj = R["proj"].ap
    H = 16
    with ExitStack() as P:
        pdb = m.pdb

        def bc(name, t, ap_row, n=1024, stk=None):
            x = k.sb(stk if stk is not None else P, [64, n], F32, name, const=True)
            k.ld(x, x[:, :], t, ap_row.to_broadcast([64, n]))
            return x
        mu = [[bc("mu%d%d" % (i, s_), R["rwkv_mu"], R["rwkv_mu"].ap[j, i, s_:s_ + 1, :]) for s_ in range(2)]
              for i in range(3)]
        kk_bc = bc("kkbc", R["rwkv_k_k"], R["rwkv_k_k"].ap[j:j + 1, :])
        ka_bc = bc("kabc", R["rwkv_k_a"], R["rwkv_k_a"].ap[j:j + 1, :])
        rk_bc = bc("rkbc", R["rwkv_r_k"], R["rwkv_r_k"].ap[j:j + 1, :])
        ones_f = k.sb(P, [64, 2], F32, "ones_f2", const=True)
        k.memset("dve", ones_f, ones_f[:, :], 1.0)
        core = ScanCore(k, m, P, 64, 64, H, True, True, True, True, "rw")
        xr = k.sb(P, [64, 1024], F32, "xr")
        xk = k.sb(P, [64, 1024], F32, "xk")
        xv = k.sb(P, [64, 1024], F32, "xv")
        pv_ = k.sb(P, [64, 1024], F32, "pv")
        nx_ = k.sb(P, [64, 1024], F32, "nx")
        kkt = k.sb(P, [64, 1024], F32, "kkt")
        sig = k.sb(P, [64, 1024], F32, "sig")
        A_ = k.sb(P, [64, 1024], F32, "A_")
        kd = k.sb(P, [64, 1024], F32, "kd")
        S1 = k.sb(P, [64, 1024], F32, "S1")
        S2 = k.sb(P, [64, 1024], F32, "S2")
        S3 = k.sb(P, [64, 1024], F32, "S3")
        ssq = k.sb(P, [64, 16], F32, "rssq")
        bs = k.sb(P, [64, 16], F32, "bs")
        bs0 = k.sb(P, [64, 16], F32, "bs0")
        lw = k.sb(P, [64, 64], F32, "lw")
        la_ = k.sb(P, [64, 64], F32, "la")
        lwp = k.sb(P, [64, 128], BF16, "lwp")
        lap = k.sb(P, [64, 128], BF16, "lap")
        for t_ in (lwp, lap):
            k.memset("pool", t_, t_[:, :], 0.0)
            k.memset("pool", t_, t_[:, 0:1], 1.0)
            k.memset("pool", t_, t_[:, 32:33], 1.0)
        lwT = k.sb(P, [128, 64], BF16, "lwT")
        laT = k.sb(P, [128, 64], BF16, "laT")
        zs = k.sb(P, [64, 16], F32, "rzs")
        tl = {nm: k.sb(P, [64, 1024], BF16, "tl" + nm) for nm in ("al", "be", "kt", "rt", "v", "cv")}
        tT = {nm: k.sb(P, [64, H, 64], BF16, "tT" + nm) for nm in ("al", "be", "kt", "rt")}
        st_ = {nm: k.sb(P, [64, H, 64], F32 if nm in ("NN", "NNT") else BF16, "st" + nm)
               for nm in ("NN", "NNT", "CT", "PmT", "QmT")}
        Sst = TV(S1, S1.ap.rearrange("p (h v) -> p h v", v=64))
        for d in range(2):
            with ExitStack() as PD:
                LE, LT, LTo = m.masks["LE%d" % d], m.masks["LT%d" % d], m.masks["LT%d" % (1 - d)]
                LEf = m.masks["LEf%d" % d]
                w2aug = hilo_aug(k, PD, "w2aug", R["rwkv_w2"], R["rwkv_w2"].ap[j, d], R["rwkv_w0"],
                                 R["rwkv_w0"].ap[j, d:d + 1, :])
                a2aug = hilo_aug(k, PD, "a2aug", R["rwkv_a2"], R["rwkv_a2"].ap[j, d], R["rwkv_a0"],
                                 R["rwkv_a0"].ap[j, d:d + 1, :])
                if d == 1:
                    gng = bc("gng", R["rwkv_gn_g"], R["rwkv_gn_g"].ap[j:j + 1, :], stk=PD)
                    gnb = bc("gnb", R["rwkv_gn_b"], R["rwkv_gn_b"].ap[j:j + 1, :], stk=PD)
                    ofw = k.sb(PD, [64, 1024], F32, "rofw")
                    rz = k.sb(PD, [64, 1024], F32, "rz")
                    mo = k.sb(PD, [64, 1024], BF16, "rmo")
                    mean = k.sb(PD, [64, 16], F32, "mean")
                for sq_ in seqs:
                    if sq_["kind"] == "sample":
                        k.ld(Sst, Sst[:, :, :], R["state_rwkv"], R["state_rwkv"].ap[j, d].rearrange("h v k -> v h k"))
                        for g in range(2):
                            p_ = m.pd[g]
                            for hh in range(8):
                                k.tr(p_, p_[0:64, hh * 64:(hh + 1) * 64], Sst, Sst[:, g * 8 + hh, :], m.ident,
                                     m.ident[0:64, 0:64], partial=(hh > 0))
                            k.cp("dve", core.Z, core.Z[:, g * 8:(g + 1) * 8, :], p_,
                                 p_[0:64, :].rearrange("p (h v) -> p h v", v=64), partial=(g > 0))
                        k.cp("act", core.Zb, core.Zb[:, :, :], core.Z, core.Z[:, :, :])
                    else:
                        core.init_state()
                    chs = sq_["chunks"] if d == 0 else sq_["chunks"][::-1]
                    for runs in chs:
                        for qi, (cur, c0) in enumerate(((xr, 3104), (xk, 4128), (xv, 5152))):
                            load_rows(k, m, cur, lambda a, c, t=cur: t[a:c, :], proj, c0, c0 + 1024, runs, 0)
                            load_rows(k, m, pv_, lambda a, c, t=pv_: t[a:c, :], proj, c0, c0 + 1024, runs, -1)
                            load_rows(k, m, nx_, lambda a, c, t=nx_: t[a:c, :], proj, c0, c0 + 1024, runs, +1)
                            k.tt("pool", pv_, pv_[:, :], pv_, pv_[:, :], cur, cur[:, :], ALU.subtract)
                            k.tt("pool", pv_, pv_[:, :], pv_, pv_[:, :], mu[qi][0], mu[qi][0][:, :], ALU.mult)
                            k.tt("dve", nx_, nx_[:, :], nx_, nx_[:, :], cur, cur[:, :], ALU.subtract)
                            k.tt("dve", nx_, nx_[:, :], nx_, nx_[:, :], mu[qi][1], mu[qi][1][:, :], ALU.mult)
                            k.tt("pool", cur, cur[:, :], cur, cur[:, :], pv_, pv_[:, :], ALU.add)
                            k.tt("pool", cur, cur[:, :], cur, cur[:, :], nx_, nx_[:, :], ALU.add)
                        load_rows(k, m, lw, lambda a, c, t=lw: t[a:c, :], proj, 7200 + d * 64, 7264 + d * 64, runs, 0)
                        load_rows(k, m, la_, lambda a, c, t=la_: t[a:c, :], proj, 7328 + d * 64, 7392 + d * 64, runs, 0)
                        k.tt("pool", kkt, kkt[:, :], xk, xk[:, :], kk_bc, kk_bc[:, :], ALU.mult)
                        k.tt("pool", S1, S1[:, :], kkt, kkt[:, :], kkt, kkt[:, :], ALU.mult)

                        def red(e, o=ssq[:, :], i=S1[:, :].rearrange("p (h v) -> p h v", v=64)):
                            return e.tensor_reduce(o, i, AX.X, ALU.add)
                        k.s.op("dve", red, reads=(S1,), writes=(ssq,))
                        k.tsc("dve", ssq, ssq[:, :], ssq, ssq[:, :], EPS, None, ALU.add)
                        k.act(ssq, ssq[:, :], ssq, ssq[:, :], AF.Sqrt)

                        def rec(e, o=ssq[:, :]):
                            return e.reciprocal(o, o)
                        k.s.op("dve", rec, reads=(ssq,), writes=(ssq,))
                        kk3 = kkt[:, :].rearrange("p (h v) -> p h v", v=64)
                        k.tt("pool", kkt, kk3, kkt, kk3, ssq, ssq[:, :].unsqueeze(2).to_broadcast([64, 16, 64]), ALU.mult)
                        k.act(lwp, lwp[:, 64:128], lw, lw[:, :], AF.Tanh, partial=True)
                        k.cp("act", lap, lap[:, 64:128], la_, la_[:, :], partial=True)
                        p0, p1 = m.pd[0], m.pd[1]
                        k.mm(p0, p0[:, 0:64], lwp, lwp[:, :], m.identb, m.identb[0:64, 0:64])
                        k.mm(p0, p0[:, 64:128], lap, lap[:, :], m.identb, m.identb[0:64, 0:64], acc=True)
                        k.cp("dve", lwT, lwT[:, :], p0, p0[:, 0:64])
                        k.cp("dve", laT, laT[:, :], p0, p0[:, 64:128])
                        for hb in range(2):
                            p_ = m.pd[hb]
                            k.mm(p_, p_[0:64, :], lwT, lwT[:, :], w2aug, w2aug[:, hb * 512:(hb + 1) * 512])
                            k.act(sig, sig[:, hb * 512:(hb + 1) * 512], p_, p_[0:64, :], AF.Sigmoid, partial=(hb > 0))
                        for hb in range(2):
                            p_ = m.pd[hb]
                            k.mm(p_, p_[0:64, :], laT, laT[:, :], a2aug, a2aug[:, hb * 512:(hb + 1) * 512])
                            k.act(A_, A_[:, hb * 512:(hb + 1) * 512], p_, p_[0:64, :], AF.Sigmoid, partial=(hb > 0))
                        k.stt(S1, S1[:, :], A_, A_[:, :], -1.0, ka_bc, ka_bc[:, :], ALU.add, ALU.mult)
                        k.stt(kd, kd[:, :], S1, S1[:, :], 1.0, xk, xk[:, :], ALU.add, ALU.mult)
                        k.tt("pool", S1, S1[:, :], xr, xr[:, :], kd, kd[:, :], ALU.mult)
                        k.tt("pool", S1, S1[:, :], S1, S1[:, :], rk_bc, rk_bc[:, :], ALU.mult)

                        def red2(e, o=bs[:, :], i=S1[:, :].rearrange("p (h v) -> p h v", v=64)):
                            return e.tensor_reduce(o, i, AX.X, ALU.add)
                        k.s.op("dve", red2, reads=(S1,), writes=(bs,))
                        for hb in range(2):
                            p_ = m.pd[hb]
                            k.mm(p_, p_[0:64, :], LEf, LEf[:, :], sig, sig[:, hb * 512:(hb + 1) * 512])
                            sl = slice(hb * 512, (hb + 1) * 512)
                            k.act(S2, S2[:, sl], p_, p_[0:64, :], AF.Exp, scale=-C0, partial=(hb > 0))
                            k.act(S3, S3[:, sl], p_, p_[0:64, :], AF.Exp, scale=C0, partial=(hb > 0))
                            k.tt("dve", S1, S1[:, sl], p_, p_[0:64, :], sig, sig[:, sl], ALU.subtract, partial=(hb > 0))
                        k.tt("pool", tl["rt"], tl["rt"][:, :], xr, xr[:, :], S2, S2[:, :], ALU.mult)
                        k.tt("pool", tl["kt"], tl["kt"][:, :], kd, kd[:, :], S3, S3[:, :], ALU.mult)
                        k.tt("dve", S2, S2[:, :], kkt, kkt[:, :], A_, A_[:, :], ALU.mult)
                        k.stt(tl["al"], tl["al"][:, :], S2, S2[:, :], -1.0, S3, S3[:, :], ALU.mult, ALU.mult)
                        k.act(S1, S1[:, :], S1, S1[:, :], AF.Exp, scale=-C0)
                        k.tt("pool", tl["be"], tl["be"][:, :], kkt, kkt[:, :], S1, S1[:, :], ALU.mult)
                        k.cp("act", tl["v"], tl["v"][:, :], xv, xv[:, :])
                        p_ = m.pd[0]
                        for hh in range(H):
                            k.mm(p_, p_[0:64, 2 * hh:2 * hh + 2], sig, sig[:, hh * 64:(hh + 1) * 64], ones_f, ones_f[:, :],
                                 acc=(hh > 0))
                        k.act(zs, zs[:, :], p_, p_[0:64, 0:32].rearrange("p (h t) -> p h t", t=2)[:, :, 0], AF.Exp,
                              scale=-C0)
                        for nm in ("al", "be", "kt", "rt"):
                            for hh in range(H):
                                k.tr(pdb, pdb[0:64, hh, :], tl[nm], tl[nm][:, hh * 64:(hh + 1) * 64], m.identb,
                                     m.identb[0:64, 0:64], partial=(hh > 0))
                            k.cp("act", tT[nm], tT[nm][:, :, :], pdb, pdb[0:64, :, :])
                        prods = (("NN", "al", "be", LT), ("NNT", "be", "al", LTo), ("CT", "kt", "be", LT),
                                 ("PmT", "al", "rt", LE), ("QmT", "kt", "rt", LE))
                        for (dn, ln, rn, mk) in prods:
                            for g in range(2):
                                p_ = m.pd[g]
                                for hh in range(8):
                                    h_ = g * 8 + hh
                                    k.mm(p_, p_[0:64, hh * 64:(hh + 1) * 64], tT[ln], tT[ln][:, h_, :], tT[rn],
                                         tT[rn][:, h_, :], acc=(hh > 0))
                                k.tt("dve", st_[dn], st_[dn][:, g * 8:(g + 1) * 8, :], p_,
                                     p_[0:64, :].rearrange("p (h t) -> p h t", t=64), mk,
                                     mk[0:64, 0:64].unsqueeze(1).to_broadcast([64, 8, 64]), ALU.mult, partial=(g > 0))
                        for g in range(2):
                            p_ = m.pd[g]
                            for hh in range(8):
                                h_ = g * 8 + hh
                                k.mm(p_, p_[0:64, hh * 64:(hh + 1) * 64], st_["CT"], st_["CT"][:, h_, :], tl["v"],
                                     tl["v"][:, h_ * 64:(h_ + 1) * 64], acc=(hh > 0))
                            k.cp("dve", tl["cv"], tl["cv"][:, g * 512:(g + 1) * 512], p_, p_[0:64, :], partial=(g > 0))
                        o_t = core.step(dict(NN=st_["NN"], NNT=st_["NNT"], Bop=tl["be"], U0rhs=tl["cv"], RopT=tT["rt"],
                                             PmT=st_["PmT"], QmT=st_["QmT"], Vtok=tl["v"], Aop=tl["al"], Kop=tl["kt"],
                                             zs=zs))
                        if d == 0:
                            store_rows(k, m, R["orw"].ap, 0, 1024, runs, o_t, lambda a, c, t=o_t: t[a:c, :])
                            store_rows(k, m, R["bsum"].ap, 0, 16, runs, bs, lambda a, c, t=bs: t[a:c, :])
                        else:
                            load_rows(k, m, ofw, lambda a, c, t=ofw: t[a:c, :], R["orw"].ap, 0, 1024, runs)
                            load_rows(k, m, bs0, lambda a, c, t=bs0: t[a:c, :], R["bsum"].ap, 0, 16, runs)
                            load_rows(k, m, rz, lambda a, c, t=rz: t[a:c, :], proj, 6176, 7200, runs)
                            k.tt("dve", ofw, ofw[:, :], ofw, ofw[:, :], o_t, o_t[:, :], ALU.add)
                            k.tt("dve", bs0, bs0[:, :], bs0, bs0[:, :], bs, bs[:, :], ALU.add)
                            o3 = ofw[:, :].rearrange("p (h v) -> p h v", v=64)

                            def red3(e, o=mean[:, :], i=o3):
                                return e.tensor_reduce(o, i, AX.X, ALU.add)
                            k.s.op("dve", red3, reads=(ofw,), writes=(mean,))
                            k.tsc("dve", mean, mean[:, :], mean, mean[:, :], 1.0 / 64.0, None, ALU.mult)
                            k.tt("pool", ofw, o3, ofw, o3, mean, mean[:, :].unsqueeze(2).to_broadcast([64, 16, 64]),
                                 ALU.subtract)
                            k.tt("pool", S1, S1[:, :], ofw, ofw[:, :], ofw, ofw[:, :], ALU.mult)

                            def red4(e, o=ssq[:, :], i=S1[:, :].rearrange("p (h v) -> p h v", v=64)):
                                return e.tensor_reduce(o, i, AX.X, ALU.add)
                            k.s.op("dve", red4, reads=(S1,), writes=(ssq,))
                            k.tsc("dve", ssq, ssq[:, :], ssq, ssq[:, :], 1.0 / 64.0, GN_EPS, ALU.mult, ALU.add)
                            k.act(ssq, ssq[:, :], ssq, ssq[:, :], AF.Sqrt)
                            k.s.op("dve", rec, reads=(ssq,), writes=(ssq,))
                            k.tt("pool", ofw, o3, ofw, o3, ssq, ssq[:, :].unsqueeze(2).to_broadcast([64, 16, 64]), ALU.mult)
                            k.tt("pool", ofw, ofw[:, :], ofw, ofw[:, :], gng, gng[:, :], ALU.mult)
                            k.tt("pool", ofw, ofw[:, :], ofw, ofw[:, :], gnb, gnb[:, :], ALU.add)
                            v3 = xv[:, :].rearrange("p (h v) -> p h v", v=64)
                            k.tt("dve", S1, S1[:, :].rearrange("p (h v) -> p h v", v=64), xv, v3, bs0,
                                 bs0[:, :].unsqueeze(2).to_broadcast([64, 16, 64]), ALU.mult)
                            k.tt("pool", ofw, ofw[:, :], ofw, ofw[:, :], S1, S1[:, :], ALU.add)
                            k.act(S1, S1[:, :], rz, rz[:, :], AF.Sigmoid)
                            k.tt("pool", S1, S1[:, :], S1, S1[:, :], rz, rz[:, :], ALU.mult)
                            k.tt("dve", mo, mo[:, :], ofw, ofw[:, :], S1, S1[:, :], ALU.mult)
                            store_rows(k, m, R["mixd"].ap, 1024, 2048, runs, mo, lambda a, c, t=mo: t[a:c, :])
                    if sq_["kind"] == "prompt":
                        for g in range(2):
                            p_ = m.pd[g]
                            for hh in range(8):
                                k.tr(p_, p_[0:64, hh * 64:(hh + 1) * 64], core.Z, core.Z[:, g * 8 + hh, :], m.ident,
                                     m.ident[0:64, 0:64], partial=(hh > 0))
                            k.cp("dve", Sst, Sst[:, g * 8:(g + 1) * 8, :], p_,
                                 p_[0:64, :].rearrange("p (h v) -> p h v", v=64), partial=(g > 0))
                        k.ld(R["ns_rwkv"], R["ns_rwkv"].ap[sq_["idx"], j, d].rearrange("h v k -> v h k"), Sst,
                             Sst[:, :, :], partial=True)
                k.s.barrier()


def gdn_mixer(k, m, seqs):
    cfg, layer, R = m.cfg, m.layer, m.R
    j = layer // 2
    proj = R["proj"].ap
    HG = 8
    with ExitStack() as P:
        pdb = m.pdb
        ones_f = k.sb(P, [64, 128], F32, "g_ones", const=True)
        k.memset("dve", ones_f, ones_f[:, :], 1.0)
        gg = k.sb(P, [64, 128], F32, "gdng", const=True)
        k.ld(gg, gg[:, :], R["gdn_g"], R["gdn_g"].ap[j:j + 1, :].to_broadcast([64, 128]))
        core = ScanCore(k, m, P, 128, 128, HG, True, False, False, False, "gd")
        cur = [k.sb(P, [64, 1024], F32, "gcur%d" % i) for i in range(3)]
        pv_ = k.sb(P, [64, 1024], F32, "gpv")
        nx_ = k.sb(P, [64, 1024], F32, "gnx")
        S1 = k.sb(P, [64, 1024], F32, "gS1")
        ssq = k.sb(P, [64, 8], F32, "gssq")
        bl = k.sb(P, [64, 8], F32, "gbl")
        al = k.sb(P, [64, 8], F32, "gal")
        beta = k.sb(P, [64, 8], F32, "gbeta")
        gt = k.sb(P, [64, 8], F32, "ggt")
        gc = k.sb(P, [64, 8], F32, "ggc")
        egc = k.sb(P, [64, 8], F32, "gegc")
        coef = k.sb(P, [64, 8], F32, "gcoef")
        edec = k.sb(P, [64, 8], F32, "gedec")
        zs = k.sb(P, [128, 8], F32, "gzs")
        gLE = k.sb(P, [64, 8, 64], F32, "gLE")
        gLT = k.sb(P, [64, 8, 64], F32, "gLT")
        EMs = k.sb(P, [64, 8, 64], F32, "EMs")
        EMe = k.sb(P, [64, 8, 64], F32, "EMe")
        EMt = k.sb(P, [64, 8, 64], F32, "EMt")
        tl = {nm: k.sb(P, [64, 1024], BF16, "gtl" + nm) for nm in ("k", "kb", "q", "qe", "bop", "bv", "aop")}
        tT = {nm: k.sb(P, [128, HG, 64], BF16, "gtT" + nm) for nm in ("k", "kb", "q", "qe")}
        st_ = {nm: k.sb(P, [64, HG, 64], F32 if nm in ("NN", "NNT") else BF16, "gst" + nm)
               for nm in ("NN", "NNT", "PmT")}
        for d in range(2):
            LE, LT, LTo = m.masks["LE%d" % d], m.masks["LT%d" % d], m.masks["LT%d" % (1 - d)]
            LEf, LTof = m.masks["LEf%d" % d], m.masks["LTf%d" % (1 - d)]
            for g in range(2):
                with ExitStack() as PD:
                    cw = [[k.sb(PD, [64, 1024], F32, "cw%d%d" % (qi, tp), const=True) for tp in range(3)] for qi in range(3)]
                    for qi in range(3):
                        for tp in range(3):
                            c0 = qi * 2048 + g * 1024
                            k.ld(cw[qi][tp], cw[qi][tp][:, :], R["gdn_conv_w"],
                                 R["gdn_conv_w"].ap[j, tp:tp + 1, c0:c0 + 1024].to_broadcast([64, 1024]))
                    negA = k.sb(PD, [64, 8], F32, "negA", const=True)
                    dtb = k.sb(PD, [64, 8], F32, "dtb", const=True)
                    k.ld(negA, negA[:, :], R["gdn_A_log"], R["gdn_A_log"].ap[j, d:d + 1, g * 8:(g + 1) * 8].to_broadcast([64, 8]))
                    k.ld(dtb, dtb[:, :], R["gdn_dt_bias"],
                         R["gdn_dt_bias"].ap[j, d:d + 1, g * 8:(g + 1) * 8].to_broadcast([64, 8]))
                    k.act(negA, negA[:, :], negA, negA[:, :], AF.Exp)
                    k.tsc("dve", negA, negA[:, :], negA, negA[:, :], -1.0, None, ALU.mult)
                    if d == 1:
                        ofw = k.sb(PD, [64, 1024], F32, "gofw")
                        zt = k.sb(PD, [64, 1024], F32, "gz")
                        mo = k.sb(PD, [64, 1024], BF16, "gmo")
                    for sq_ in seqs:
                        if sq_["kind"] == "sample":
                            core.init_state(R["state_gdn"],
                                            R["state_gdn"].ap[j, d, g * 8:(g + 1) * 8].rearrange("h k v -> k h v"))
                        else:
                            core.init_state()
                        chs = sq_["chunks"] if d == 0 else sq_["chunks"][::-1]
                        for runs in chs:
                            for qi in range(3):
                                c0 = qi * 2048 + g * 1024
                                X = cur[qi]
                                load_rows(k, m, X, lambda a, c, t=X: t[a:c, :], proj, c0, c0 + 1024, runs, 0)
                                load_rows(k, m, pv_, lambda a, c, t=pv_: t[a:c, :], proj, c0, c0 + 1024, runs, -1)
                                load_rows(k, m, nx_, lambda a, c, t=nx_: t[a:c, :], proj, c0, c0 + 1024, runs, +1)
                                k.tt("pool", X, X[:, :], X, X[:, :], cw[qi][1], cw[qi][1][:, :], ALU.mult)
                                k.tt("pool", pv_, pv_[:, :], pv_, pv_[:, :], cw[qi][0], cw[qi][0][:, :], ALU.mult)
                                k.tt("dve", nx_, nx_[:, :], nx_, nx_[:, :], cw[qi][2], cw[qi][2][:, :], ALU.mult)
                                k.tt("pool", X, X[:, :], X, X[:, :], pv_, pv_[:, :], ALU.add)
                                k.tt("pool", X, X[:, :], X, X[:, :], nx_, nx_[:, :], ALU.add)
                                k.act(S1, S1[:, :], X, X[:, :], AF.Sigmoid)
                                k.tt("pool", X, X[:, :], X, X[:, :], S1, S1[:, :], ALU.mult)
                            load_rows(k, m, bl, lambda a, c, t=bl: t[a:c, :], proj, 8192 + d * 16 + g * 8,
                                      8192 + d * 16 + g * 8 + 8, runs, 0)
                            load_rows(k, m, al, lambda a, c, t=al: t[a:c, :], proj, 8224 + d * 16 + g * 8,
                                      8224 + d * 16 + g * 8 + 8, runs, 0)
                            for qi, scl in ((0, 128.0 ** -0.5), (1, 1.0)):
                                X = cur[qi]
                                k.tt("pool", S1, S1[:, :], X, X[:, :], X, X[:, :], ALU.mult)

                                def red(e, o=ssq[:, :], i=S1[:, :].rearrange("p (h v) -> p h v", v=128)):
                                    return e.tensor_reduce(o, i, AX.X, ALU.add)
                                k.s.op("dve", red, reads=(S1,), writes=(ssq,))
                                k.tsc("dve", ssq, ssq[:, :], ssq, ssq[:, :], EPS, None, ALU.add)
                                k.act(ssq, ssq[:, :], ssq, ssq[:, :], AF.Sqrt)

                                def rec(e, o=ssq[:, :]):
                                    return e.reciprocal(o, o)
                                k.s.op("dve", rec, reads=(ssq,), writes=(ssq,))
                                if scl != 1.0:
                                    k.tsc("dve", ssq, ssq[:, :], ssq, ssq[:, :], scl, None, ALU.mult)
                                x3 = X[:, :].rearrange("p (h v) -> p h v", v=128)
                                k.tt("pool", X, x3, X, x3, ssq, ssq[:, :].unsqueeze(2).to_broadcast([64, 8, 128]), ALU.mult)
                            Q, Kk, Vv = cur[0], cur[1], cur[2]
                            k.act(beta, beta[:, :], bl, bl[:, :], AF.Sigmoid)
                            k.tt("dve", gt, gt[:, :], al, al[:, :], dtb, dtb[:, :], ALU.add)
                            k.act(gt, gt[:, :], gt, gt[:, :], AF.Exp)
                            k.act(gt, gt[:, :], gt, gt[:, :], AF.Ln, bias=1.0)
                            k.tt("dve", gt, gt[:, :], gt, gt[:, :], negA, negA[:, :], ALU.mult)
                            p0, p1 = m.pd[0], m.pd[1]
                            k.mm(p0, p0[0:64, 0:8], LEf, LEf[:, :], gt, gt[:, :])
                            k.mm(p0, p0[0:64, 8:16], LTof, LTof[:, :], gt, gt[:, :], acc=True)
                            k.mm(p0, p0[:, 16:24], ones_f, ones_f[:, :], gt, gt[:, :], acc=True)
                            k.act(egc, egc[:, :], p0, p0[0:64, 0:8], AF.Exp)
                            k.act(edec, edec[:, :], p0, p0[0:64, 8:16], AF.Exp)
                            k.act(zs, zs[:, :], p0, p0[:, 16:24], AF.Exp)
                            k.tt("pool", gLE, gLE[:, :, :], m.masks["LEf%d" % d],
                                 m.masks["LEf%d" % d][0:64, 0:64].unsqueeze(1).to_broadcast([64, 8, 64]), gt,
                                 gt[:, :].unsqueeze(2).to_broadcast([64, 8, 64]), ALU.mult)
                            k.tt("pool", gLT, gLT[:, :, :], LTof, LTof[0:64, 0:64].unsqueeze(1).to_broadcast([64, 8, 64]), gt,
                                 gt[:, :].unsqueeze(2).to_broadcast([64, 8, 64]), ALU.mult)
                            k.mm(p1, p1[0:64, :], LTof, LTof[:, :], gLE, gLE[:, :, :].rearrange("p h t -> p (h t)"))
                            k.act(EMe, EMe[:, :, :], p1, p1[0:64, :].rearrange("p (h t) -> p h t", t=64), AF.Exp)
                            k.mm(p1, p1[0:64, :], LEf, LEf[:, :], gLT, gLT[:, :, :].rearrange("p h t -> p (h t)"))
                            k.act(EMt, EMt[:, :, :], p1, p1[0:64, :].rearrange("p (h t) -> p h t", t=64), AF.Exp)
                            k.tt("pool", EMs, EMs[:, :, :], EMe, EMe[:, :, :], m.masks["LTf%d" % d],
                                 m.masks["LTf%d" % d][0:64, 0:64].unsqueeze(1).to_broadcast([64, 8, 64]), ALU.mult)
                            k.tt("pool", EMe, EMe[:, :, :], EMe, EMe[:, :, :], LEf,
                                 LEf[0:64, 0:64].unsqueeze(1).to_broadcast([64, 8, 64]), ALU.mult)
                            k.tt("pool", EMt, EMt[:, :, :], EMt, EMt[:, :, :], LTof,
                                 LTof[0:64, 0:64].unsqueeze(1).to_broadcast([64, 8, 64]), ALU.mult)
                            k3 = Kk[:, :].rearrange("p (h v) -> p h v", v=128)
                            q3 = Q[:, :].rearrange("p (h v) -> p h v", v=128)
                            v3 = Vv[:, :].rearrange("p (h v) -> p h v", v=128)

                            def t3(nm):
                                return tl[nm][:, :].rearrange("p (h v) -> p h v", v=128)

                            def bcs(t_):
                                return t_[:, :].unsqueeze(2).to_broadcast([64, 8, 128])
                            k.cp("act", tl["k"], tl["k"][:, :], Kk, Kk[:, :])
                            k.cp("act", tl["q"], tl["q"][:, :], Q, Q[:, :])
                            k.tt("dve", tl["kb"], t3("kb"), Kk, k3, beta, bcs(beta), ALU.mult)
                            k.tt("pool", tl["bv"], t3("bv"), Vv, v3, beta, bcs(beta), ALU.mult)
                            k.tt("pool", tl["qe"], t3("qe"), Q, q3, egc, bcs(egc), ALU.mult)
                            k.tt("dve", tl["aop"], t3("aop"), Kk, k3, edec, bcs(edec), ALU.mult)
                            k.stt(coef, coef[:, :], beta, beta[:, :], -1.0, egc, egc[:, :], ALU.mult, ALU.mult)
                            k.tt("pool", tl["bop"], t3("bop"), Kk, k3, coef, bcs(coef), ALU.mult)
                            for ti_, nm in enumerate(("k", "kb", "q", "qe")):
                                for hh in range(HG):
                                    k.tr(pdb, pdb[:, (ti_ % 2) * 8 + hh, :], tl[nm], tl[nm][:, hh * 128:(hh + 1) * 128],
                                         m.identb, m.identb[0:64, 0:64], partial=not (hh == 0 and ti_ % 2 == 0))
                                k.cp("act", tT[nm], tT[nm][:, :, :], pdb, pdb[:, (ti_ % 2) * 8:(ti_ % 2) * 8 + 8, :])
                            for (dn, ln, rn, em, sgn) in (("NN", "k", "kb", EMs, -1.0), ("NNT", "kb", "k", EMt, -1.0),
                                                          ("PmT", "k", "q", EMe, 1.0)):
                                p_ = m.pd[0] if dn != "NNT" else m.pd[1]
                                for hh in range(HG):
                                    k.mm(p_, p_[0:64, hh * 64:(hh + 1) * 64], tT[ln], tT[ln][:, hh, :], tT[rn], tT[rn][:, hh, :],
                                         acc=(hh > 0))
                                k.stt(st_[dn], st_[dn][:, :, :], p_, p_[0:64, :].rearrange("p (h t) -> p h t", t=64), sgn,
                                      em, em[:, :, :], ALU.mult, ALU.mult)
                            o_t = core.step(dict(NN=st_["NN"], NNT=st_["NNT"], Bop=tl["bop"], U0rhs=tl["bv"], RopT=tT["qe"],
                                                 PmT=st_["PmT"], Aop=tl["aop"], zs=zs))
                            oc0 = g * 1024
                            if d == 0:
                                store_rows(k, m, R["ogd"].ap, oc0, oc0 + 1024, runs, o_t, lambda a, c, t=o_t: t[a:c, :])
                            else:
                                load_rows(k, m, ofw, lambda a, c, t=ofw: t[a:c, :], R["ogd"].ap, oc0, oc0 + 1024, runs)
                                load_rows(k, m, zt, lambda a, c, t=zt: t[a:c, :], proj, 6144 + oc0, 6144 + oc0 + 1024, runs)
                                k.tt("dve", ofw, ofw[:, :], ofw, ofw[:, :], o_t, o_t[:, :], ALU.add)
                                k.tt("pool", S1, S1[:, :], ofw, ofw[:, :], ofw, ofw[:, :], ALU.mult)

                                def red5(e, o=ssq[:, :], i=S1[:, :].rearrange("p (h v) -> p h v", v=128)):
                                    return e.tensor_reduce(o, i, AX.X, ALU.add)
                                k.s.op("dve", red5, reads=(S1,), writes=(ssq,))
                                k.tsc("dve", ssq, ssq[:, :], ssq, ssq[:, :], 1.0 / 128.0, EPS, ALU.mult, ALU.add)
                                k.act(ssq, ssq[:, :], ssq, ssq[:, :], AF.Sqrt)

                                def rec2(e, o=ssq[:, :]):
                                    return e.reciprocal(o, o)
                                k.s.op("dve", rec2, reads=(ssq,), writes=(ssq,))
                                o3 = ofw[:, :].rearrange("p (h v) -> p h v", v=128)
                                k.tt("pool", ofw, o3, ofw, o3, ssq, ssq[:, :].unsqueeze(2).to_broadcast([64, 8, 128]), ALU.mult)
                                k.tt("pool", ofw, o3, ofw, o3, gg, gg[:, :].unsqueeze(1).to_broadcast([64, 8, 128]), ALU.mult)
                                k.act(S1, S1[:, :], zt, zt[:, :], AF.Sigmoid)
                                k.tt("pool", S1, S1[:, :], S1, S1[:, :], zt, zt[:, :], ALU.mult)
                                k.tt("dve", mo, mo[:, :], ofw, ofw[:, :], S1, S1[:, :], ALU.mult)
                                store_rows(k, m, R["mixd"].ap, oc0, oc0 + 1024, runs, mo, lambda a, c, t=mo: t[a:c, :])
                        if sq_["kind"] == "prompt":
                            core.store_state(R["ns_gdn"],
                                             R["ns_gdn"].ap[sq_["idx"], j, d, g * 8:(g + 1) * 8].rearrange("h k v -> k h v"))
                    k.s.barrier()


_CACHE = {}


def kernel(**inputs):
    cfg = Cfg(depth=4, ts=4096, np_=4, tp=256)
    n = 8
    if "kb" not in _CACHE:
        _CACHE["kb"] = build(cfg)
    kb = _CACHE["kb"]
    consts = make_consts()
    in_maps = [core_inputs(inputs, c, cfg, consts) for c in range(n)]
    res = run_bass_kernel_spmd(kb.nc, in_maps, core_ids=list(range(n))).results
    y_sample = np.stack([res[c]["y"][:cfg.ts] for c in range(n)]).astype(np.float32)
    y_prompt = np.concatenate([res[c]["y"][cfg.ts:].reshape(cfg.np, cfg.tp, D) for c in range(n)]).astype(np.float32)
    ns_gla = np.concatenate([res[c]["ns_gla"] for c in range(n)]).astype(np.float32)
    ns_rwkv = np.concatenate([res[c]["ns_rwkv"] for c in range(n)]).astype(np.float32)
    ns_gdn = np.concatenate([res[c]["ns_gdn"] for c in range(n)]).astype(np.float32)
    return (y_prompt, y_sample, ns_gla, ns_rwkv, ns_gdn)
```

```python
from contextlib import ExitStack
import numpy as np
import concourse.bass as bass
import concourse.mybir as mybir
from concourse.bass_utils import run_bass_kernel_spmd

F32 = mybir.dt.float32
BF16 = mybir.dt.bfloat16
AF = mybir.ActivationFunctionType
ALU = mybir.AluOpType
AX = mybir.AxisListType

import os
DBG = int(os.environ.get("KDBG", "0"))
SELF_SYNC = True
NDMASEM = 12


class T:
    __slots__ = ("ap", "w", "r", "war", "name", "const", "fw")

    def __init__(self, ap, name="", const=False):
        self.ap = ap
        self.w = []
        self.r = []
        self.war = []
        self.name = name
        self.const = const
        self.fw = None

    def __getitem__(self, idx):
        return self.ap[idx]


class TV:
    def __init__(self, base, ap):
        object.__setattr__(self, "base", base)
        object.__setattr__(self, "ap", ap)

    def __getattr__(self, nm):
        return getattr(self.base, nm)

    def __setattr__(self, nm, v):
        setattr(self.base, nm, v)

    def __getitem__(self, idx):
        return self.ap[idx]


class Op:
    __slots__ = ("eng", "fn", "deps", "signal", "is_dma", "sem", "val", "pos", "prewait")

    def __init__(self, eng, fn, is_dma):
        self.eng = eng
        self.fn = fn
        self.is_dma = is_dma
        self.deps = []
        self.signal = is_dma
        self.sem = None
        self.val = 0
        self.pos = 0
        self.prewait = None


class Sched:
    ENGS = ("pe", "dve", "act", "pool", "sp")

    def __init__(self, nc):
        self.nc = nc
        self.q = {e: [] for e in self.ENGS}
        self.last = {e: None for e in self.ENGS}
        self.dmas_since_barrier = []
        self.nops = 0
        self.cap = None

    def op(self, eng, fn, reads=(), writes=(), pwrites=(), dma=False):
        o = Op(eng, fn, dma)
        deps = o.deps
        for t in reads:
            if t.w:
                deps.extend(t.w)
        for t in writes:
            deps.extend(t.w)
            deps.extend(t.r)
            deps.extend(t.war)
        for t in pwrites:
            if t.r:
                t.war = t.r + t.w
                t.r = []
                t.w = []
            deps.extend(t.war)
            if t.fw is not None:
                deps.append(t.fw)
        for t in reads:
            if not t.const:
                t.r.append(o)
        for t in writes:
            t.war = t.r + t.w
            t.w = [o]
            t.r = []
            t.fw = o
        for t in pwrites:
            t.w.append(o)
        self.nops += 1
        if self.cap is not None:
            self.cap.append(o)
            return o
        self._append(o)
        return o

    def _append(self, o):
        o.pos = len(self.q[o.eng])
        self.q[o.eng].append(o)
        self.last[o.eng] = o
        if o.is_dma:
            self.dmas_since_barrier.append(o)

    def capture(self, fn):
        assert self.cap is None
        self.cap = []
        try:
            fn()
        finally:
            ops, self.cap = self.cap, None
        return ops

    def append_merged(self, A, B):
        ia = ib = 0
        la, lb = len(A), len(B)
        while ia < la or ib < lb:
            if ib >= lb or (ia < la and ia * lb <= ib * la):
                self._append(A[ia])
                ia += 1
            else:
                self._append(B[ib])
                ib += 1

    def dma(self, eng, out_t, out_ap, in_t, in_ap, partial=False, **kw):
        def fn(e):
            return e.dma_start(out=out_ap, in_=in_ap, **kw)
        if partial:
            return self.op(eng, fn, reads=(in_t,), pwrites=(out_t,), dma=True)
        return self.op(eng, fn, reads=(in_t,), writes=(out_t,), dma=True)

    def barrier(self):
        lasts = [self.last[e] for e in self.ENGS if self.last[e] is not None]
        dm = list(self.dmas_since_barrier)
        self.dmas_since_barrier = []
        for e in self.ENGS:
            o = Op(e, None, False)
            o.deps = [x for x in lasts if x.eng != e and not x.is_dma] + dm
            o.pos = len(self.q[e])
            self.q[e].append(o)

    def emit(self):
        nc = self.nc
        for e in self.ENGS:
            for o in self.q[e]:
                for d in o.deps:
                    if d.is_dma:
                        continue
                    if d.eng != o.eng:
                        d.signal = True
                    elif o.eng != "pe" and (SELF_SYNC or o.is_dma):
                        d.signal = True
        esem = {e: nc.alloc_semaphore("sem_" + e) for e in self.ENGS}
        dsem = {e: [nc.alloc_semaphore("dsem_%s_%d" % (e, i)) for i in range(NDMASEM)]
                for e in self.ENGS if any(o.is_dma for o in self.q[e])}
        for e in self.ENGS:
            cnt = 0
            nd = 0
            for o in self.q[e]:
                if o.is_dma:
                    o.sem = dsem[e][nd % NDMASEM]
                    o.val = 16 * (nd // NDMASEM + 1)
                    if nd >= NDMASEM:
                        o.prewait = (o.sem, 16 * (nd // NDMASEM))
                    nd += 1
                elif o.signal:
                    cnt += 1
                    o.sem = esem[e]
                    o.val = cnt
        engobj = {"pe": "tensor", "dve": "vector", "act": "scalar", "pool": "gpsimd", "sp": "sync"}
        with nc.Block() as block:
            for e in self.ENGS:
                ops = self.q[e]
                if not ops:
                    continue

                def body(eng, ops=ops, e=e):
                    waited = {}
                    for o in ops:
                        need = {}
                        if o.prewait is not None:
                            need[id(o.prewait[0])] = (o.prewait[0], o.prewait[1])
                        for d in o.deps:
                            if (not d.is_dma) and d.eng == e:
                                if e == "pe" or not d.signal:
                                    continue
                            k = id(d.sem)
                            if k not in need or need[k][1] < d.val:
                                need[k] = (d.sem, d.val)
                        for k, (sem, val) in need.items():
                            if waited.get(k, 0) < val:
                                eng.wait_ge(sem, val)
                                waited[k] = val
                        if o.fn is None:
                            continue
                        ins = o.fn(eng)
                        if o.is_dma:
                            ins.then_inc(o.sem, 16)
                        elif o.signal:
                            ins.then_inc(o.sem, 1)
                    fin = {}
                    for o in ops:
                        if o.is_dma:
                            fin[id(o.sem)] = (o.sem, o.val)
                    for k, (sem, val) in fin.items():
                        if waited.get(k, 0) < val:
                            eng.wait_ge(sem, val)
                            waited[k] = val

                getattr(block, engobj[e])(body)


D = 2048
GRID_W = 64
CH = 64
EVEN_SIZES = (512, 512, 1024, 1024, 32, 1024, 1024, 1024, 1024, 128, 128)
ODD_SIZES = (2048, 2048, 2048, 2048, 32, 32)
EVEN_IN = sum(EVEN_SIZES)
ODD_IN = sum(ODD_SIZES)
EPS = 1e-6
GN_EPS = 64e-5


class Cfg:
    def __init__(self, depth=4, ts=4096, np_=4, tp=256):
        self.depth = depth
        self.ts = ts
        self.np = np_
        self.tp = tp
        self.ntok = ts + np_ * tp
        assert self.ntok % 128 == 0 and ts % 128 == 0
        self.ntile = self.ntok // 128
        self.n_even = (depth + 1) // 2
        self.n_odd = depth // 2


def make_consts():
    c = {}
    c["ident"] = np.eye(128, dtype=np.float32)
    sel = np.zeros((2, 2, 128), np.float32)
    sel[0, 0, :] = 1.0
    sel[1, 1, :] = 1.0
    c["sel"] = sel
    i = np.arange(64)
    le0 = (i[:, None] <= i[None, :]).astype(np.float32)
    lt0 = (i[:, None] < i[None, :]).astype(np.float32)
    c["masks"] = np.stack([le0, le0.T.copy(), lt0, lt0.T.copy()]).astype(np.float32)
    c["zeros"] = np.zeros((1, ODD_IN), np.float32)
    blk = lambda n: ((i[:, None] // n) == (i[None, :] // n)).astype(np.float32)
    c["bmask"] = np.stack([blk(8), blk(16) - blk(8), blk(32) - blk(16), 1.0 - blk(32)]).astype(np.float32)
    return c


class K:
    def __init__(self, cfg):
        self.cfg = cfg
        self.nc = bass.Bass("TRN2", target_bir_lowering=False)
        self.s = Sched(self.nc)
        self.es = ExitStack()
        self.uid = 0

    def dram(self, name, shape, kind, dt=F32):
        h = self.nc.dram_tensor(name, list(shape), dt, kind=kind)
        return T(h.ap(), name)

    def sb(self, stack, shape, dt=F32, name=None, const=False):
        self.uid += 1
        nm = "%s_%d" % (name or "t", self.uid)
        ap = stack.enter_context(self.nc.sbuf_tensor(nm, list(shape), dt))
        return T(ap, nm, const=const)

    def ps(self, stack, shape, dt=F32, name=None):
        self.uid += 1
        nm = "%s_%d" % (name or "p", self.uid)
        ap = stack.enter_context(self.nc.psum_tensor(nm, list(shape), dt))
        return T(ap, nm)

    def mm(self, out_t, out_ap, l_t, l_ap, r_t, r_ap, start=True, stop=True, acc=False):
        def fn(e):
            return e.matmul(out_ap, l_ap, r_ap, start=start, stop=stop)
        return self.s.op("pe", fn, reads=(l_t, r_t), writes=(out_t,)) if not acc else \
            self.s.op("pe", fn, reads=(l_t, r_t), pwrites=(out_t,))

    def tr(self, out_t, out_ap, in_t, in_ap, id_t, id_ap, partial=True):
        def fn(e):
            return e.transpose(out_ap, in_ap, id_ap)
        if partial:
            return self.s.op("pe", fn, reads=(in_t, id_t), pwrites=(out_t,))
        return self.s.op("pe", fn, reads=(in_t, id_t), writes=(out_t,))

    def act(self, out_t, out_ap, in_t, in_ap, func, bias=None, scale=None, accum=None, extra_reads=(),
            partial=False, eng="act"):
        kw = {}
        if bias is not None:
            kw["bias"] = bias
        if scale is not None:
            kw["scale"] = scale
        wr = [out_t]
        if accum is not None:
            kw["accum_out"] = accum[1]
            wr.append(accum[0])

        def fn(e):
            return e.activation(out_ap, in_ap, func, **kw)
        if partial:
            return self.s.op(eng, fn, reads=(in_t,) + tuple(extra_reads), pwrites=tuple(wr))
        return self.s.op(eng, fn, reads=(in_t,) + tuple(extra_reads), writes=tuple(wr))

    def tsc(self, eng, out_t, out_ap, in_t, in_ap, s1, s2, op0, op1=None, extra_reads=(), partial=False):
        def fn(e):
            if op1 is None:
                return e.tensor_scalar(out_ap, in_ap, s1, None, op0)
            return e.tensor_scalar(out_ap, in_ap, s1, s2, op0, op1)
        if partial:
            return self.s.op(eng, fn, reads=(in_t,) + tuple(extra_reads), pwrites=(out_t,))
        return self.s.op(eng, fn, reads=(in_t,) + tuple(extra_reads), writes=(out_t,))

    def tt(self, eng, out_t, out_ap, a_t, a_ap, b_t, b_ap, op, partial=False):
        def fn(e):
            return e.tensor_tensor(out_ap, a_ap, b_ap, op)
        if partial:
            return self.s.op(eng, fn, reads=(a_t, b_t), pwrites=(out_t,))
        return self.s.op(eng, fn, reads=(a_t, b_t), writes=(out_t,))

    def stt(self, out_t, out_ap, a_t, a_ap, scalar, b_t, b_ap, op0, op1, extra_reads=(), partial=False):
        def fn(e):
            return e.scalar_tensor_tensor(out_ap, a_ap, scalar, b_ap, op0, op1)
        if partial:
            return self.s.op("dve", fn, reads=(a_t, b_t) + tuple(extra_reads), pwrites=(out_t,))
        return self.s.op("dve", fn, reads=(a_t, b_t) + tuple(extra_reads), writes=(out_t,))

    def cp(self, eng, out_t, out_ap, in_t, in_ap, partial=False):
        if eng == "act":
            def fn(e):
                return e.copy(out_ap, in_ap)
        else:
            def fn(e):
                return e.tensor_copy(out_ap, in_ap)
        if partial:
            return self.s.op(eng, fn, reads=(in_t,), pwrites=(out_t,))
        return self.s.op(eng, fn, reads=(in_t,), writes=(out_t,))

    def memset(self, eng, t, ap, val):
        def fn(e):
            return e.memset(ap, val)
        return self.s.op(eng, fn, writes=(t,))

    def ld(self, out_t, out_ap, in_t, in_ap, eng="sp", partial=False, **kw):
        return self.s.dma(eng, out_t, out_ap, in_t, in_ap, partial=partial, **kw)


class StopBuild(Exception):
    pass


_CKPT = [0]
KSTOPG = int(os.environ.get("KSTOPG", "0"))


def ckpt():
    _CKPT[0] += 1
    if KSTOPG and _CKPT[0] >= KSTOPG:
        raise StopBuild()


def build(cfg, mixer_mode="full", stop=None):
    k = K(cfg)
    try:
        _build(k, cfg, mixer_mode, stop)
    except StopBuild:
        pass
    k.s.emit()
    return k


def _build(k, cfg, mixer_mode, stop):
    def stage(name):
        if stop == name:
            raise StopBuild()
    nc = k.nc
    s = k.s
    NT = cfg.ntile
    depth = cfg.depth
    xin = k.dram("xin", [cfg.ntok, D], "ExternalInput")
    cvec = k.dram("cvec", [2, D], "ExternalInput")
    norm_g = k.dram("norm_g", [depth, D], "ExternalInput")
    w_ada = k.dram("w_ada", [depth, D, 3 * D], "ExternalInput")
    b_ada = k.dram("b_ada", [depth, 3 * D], "ExternalInput")
    w_in_even = k.dram("w_in_even", [cfg.n_even, D, EVEN_IN], "ExternalInput")
    w_in_odd = k.dram("w_in_odd", [max(cfg.n_odd, 1), D, ODD_IN], "ExternalInput")
    w_out = k.dram("w_out", [depth, D, D], "ExternalInput")
    final_g = k.dram("final_g", [1, D], "ExternalInput")
    ident_d = k.dram("ident", [128, 128], "ExternalInput")
    sel_d = k.dram("sel", [2, 2, 128], "ExternalInput")
    masks_d = k.dram("masks", [4, 64, 64], "ExternalInput")
    bmask_d = k.dram("bmask", [4, 64, 64], "ExternalInput")
    zeros = k.dram("zeros", [1, ODD_IN], "ExternalInput")
    ne, no_ = cfg.n_even, max(cfg.n_odd, 1)
    P_ = {}
    for nm, shp in (("state_gla", [ne, 2, 4, 128, 256]), ("state_rwkv", [ne, 2, 16, 64, 64]),
                    ("state_gdn", [no_, 2, 16, 128, 128]),
                    ("gla_w2", [ne, 2, 16, 512]), ("gla_b", [ne, 2, 512]), ("gla_g", [ne, 256]),
                    ("rwkv_mu", [ne, 3, 2, 1024]), ("rwkv_w0", [ne, 2, 1024]), ("rwkv_w2", [ne, 2, 64, 1024]),
                    ("rwkv_a0", [ne, 2, 1024]), ("rwkv_a2", [ne, 2, 64, 1024]), ("rwkv_k_k", [ne, 1024]),
                    ("rwkv_k_a", [ne, 1024]), ("rwkv_r_k", [ne, 1024]), ("rwkv_gn_g", [ne, 1024]),
                    ("rwkv_gn_b", [ne, 1024]), ("gdn_conv_w", [no_, 3, 6144]), ("gdn_A_log", [no_, 2, 16]),
                    ("gdn_dt_bias", [no_, 2, 16]), ("gdn_g", [no_, 128])):
        P_[nm] = k.dram(nm, shp, "ExternalInput")
    P_["ns_gla"] = k.dram("ns_gla", [cfg.np, ne, 2, 4, 128, 256], "ExternalOutput")
    P_["ns_rwkv"] = k.dram("ns_rwkv", [cfg.np, ne, 2, 16, 64, 64], "ExternalOutput")
    P_["ns_gdn"] = k.dram("ns_gdn", [cfg.np, no_, 2, 16, 128, 128], "ExternalOutput")
    P_["ogla"] = k.dram("ogla", [cfg.ntok, 1024], "Internal")
    P_["orw"] = k.dram("orw", [cfg.ntok, 1024], "Internal")
    P_["ogd"] = k.dram("ogd", [cfg.ntok, 2048], "Internal")
    P_["bsum"] = k.dram("bsum", [cfg.ntok, 16], "Internal")
    P_["zeros"] = zeros
    yout = k.dram("y", [cfg.ntok, D], "ExternalOutput")
    xcur = [k.dram("xcur%d" % i, [cfg.ntok, D], "Internal") for i in range(2)]
    proj = k.dram("proj", [cfg.ntok, ODD_IN], "Internal")
    mixd = k.dram("mixd", [cfg.ntok, D], "Internal", BF16)
    def rowtiles(t, n):
        return [T(t.ap, "%s_r%d" % (t.name, i)) for i in range(n)]
    xcur_rt = [rowtiles(x, NT) for x in xcur]
    proj_rt = rowtiles(proj, NT)
    mix_rt = rowtiles(mixd, NT)
    yout_rt = rowtiles(yout, NT)

    def cond_of_tile(tt):
        return 0 if tt * 128 < cfg.ts else 1

    with ExitStack() as glob:
        ident = k.sb(glob, [128, 128], F32, "ident", const=True)
        k.ld(ident, ident[:, :], ident_d, ident_d[:, :])
        gT = k.sb(glob, [128, depth * 16], F32, "gT")
        bT = k.sb(glob, [128, depth * 48], F32, "bT")
        sT = k.sb(glob, [128, 16, 2], F32, "sT")
        fg_bc = k.sb(glob, [128, D], F32, "fg_bc")
        k.ld(fg_bc, fg_bc[:, :], final_g, final_g.ap[0:1, :].to_broadcast([128, D]))
        sel = k.sb(glob, [2, 2, 128], F32, "sel", const=True)
        k.ld(sel, sel[:, :, :], sel_d, sel_d[:, :, :])
        identb = k.sb(glob, [128, 128], BF16, "identb", const=True)
        k.cp("dve", identb, identb[:, :], ident, ident[:, :])
        maskf = k.sb(glob, [64, 4, 64], F32, "maskf", const=True)
        k.ld(maskf, maskf[:, :, :], masks_d, masks_d.ap.rearrange("m s t -> s m t"))
        maskb = k.sb(glob, [64, 4, 64], BF16, "maskb", const=True)
        k.cp("dve", maskb, maskb[:, :, :], maskf, maskf[:, :, :])
        bmaskf = k.sb(glob, [64, 4, 64], F32, "bmaskf", const=True)
        k.ld(bmaskf, bmaskf[:, :, :], bmask_d, bmask_d.ap.rearrange("m s t -> s m t"))
        masks = {}
        masks["bmask"] = [T(bmaskf.ap[:, mi, :], "bm%d" % mi, const=True) for mi in range(4)]
        masks["identf64"] = T(ident.ap[0:64, 0:64], "identf64", const=True)
        for mi, mn in enumerate(("LE0", "LE1", "LT0", "LT1")):
            masks[mn] = T(maskb.ap[:, mi, :], mn, const=True)
            masks[mn[0:2] + "f" + mn[2]] = T(maskf.ap[:, mi, :], mn + "f", const=True)
        with ExitStack() as st0:
            rows = k.sb(st0, [64, 128], F32, "rows")
            pst = k.ps(st0, [128, 512], F32, "pst")
            n_r = depth * 16
            k.ld(rows, rows[0:n_r, :], norm_g, norm_g.ap.rearrange("l (c p) -> (l c) p", p=128))
            k.tr(pst, pst[:, 0:n_r], rows, rows[0:n_r, :], ident, ident[0:n_r, 0:n_r], partial=False)
            k.cp("dve", gT, gT[:, :], pst, pst[:, 0:n_r])
            for l in range(depth):
                k.ld(rows, rows[0:48, :], b_ada, b_ada.ap[l:l + 1, :].rearrange("o (c p) -> (o c) p", p=128))
                k.tr(pst, pst[:, 0:48], rows, rows[0:48, :], ident, ident[0:48, 0:48], partial=False)
                k.cp("dve", bT, bT[:, l * 48:(l + 1) * 48], pst, pst[:, 0:48], partial=True)
            cv = k.sb(st0, [2, D], F32, "cv")
            sg = k.sb(st0, [2, D], F32, "sg")
            k.ld(cv, cv[:, :], cvec, cvec[:, :])
            k.act(sg, sg[:, :], cv, cv[:, :], AF.Sigmoid)
            k.tt("dve", sg, sg[:, :], sg, sg[:, :], cv, cv[:, :], ALU.mult)
            for c in range(16):
                k.tr(pst, pst[:, 64 + 2 * c:64 + 2 * c + 2], sg, sg[0:2, c * 128:(c + 1) * 128], ident, ident[0:2, 0:2],
                     partial=(c > 0))
            k.cp("dve", sT, sT[:, :, :], pst, pst[:, 64:96].rearrange("p (c t) -> p c t", t=2))
        s.barrier()
        stage("consts")

        for layer in range(depth + 1):
            last = layer == depth
            with ExitStack() as L:
                gate_bc = None
                A_T = B_T = None
                if not last:
                    A_T = k.sb(L, [128, 2, 16], F32, "A_T")
                    B_T = k.sb(L, [128, 2, 16], F32, "B_T")
                if layer > 0:
                    gate_bc = k.sb(L, [128, 2, D], F32, "gate_bc")
                with ExitStack() as M:
                    slab = [k.sb(M, [128, 16, 512], F32, "adaslab") for _ in range(2)]
                    psm = k.ps(M, [128, 512], F32, "psm")
                    psg = k.ps(M, [128, 512], F32, "psg")
                    nsl = 0
                    if not last:
                        modT = k.sb(M, [128, 32, 2], F32, "modT")
                        for sl in range(8):
                            sb_ = slab[nsl % 2]
                            nsl += 1
                            k.ld(sb_, sb_[:, :, :], w_ada,
                                 w_ada.ap[layer, :, sl * 512:(sl + 1) * 512].rearrange("(c p) n -> p c n", p=128))
                            for j in range(4):
                                blk = sl * 4 + j
                                for kc in range(16):
                                    k.mm(psm, psm[:, 2 * blk:2 * blk + 2], sb_, sb_[:, kc, j * 128:(j + 1) * 128],
                                         sT, sT[:, kc, :], start=(kc == 0), stop=(kc == 15),
                                         acc=not (blk == 0 and kc == 0))
                        k.cp("dve", modT, modT[:, :, :], psm, psm[:, 0:64].rearrange("p (b t) -> p b t", t=2))
                        for cnd in range(2):
                            k.tt("dve", B_T, B_T[:, cnd, :], modT, modT[:, 0:16, cnd], bT,
                                 bT[:, layer * 48:layer * 48 + 16], ALU.add, partial=(cnd > 0))
                            k.tt("dve", A_T, A_T[:, cnd, :], modT, modT[:, 16:32, cnd], bT,
                                 bT[:, layer * 48 + 16:layer * 48 + 32], ALU.add, partial=(cnd > 0))
                        for cnd in range(2):
                            k.stt(A_T, A_T[:, cnd, :], A_T, A_T[:, cnd, :], 1.0, gT, gT[:, layer * 16:(layer + 1) * 16],
                                  ALU.add, ALU.mult, partial=True)
                    if layer > 0:
                        pl = layer - 1
                        grow = k.sb(M, [2, D], F32, "grow")
                        brow = k.sb(M, [2, D], F32, "brow")
                        for r in range(2):
                            k.ld(brow, brow[r:r + 1, :], b_ada, b_ada.ap[pl:pl + 1, 2 * D:3 * D], partial=(r > 0))
                        for sl in range(4):
                            sb_ = slab[nsl % 2]
                            nsl += 1
                            k.ld(sb_, sb_[:, :, :], w_ada,
                                 w_ada.ap[pl, :, 2 * D + sl * 512:2 * D + (sl + 1) * 512].rearrange("(c p) n -> p c n", p=128))
                            for kc in range(16):
                                k.mm(psg, psg[0:2, :], sT, sT[:, kc, :], sb_, sb_[:, kc, :],
                                     start=(kc == 0), stop=(kc == 15), acc=(kc > 0))
                            k.tt("dve", grow, grow[:, sl * 512:(sl + 1) * 512], psg, psg[0:2, :], brow,
                                 brow[:, sl * 512:(sl + 1) * 512], ALU.add, partial=(sl > 0))
                        for cnd in range(2):
                            for sl in range(4):
                                k.mm(psg, psg[:, :], sel, sel[:, cnd, :], grow, grow[:, sl * 512:(sl + 1) * 512])
                                k.cp("dve", gate_bc, gate_bc[:, cnd, sl * 512:(sl + 1) * 512], psg, psg[:, :],
                                     partial=not (cnd == 0 and sl == 0))
                s.barrier()
                stage("mod%d" % layer)
                lin_phase(k, cfg, layer, L, dict(
                    xin=xin, xcur=xcur, xcur_rt=xcur_rt, proj=proj, proj_rt=proj_rt, mixd=mixd, mix_rt=mix_rt,
                    yout=yout, yout_rt=yout_rt, w_in_even=w_in_even, w_in_odd=w_in_odd, w_out=w_out,
                    ident=ident, identb=identb, A_T=A_T, B_T=B_T, gate_bc=gate_bc, fg_bc=fg_bc, cond_of_tile=cond_of_tile))
                s.barrier()
                stage("lin%d" % layer)
            if True:
                if not last:
                    RR = dict(proj=proj, proj_rt=proj_rt, mixd=mixd, mix_rt=mix_rt, ident=ident, identb=identb,
                              masks=masks)
                    RR.update(P_)
                    mixer_phase(k, cfg, layer, RR, mixer_mode)
                    s.barrier()


def lin_phase(k, cfg, layer, L, R):
    s = k.s
    depth = cfg.depth
    last = layer == depth
    first = layer == 0
    NT = cfg.ntile
    ident = R["ident"]
    identb = R["identb"]
    x_src = R["xin"] if layer <= 1 else R["xcur"][(layer - 1) % 2]
    x_src_rt = None if layer <= 1 else R["xcur_rt"][(layer - 1) % 2]
    x_dst = R["xcur"][layer % 2]
    x_dst_rt = R["xcur_rt"][layer % 2]
    if not last:
        even = layer % 2 == 0
        win = R["w_in_even"] if even else R["w_in_odd"]
        NOUT = EVEN_IN if even else ODD_IN
        nslab = (NOUT + 511) // 512
    GT = 8
    groups = []
    t0 = 0
    nts = cfg.ts // 128
    while t0 < nts:
        g = min(GT, nts - t0)
        groups.append((t0, g))
        t0 += g
    while t0 < NT:
        g = min(GT, NT - t0)
        groups.append((t0, g))
        t0 += g
    with ExitStack() as P:
        hT = k.sb(P, [128, 16, GT * 128], BF16, "hT") if not last else None
        xt = [k.sb(P, [128, D], F32, "xt") for _ in range(2)]
        xn = [k.sb(P, [128, 512], F32, "xn") for _ in range(2)]
        ss = [k.sb(P, [128, 1], F32, "ss") for _ in range(2)]
        rstd = [k.sb(P, [128, 1], F32, "rstd") for _ in range(2)]
        junk = k.sb(P, [128, D], BF16, "junk")
        pt = [k.ps(P, [128, 512], F32, "pt") for _ in range(2)]
        pm = [k.ps(P, [128, 512], F32, "pm") for _ in range(4)]
        if not last:
            wsl = [k.sb(P, [128, 16, 512], BF16, "wsl") for _ in range(2)]
            po = [k.sb(P, [128, 512], F32, "po") for _ in range(2)]
        if last:
            yb = [k.sb(P, [128, D], F32, "yb") for _ in range(2)]
        if not first:
            ptb = [k.ps(P, [128, 8, 128], BF16, "ptb") for _ in range(2)]
            mt = [k.sb(P, [128, D], BF16, "mt") for _ in range(2)]
            mixT = [k.sb(P, [128, 16, 128], BF16, "mixT") for _ in range(2)]
            tmp = [k.sb(P, [128, 512], F32, "tmp") for _ in range(2)]
            wout = [k.sb(P, [128, 16, 512], BF16, "wout") for _ in range(4)]
            for oc in range(4):
                k.ld(wout[oc], wout[oc][:, :, :], R["w_out"],
                     R["w_out"].ap[layer - 1, :, oc * 512:(oc + 1) * 512].rearrange("(c p) n -> p c n", p=128),
                     eng="pool")
        nws = 0
        npm = 0
        npo = 0
        ntl = 0
        nxn = 0
        for (g0, gn) in groups:
            for ti in range(gn):
                tt = g0 + ti
                cnd = R["cond_of_tile"](tt)
                b = ntl % 2
                ntl += 1
                X = xt[b]
                rows = slice(tt * 128, (tt + 1) * 128)
                if first:
                    k.ld(X, X[:, :], x_src, x_src.ap[rows, :])
                else:
                    k.ld(X, X[:, :], x_src_rt[tt] if x_src_rt is not None else x_src, x_src.ap[rows, :])
                    M_ = mt[b]
                    k.ld(M_, M_[:, :], R["mix_rt"][tt], R["mixd"].ap[rows, :])
                    MT = mixT[b]
                    for h2 in range(2):
                        p_ = ptb[h2]
                        for j in range(8):
                            c = h2 * 8 + j
                            k.tr(p_, p_[:, j, :], M_, M_[:, c * 128:(c + 1) * 128], identb, identb[:, :],
                                 partial=(j > 0))
                        k.cp("act", MT, MT[:, h2 * 8:(h2 + 1) * 8, :], p_, p_[:, :, :], partial=(h2 > 0))
                    for oc in range(4):
                        w = wout[oc]
                        p_ = pm[npm % 4]
                        npm += 1
                        for kc in range(16):
                            k.mm(p_, p_[:, :], MT, MT[:, kc, :], w, w[:, kc, :], start=(kc == 0), stop=(kc == 15),
                                 acc=(kc > 0))
                        tm = tmp[oc % 2]
                        k.tt("dve", tm, tm[:, :], p_, p_[:, :], R["gate_bc"], R["gate_bc"][:, cnd, oc * 512:(oc + 1) * 512],
                             ALU.mult)
                        k.tt("pool", X, X[:, oc * 512:(oc + 1) * 512], tm, tm[:, :], X, X[:, oc * 512:(oc + 1) * 512],
                             ALU.add, partial=True)
                    if not last:
                        k.ld(x_dst_rt[tt], x_dst.ap[rows, :], X, X[:, :], eng="sp")
                k.act(junk, junk[:, :], X, X[:, :], AF.Square, accum=(ss[b], ss[b][:, :]))
                k.tsc("dve", rstd[b], rstd[b][:, :], ss[b], ss[b][:, :], 1.0 / D, EPS, ALU.mult, ALU.add)
                k.act(rstd[b], rstd[b][:, :], rstd[b], rstd[b][:, :], AF.Sqrt)

                def rec(e, o=rstd[b][:, :]):
                    return e.reciprocal(o, o)
                s.op("dve", rec, reads=(rstd[b],), writes=(rstd[b],))
                if last:
                    k.stt(yb[b], yb[b][:, :], X, X[:, :], rstd[b][:, 0:1], R["fg_bc"], R["fg_bc"][:, :], ALU.mult,
                          ALU.mult, extra_reads=(rstd[b],))
                    k.ld(R["yout_rt"][tt], R["yout"].ap[rows, :], yb[b], yb[b][:, :], eng="sp")
                    continue
                if DBG & 4:
                    continue
                for q4 in range(4):
                    xq = xn[nxn % 2]
                    nxn += 1
                    k.tsc("dve" if (q4 % 2 == 0 or DBG & 1) else "pool", xq, xq[:, :], X, X[:, q4 * 512:(q4 + 1) * 512],
                          rstd[b][:, 0:1], None, ALU.mult, extra_reads=(rstd[b],))
                    p_ = pt[q4 % 2]
                    for j in range(4):
                        k.tr(p_, p_[:, j * 128:(j + 1) * 128], xq, xq[:, j * 128:(j + 1) * 128], ident, ident[:, :],
                             partial=(j > 0))
                    for j in range(4):
                        c = q4 * 4 + j
                        eng = "act" if (j % 2 == 0 and DBG & 8) else "dve"
                        if DBG & 16:
                            continue
                        if eng == "act":
                            k.act(hT, hT[:, c, ti * 128:(ti + 1) * 128], p_, p_[:, j * 128:(j + 1) * 128], AF.Identity,
                                  bias=R["B_T"][:, cnd, c:c + 1], scale=R["A_T"][:, cnd, c:c + 1],
                                  extra_reads=(R["A_T"], R["B_T"]), partial=True)
                        else:
                            k.tsc("dve", hT, hT[:, c, ti * 128:(ti + 1) * 128], p_, p_[:, j * 128:(j + 1) * 128],
                                  R["A_T"][:, cnd, c:c + 1], R["B_T"][:, cnd, c:c + 1], ALU.mult, ALU.add,
                                  extra_reads=(R["A_T"], R["B_T"]), partial=True)
            if last or DBG & 2:
                continue
            for sl in range(nslab):
                c0 = sl * 512
                cw = min(512, NOUT - c0)
                w = wsl[nws % 2]
                nws += 1
                k.ld(w, w[:, :, 0:cw], win, win.ap[layer // 2, :, c0:c0 + cw].rearrange("(c p) n -> p c n", p=128),
                     eng="pool")
                for ti in range(gn):
                    tt = g0 + ti
                    p_ = pm[npm % 4]
                    npm += 1
                    for kc in range(16):
                        k.mm(p_, p_[:, 0:cw], hT, hT[:, kc, ti * 128:(ti + 1) * 128], w, w[:, kc, 0:cw],
                             start=(kc == 0), stop=(kc == 15), acc=(kc > 0))
                    o_ = po[npo % 2]
                    k.cp("act" if npo % 2 == 0 else "dve", o_, o_[:, 0:cw], p_, p_[:, 0:cw])
                    npo += 1
                    k.ld(R["proj_rt"][tt], R["proj"].ap[tt * 128:(tt + 1) * 128, c0:c0 + cw], o_, o_[:, 0:cw],
                         eng="sp", partial=True)


def seq_list(cfg, layer):
    even = layer % 2 == 0
    seqs = []
    nch = cfg.ts // 64
    R = cfg.ts // 64
    chunks = []
    for j in range(nch):
        if even:
            chunks.append([(0, 64, j * 64, 1, True, True)])
        elif R >= 64:
            assert R == 64
            chunks.append([(0, 64, j, 64, True, True)])
        else:
            cpc = 64 // R
            chunks.append([(m * R, R, j * cpc + m, 64, True, True) for m in range(cpc)])
    seqs.append(dict(kind="sample", idx=0, chunks=chunks))
    ncp = cfg.tp // 64
    for q in range(cfg.np):
        b = cfg.ts + q * cfg.tp
        seqs.append(dict(kind="prompt", idx=q,
                         chunks=[[(0, 64, b + j * 64, 1, j == 0, j == ncp - 1)] for j in range(ncp)]))
    return seqs


class MX:
    pass


def load_rows(k, m, dst_t, rowfn, src_ap, c0, c1, runs, shift=0):
    for (p0, n, r0, st, s0, s1) in runs:
        if shift == 0:
            k.ld(dst_t, rowfn(p0, p0 + n), m.dr, src_ap[r0:r0 + st * (n - 1) + 1:st, c0:c1], partial=True)
        elif shift == -1:
            if s0:
                k.ld(dst_t, rowfn(p0, p0 + 1), m.dr, m.zeros.ap[0:1, 0:c1 - c0], partial=True)
                if n > 1:
                    k.ld(dst_t, rowfn(p0 + 1, p0 + n), m.dr, src_ap[r0:r0 + st * (n - 2) + 1:st, c0:c1], partial=True)
            else:
                k.ld(dst_t, rowfn(p0, p0 + n), m.dr, src_ap[r0 - st:r0 - st + st * (n - 1) + 1:st, c0:c1], partial=True)
        else:
            if s1:
                k.ld(dst_t, rowfn(p0 + n - 1, p0 + n), m.dr, m.zeros.ap[0:1, 0:c1 - c0], partial=True)
                if n > 1:
                    k.ld(dst_t, rowfn(p0, p0 + n - 1), m.dr, src_ap[r0 + st:r0 + st + st * (n - 2) + 1:st, c0:c1],
                         partial=True)
            else:
                k.ld(dst_t, rowfn(p0, p0 + n), m.dr, src_ap[r0 + st:r0 + st + st * (n - 1) + 1:st, c0:c1], partial=True)


def store_rows(k, m, dst_ap, c0, c1, runs, src_t, rowfn):
    for (p0, n, r0, st, s0, s1) in runs:
        k.ld(m.dw, dst_ap[r0:r0 + st * (n - 1) + 1:st, c0:c1], src_t, rowfn(p0, p0 + n), partial=True)


class ScanCore:
    def __init__(self, k, m, P, K, V, H, has_u, has_q, has_k2, post_scale, name):
        self.k, self.m = k, m
        self.K, self.V, self.H = K, V, H
        self.has_u, self.has_q, self.has_k2, self.post_scale = has_u, has_q, has_k2, post_scale
        self.Z = k.sb(P, [K, H, V], F32, name + "Z")
        self.Zb = k.sb(P, [K, H, V], BF16, name + "Zb")
        self.o = [k.sb(P, [64, H * V], F32, name + "o") for _ in range(1)]
        self.no = 0
        if has_u:
            self.iv = {nm: k.sb(P, [64, 8, 64], F32, name + nm) for nm in
                       ("N0", "NT0", "E0", "E1", "E2", "Xa", "Xb", "XTa", "XTb", "N2", "NT2", "N4")}
            self.iv["Y"] = self.iv["N2"]
            self.Xf = k.sb(P, [64, H, 64], BF16, name + "Xf")
            self.BhT = k.sb(P, [K, H, 64], BF16, name + "BhT")
            self.U0 = k.sb(P, [64, H * V], F32, name + "U0")
            self.U = k.sb(P, [64, H * V], BF16, name + "U")
        self.ztmp = k.sb(P, [K, H, V], F32, name + "zt")

    def init_state(self, src_t=None, src_ap=None):
        k = self.k
        if src_ap is None:
            k.memset("pool", self.Z, self.Z[:, :, :], 0.0)
        else:
            k.ld(self.Z, self.Z[:, :, :], src_t, src_ap)
        k.cp("act", self.Zb, self.Zb[:, :, :], self.Z, self.Z[:, :, :])

    def store_state(self, dst_t, dst_ap):
        self.k.ld(dst_t, dst_ap, self.Z, self.Z[:, :, :], partial=True)

    def _banks(self, width):
        return (width + 511) // 512

    def step(self, I):
        k, m = self.k, self.m
        K, V, H = self.K, self.V, self.H
        HV = H * V
        nb = self._banks(HV)
        hpb = 512 // V
        pd = m.pd
        evn = [0]

        def evac_eng():
            evn[0] += 1
            return "act" if evn[0] % 2 == 0 else "dve"

        if self.has_u:
            NN, NNT = I["NN"], I["NNT"]
            ident = m.identf64
            iv = self.iv
            for hg in range(0, H, 8):
                nh = min(8, H - hg)
                hs = slice(hg, hg + nh)

                def bcm(mt):
                    return mt[0:64, 0:64].unsqueeze(1).to_broadcast([64, nh, 64])

                def v(t_):
                    return t_[:, 0:nh, :]
                k.tt("dve", iv["N0"], v(iv["N0"]), NN, NN[:, hs, :], m.bmask[0], bcm(m.bmask[0]), ALU.mult)
                k.tt("dve", iv["NT0"], v(iv["NT0"]), NNT, NNT[:, hs, :], m.bmask[0], bcm(m.bmask[0]), ALU.mult)
                for l in range(3):
                    k.tt("dve", iv["E%d" % l], v(iv["E%d" % l]), NN, NN[:, hs, :], m.bmask[l + 1], bcm(m.bmask[l + 1]),
                         ALU.mult)
                X, XT = iv["Xa"], iv["XTa"]
                Xn, XTn = iv["Xb"], iv["XTb"]
                k.tt("dve", X, v(X), iv["N0"], v(iv["N0"]), ident, bcm(ident), ALU.add)
                k.tt("dve", XT, v(XT), iv["NT0"], v(iv["NT0"]), ident, bcm(ident), ALU.add)
                pc = [0]

                def mmh(L_, R_):
                    p_ = m.pd[pc[0] % 2]
                    pc[0] += 1
                    for hh in range(nh):
                        k.mm(p_, p_[0:64, hh * 64:(hh + 1) * 64], L_, L_[:, hh, :], R_, R_[:, hh, :], acc=(hh > 0))
                    return p_, p_[0:64, 0:nh * 64].rearrange("p (h t) -> p h t", t=64)

                def evc(dst, L_, R_):
                    p_, pap = mmh(L_, R_)
                    k.cp("act", dst, v(dst), p_, pap)

                def eva(dst, base, L_, R_):
                    p_, pap = mmh(L_, R_)
                    k.tt("dve", dst, v(dst), p_, pap, base, v(base), ALU.add)
                evc(iv["N2"], iv["NT0"], iv["N0"])
                evc(iv["NT2"], iv["N0"], iv["NT0"])
                eva(Xn, X, XT, iv["N2"])
                eva(XTn, XT, iv["N2"], XT)
                X, XT, Xn, XTn = Xn, XTn, X, XT
                evc(iv["N4"], iv["NT2"], iv["N2"])
                eva(Xn, X, XT, iv["N4"])
                eva(XTn, XT, iv["N4"], XT)
                X, XT, Xn, XTn = Xn, XTn, X, XT
                for l in range(3):
                    evc(iv["Y"], iv["E%d" % l], XT)
                    eva(Xn, X, iv["Y"], X)
                    if l < 2:
                        eva(XTn, XT, X, iv["Y"])
                    X, XT, Xn, XTn = Xn, XTn, X, XT
                k.cp("act", self.Xf, self.Xf[:, hs, :], X, v(X), partial=(hg > 0))
            X = self.Xf
            Bop = I["Bop"]
            kpb = 512 // 64
            for g in range(0, H, kpb):
                p_ = pd[(g // kpb) % 2]
                for hh in range(g, min(H, g + kpb)):
                    k.mm(p_, p_[0:K, (hh - g) * 64:(hh - g + 1) * 64], Bop, Bop[:, hh * K:(hh + 1) * K], X, X[:, hh, :],
                         acc=(hh > g))
                nh = min(H, g + kpb) - g
                k.cp("dve", self.BhT, self.BhT[:, g:g + nh, :], p_,
                     p_[0:K, 0:nh * 64].rearrange("p (h t) -> p h t", t=64), partial=(g > 0))
            U0r = I["U0rhs"]
            for b in range(nb):
                p_ = pd[b % 2]
                for hh in range(b * hpb, min(H, (b + 1) * hpb)):
                    k.mm(p_, p_[0:64, (hh - b * hpb) * V:(hh - b * hpb + 1) * V], X, X[:, hh, :], U0r,
                         U0r[:, hh * V:(hh + 1) * V], acc=(hh > b * hpb))
                w = min(HV, (b + 1) * 512) - b * 512
                k.cp(evac_eng(), self.U0, self.U0[:, b * 512:b * 512 + w], p_, p_[0:64, 0:w], partial=(b > 0))
            for b in range(nb):
                p_ = m.pU[b]
                for hh in range(b * hpb, min(H, (b + 1) * hpb)):
                    k.mm(p_, p_[0:64, (hh - b * hpb) * V:(hh - b * hpb + 1) * V], self.BhT, self.BhT[:, hh, :], self.Zb,
                         self.Zb[:, hh, :], acc=(hh > b * hpb))
                w = min(HV, (b + 1) * 512) - b * 512
                k.tt("dve", self.U, self.U[:, b * 512:b * 512 + w], p_, p_[0:64, 0:w], self.U0,
                     self.U0[:, b * 512:b * 512 + w], ALU.add, partial=(b > 0))
        RopT = I["RopT"]
        o_t = self.o[0]
        self.no += 1
        for b in range(nb):
            p_ = m.pO[b]
            for hh in range(b * hpb, min(H, (b + 1) * hpb)):
                oap = p_[0:64, (hh - b * hpb) * V:(hh - b * hpb + 1) * V]
                terms = [(RopT, RopT[:, hh, :], self.Zb, self.Zb[:, hh, :])]
                if self.has_u:
                    terms.append((I["PmT"], I["PmT"][:, hh, :], self.U, self.U[:, hh * V:(hh + 1) * V]))
                if self.has_q:
                    terms.append((I["QmT"], I["QmT"][:, hh, :], I["Vtok"], I["Vtok"][:, hh * V:(hh + 1) * V]))
                for ti, (lt, lap, rt, rap) in enumerate(terms):
                    k.mm(p_, oap, lt, lap, rt, rap, start=(ti == 0), stop=(ti == len(terms) - 1),
                         acc=not (hh == b * hpb and ti == 0))
            w = min(HV, (b + 1) * 512) - b * 512
            k.cp(evac_eng(), o_t, o_t[:, b * 512:b * 512 + w], p_, p_[0:64, 0:w], partial=(b > 0))
        zs = I["zs"]
        for b in range(nb):
            p_ = m.pZ[b]
            for hh in range(b * hpb, min(H, (b + 1) * hpb)):
                zap = p_[0:K, (hh - b * hpb) * V:(hh - b * hpb + 1) * V]
                terms = []
                if self.has_u:
                    terms.append((I["Aop"], I["Aop"][:, hh * K:(hh + 1) * K], self.U, self.U[:, hh * V:(hh + 1) * V]))
                if self.has_k2:
                    terms.append((I["Kop"], I["Kop"][:, hh * K:(hh + 1) * K], I["Vtok"], I["Vtok"][:, hh * V:(hh + 1) * V]))
                for ti, (lt, lap, rt, rap) in enumerate(terms):
                    k.mm(p_, zap, lt, lap, rt, rap, start=(ti == 0), stop=(ti == len(terms) - 1),
                         acc=not (hh == b * hpb and ti == 0))
            h0 = b * hpb
            nh = min(H, (b + 1) * hpb) - h0
            zsb = zs[0:K, h0:h0 + nh].unsqueeze(2).to_broadcast([K, nh, V])
            pz = p_[0:K, 0:nh * V].rearrange("p (h v) -> p h v", v=V)
            if self.post_scale:
                k.tt("dve", self.ztmp, self.ztmp[:, h0:h0 + nh, :], p_, pz, self.Z, self.Z[:, h0:h0 + nh, :], ALU.add,
                     partial=(b > 0))
                k.tt("dve", self.Z, self.Z[:, h0:h0 + nh, :], self.ztmp, self.ztmp[:, h0:h0 + nh, :], zs, zsb, ALU.mult,
                     partial=(b > 0))
            else:
                k.tt("dve", self.ztmp, self.ztmp[:, h0:h0 + nh, :], self.Z, self.Z[:, h0:h0 + nh, :], zs, zsb, ALU.mult,
                     partial=(b > 0))
                k.tt("dve", self.Z, self.Z[:, h0:h0 + nh, :], p_, pz, self.ztmp, self.ztmp[:, h0:h0 + nh, :], ALU.add,
                     partial=(b > 0))
            k.cp("act", self.Zb, self.Zb[:, h0:h0 + nh, :], self.Z, self.Z[:, h0:h0 + nh, :], partial=(b > 0))
        return o_t


def mixer_phase(k, cfg, layer, R, mode):
    m = MX()
    m.cfg = cfg
    m.layer = layer
    m.R = R
    m.dr = T(None, "dram_read", const=True)
    m.dw = T(None, "dram_write")
    m.zeros = R["zeros"]
    m.ident = R["ident"]
    m.identb = R["identb"]
    m.masks = R["masks"]
    m.bmask = R["masks"]["bmask"]
    m.identf64 = R["masks"]["identf64"]
    seqs = seq_list(cfg, layer)
    with ExitStack() as P:
        m.pd = [k.ps(P, [128, 512], F32, "pd") for _ in range(2)]
        m.pdb = k.ps(P, [128, 16, 64], BF16, "pdb")
        m.pf = k.ps(P, [128, 512], F32, "pf")
        m.pU = [k.ps(P, [128, 512], F32, "pU") for _ in range(2)]
        m.pO = [k.ps(P, [128, 512], F32, "pO") for _ in range(2)]
        m.pZ = m.pU
        if layer % 2 == 0:
            if mode in ("full", "gla"):
                gla_mixer(k, m, seqs)
            if mode in ("full", "rwkv"):
                rwkv_mixer(k, m, seqs)
            if mode in ("gla", "rwkv"):
                zt = k.sb(P, [128, 1024], BF16, "zt")
                k.memset("dve", zt, zt[:, :], 0.0)
                c0 = 1024 if mode == "gla" else 0
                for tt in range(cfg.ntile):
                    k.ld(m.dw, R["mixd"].ap[tt * 128:(tt + 1) * 128, c0:c0 + 1024], zt, zt[:, :], partial=True)
        else:
            gdn_mixer(k, m, seqs)
    return


def gla_mixer(k, m, seqs):
    cfg, layer, R = m.cfg, m.layer, m.R
    j = layer // 2
    proj = R["proj"].ap
    with ExitStack() as P:
        w2 = k.sb(P, [16, 2, 512], F32, "gw2", const=True)
        k.ld(w2, w2[:, :, :], R["gla_w2"], R["gla_w2"].ap[j].rearrange("d l n -> l d n"))
        gb = k.sb(P, [1, 2, 512], F32, "gb", const=True)
        k.ld(gb, gb[:, :, :], R["gla_b"], R["gla_b"].ap[j:j + 1, :, :])
        gg = k.sb(P, [64, 256], F32, "gg", const=True)
        k.ld(gg, gg[:, :], R["gla_g"], R["gla_g"].ap[j:j + 1, :].to_broadcast([64, 256]))
        ones_f = k.sb(P, [64, 64], F32, "ones_f", const=True)
        k.memset("dve", ones_f, ones_f[:, :], 1.0)
        core = ScanCore(k, m, P, 128, 256, 4, False, True, True, True, "gla")
        NB = 2
        gq = [k.sb(P, [64, 512], F32, "gq") for _ in range(NB)]
        gk = [k.sb(P, [64, 512], F32, "gk") for _ in range(NB)]
        gv = [k.sb(P, [64, 1024], F32, "gv") for _ in range(NB)]
        gl = [k.sb(P, [64, 16], F32, "gl") for _ in range(NB)]
        glT = [k.sb(P, [16, 64], F32, "glT") for _ in range(NB)]
        la = [k.sb(P, [64, 512], F32, "la") for _ in range(NB)]
        eW = [k.sb(P, [64, 512], F32, "eW") for _ in range(NB)]
        eWi = [k.sb(P, [64, 512], F32, "eWi") for _ in range(NB)]
        qtb = [k.sb(P, [64, 512], BF16, "qtb") for _ in range(NB)]
        ktb = [k.sb(P, [64, 512], BF16, "ktb") for _ in range(NB)]
        vt = [k.sb(P, [64, 1024], BF16, "vt") for _ in range(NB)]
        qT = [k.sb(P, [128, 4, 64], BF16, "qT") for _ in range(NB)]
        kT = [k.sb(P, [128, 4, 64], BF16, "kT") for _ in range(NB)]
        QmT = [k.sb(P, [64, 4, 64], BF16, "QmT") for _ in range(NB)]
        zs = [k.sb(P, [128, 4], F32, "zs") for _ in range(NB)]
        ofw = [k.sb(P, [64, 1024], F32, "ofw") for _ in range(NB)]
        gz = [k.sb(P, [64, 1024], F32, "gz") for _ in range(NB)]
        sq = k.sb(P, [64, 1024], F32, "sq")
        ssq = [k.sb(P, [64, 4], F32, "ssq") for _ in range(NB)]
        mo = [k.sb(P, [64, 1024], BF16, "mo") for _ in range(NB)]
        n = 0
        for d in range(2):
            LE = m.masks["LE%d" % d]
            LEf = m.masks["LEf%d" % d]
            for sq_ in seqs:
                if sq_["kind"] == "sample":
                    core.init_state(R["state_gla"], R["state_gla"].ap[j, d].rearrange("h k v -> k h v"))
                else:
                    core.init_state()
                chs = sq_["chunks"] if d == 0 else sq_["chunks"][::-1]
                for runs in chs:
                    b = n % NB
                    n += 1
                    load_rows(k, m, gq[b], lambda a, c, t=gq[b]: t[a:c, :], proj, 0, 512, runs)
                    load_rows(k, m, gk[b], lambda a, c, t=gk[b]: t[a:c, :], proj, 512, 1024, runs)
                    load_rows(k, m, gv[b], lambda a, c, t=gv[b]: t[a:c, :], proj, 1024, 2048, runs)
                    load_rows(k, m, gl[b], lambda a, c, t=gl[b]: t[a:c, :], proj, 3072 + d * 16, 3088 + d * 16, runs)
                    p0 = m.pd[0]
                    k.tr(p0, p0[0:16, 0:64], gl[b], gl[b][:, :], m.ident, m.ident[0:64, 0:64], partial=False)
                    k.cp("act", glT[b], glT[b][:, :], p0, p0[0:16, 0:64])
                    p1 = m.pd[1]
                    k.mm(p1, p1[0:64, :], glT[b], glT[b][:, :], w2, w2[:, d, :], start=True, stop=False)
                    k.mm(p1, p1[0:64, :], ones_f, ones_f[0:1, 0:64], gb, gb[0:1, d, :], start=False, stop=True, acc=True)
                    ckpt()
                    k.act(la[b], la[b][:, :], p1, p1[0:64, :], AF.Exp, scale=-1.0)
                    k.act(la[b], la[b][:, :], la[b], la[b][:, :], AF.Ln, bias=1.0)
                    k.tsc("dve", la[b], la[b][:, :], la[b], la[b][:, :], -1.0 / 16.0, None, ALU.mult)
                    ckpt()
                    k.mm(p0, p0[0:64, :], LEf, LEf[:, :], la[b], la[b][:, :])
                    k.act(eW[b], eW[b][:, :], p0, p0[0:64, :], AF.Exp)
                    k.act(eWi[b], eWi[b][:, :], p0, p0[0:64, :], AF.Exp, scale=-1.0)
                    for hh in range(4):
                        k.mm(p1, p1[:, 2 * hh:2 * hh + 2], la[b], la[b][:, hh * 128:(hh + 1) * 128], ones_f, ones_f[:, 0:2],
                             acc=(hh > 0))
                    k.act(zs[b], zs[b][:, :], p1, p1[:, 0:8].rearrange("p (h t) -> p h t", t=2)[:, :, 0], AF.Exp)
                    ckpt()
                    k.stt(qtb[b], qtb[b][0:64, :], gq[b], gq[b][:, :], 128.0 ** -0.5, eW[b], eW[b][:, :], ALU.mult, ALU.mult)
                    k.tt("dve" if DBG & 64 else "pool", ktb[b], ktb[b][0:64, :], gk[b], gk[b][:, :], eWi[b], eWi[b][:, :], ALU.mult)
                    k.cp("act", vt[b], vt[b][:, :], gv[b], gv[b][:, :])
                    ckpt()
                    pb = m.pO[0] if DBG & 128 else m.pd[1]
                    for hh in range(4):
                        k.mm(pb, pb[:, hh * 64:(hh + 1) * 64], qtb[b], qtb[b][:, hh * 128:(hh + 1) * 128], m.identb,
                             m.identb[0:64, 0:64], acc=(hh > 0))
                    for hh in range(4 if not DBG & 32 else 0):
                        k.mm(pb, pb[:, (4 + hh) * 64:(5 + hh) * 64], ktb[b], ktb[b][:, hh * 128:(hh + 1) * 128], m.identb,
                             m.identb[0:64, 0:64], acc=True)
                    if DBG & 256:
                        ckpt()
                    k.cp("dve", qT[b], qT[b][:, :, :], pb, pb[:, 0:256].rearrange("p (h t) -> p h t", t=64))
                    if DBG & 512:
                        ckpt()
                    k.cp("dve", kT[b], kT[b][:, :, :], pb, pb[:, 256:512].rearrange("p (h t) -> p h t", t=64))
                    ckpt()
                    for hh in range(4):
                        k.mm(p0, p0[0:64, hh * 64:(hh + 1) * 64], kT[b], kT[b][:, hh, :], qT[b], qT[b][:, hh, :], acc=(hh > 0))
                    k.tt("dve", QmT[b], QmT[b][:, :, :], p0, p0[0:64, 0:256].rearrange("p (h t) -> p h t", t=64), LE,
                         LE[0:64, 0:64].unsqueeze(1).to_broadcast([64, 4, 64]), ALU.mult)
                    ckpt()
                    o_t = core.step(dict(RopT=qT[b], QmT=QmT[b], Vtok=vt[b], Kop=ktb[b], zs=zs[b]))
                    ckpt()
                    if d == 0:
                        store_rows(k, m, R["ogla"].ap, 0, 1024, runs, o_t, lambda a, c, t=o_t: t[a:c, :])
                    else:
                        load_rows(k, m, ofw[b], lambda a, c, t=ofw[b]: t[a:c, :], R["ogla"].ap, 0, 1024, runs)
                        load_rows(k, m, gz[b], lambda a, c, t=gz[b]: t[a:c, :], proj, 2048, 3072, runs)
                        k.tt("dve", ofw[b], ofw[b][:, :], ofw[b], ofw[b][:, :], o_t, o_t[:, :], ALU.add)
                        k.tt("dve", sq, sq[:, :], ofw[b], ofw[b][:, :], ofw[b], ofw[b][:, :], ALU.mult)

                        def red(e, o=ssq[b][:, :], i=sq[:, :].rearrange("p (h v) -> p h v", v=256)):
                            return e.tensor_reduce(o, i, AX.X, ALU.add)
                        k.s.op("dve", red, reads=(sq,), writes=(ssq[b],))
                        k.tsc("dve", ssq[b], ssq[b][:, :], ssq[b], ssq[b][:, :], 1.0 / 256.0, EPS, ALU.mult, ALU.add)
                        k.act(ssq[b], ssq[b][:, :], ssq[b], ssq[b][:, :], AF.Sqrt)

                        def rec(e, o=ssq[b][:, :]):
                            return e.reciprocal(o, o)
                        k.s.op("dve", rec, reads=(ssq[b],), writes=(ssq[b],))
                        o3 = ofw[b][:, :].rearrange("p (h v) -> p h v", v=256)
                        k.tt("dve", ofw[b], o3, ofw[b], o3, ssq[b], ssq[b][:, :].unsqueeze(2).to_broadcast([64, 4, 256]), ALU.mult)
                        k.tt("dve", ofw[b], o3, ofw[b], o3, gg, gg[:, :].unsqueeze(1).to_broadcast([64, 4, 256]), ALU.mult)
                        k.act(sq, sq[:, :], gz[b], gz[b][:, :], AF.Sigmoid)
                        k.tt("dve", sq, sq[:, :], sq, sq[:, :], gz[b], gz[b][:, :], ALU.mult)
                        k.tt("dve", mo[b], mo[b][:, :], ofw[b], ofw[b][:, :], sq, sq[:, :], ALU.mult)
                        store_rows(k, m, R["mixd"].ap, 0, 1024, runs, mo[b], lambda a, c, t=mo[b]: t[a:c, :])
                if sq_["kind"] == "prompt":
                    core.store_state(R["ns_gla"], R["ns_gla"].ap[sq_["idx"], j, d].rearrange("h k v -> k h v"))
            k.s.barrier()


def core_inputs(inp, core, cfg, consts):
    f = lambda a: np.ascontiguousarray(np.asarray(a, dtype=np.float32))
    ne, no_ = cfg.n_even, max(cfg.n_odd, 1)
    m = {}
    xs = f(inp["x_sample"])[core]
    xp = f(inp["x_prompt"])[core * cfg.np:(core + 1) * cfg.np].reshape(-1, D)
    m["xin"] = np.ascontiguousarray(np.concatenate([xs, xp], axis=0))
    m["cvec"] = np.ascontiguousarray(np.stack([f(inp["c"])[core], f(inp["c_ctx"])]))
    m["state_gla"] = f(inp["state_gla"])[core]
    m["state_rwkv"] = f(inp["state_rwkv"])[core]
    sg = f(inp["state_gdn"])[core]
    m["state_gdn"] = sg if cfg.n_odd > 0 else np.zeros((1, 2, 16, 128, 128), np.float32)
    for nm in ("norm_g", "w_ada", "b_ada", "w_in_even", "w_out", "gla_w2", "gla_b", "gla_g", "rwkv_mu", "rwkv_w0",
               "rwkv_w2", "rwkv_a0", "rwkv_a2", "rwkv_k_k", "rwkv_k_a", "rwkv_gn_g", "rwkv_gn_b"):
        m[nm] = f(inp[nm])
    m["rwkv_r_k"] = f(inp["rwkv_r_k"]).reshape(ne, 1024)
    m["final_g"] = f(inp["final_g"]).reshape(1, D)
    if cfg.n_odd > 0:
        for nm in ("w_in_odd", "gdn_conv_w", "gdn_A_log", "gdn_dt_bias", "gdn_g"):
            m[nm] = f(inp[nm])
    else:
        m["w_in_odd"] = np.zeros((1, D, ODD_IN), np.float32)
        m["gdn_conv_w"] = np.zeros((1, 3, 6144), np.float32)
        m["gdn_A_log"] = np.zeros((1, 2, 16), np.float32)
        m["gdn_dt_bias"] = np.zeros((1, 2, 16), np.float32)
        m["gdn_g"] = np.zeros((1, 128), np.float32)
    m.update(consts)
    return m


C0 = 0.6065306597126334


def hilo_aug(k, P, name, w_t, w_ap64, row_t, row_ap):
    aug = k.sb(P, [128, 1024], BF16, name, const=True)
    k.memset("pool", aug, aug[:, :], 0.0)
    k.ld(aug, aug[64:128, :], w_t, w_ap64, eng="pool", partial=True)
    with ExitStack() as tmp:
        st = k.sb(tmp, [33, 1024], F32, name + "st")
        k.ld(st, st[0:1, :], row_t, row_ap)
        k.ld(st, st[32:33, :], row_t, row_ap, partial=True)
        k.cp("dve", aug, aug[0:1, :], st, st[0:1, :], partial=True)
        k.cp("dve", aug, aug[32:33, :], st, st[32:33, :], partial=True)
        k.tt("dve", aug, aug[32:33, :], st, st[32:33, :], aug, aug[32:33, :], ALU.subtract, partial=True)
    k.s.barrier()
    return aug


def rwkv_mixer(k, m, seqs):
    cfg, layer, R = m.cfg, m.layer, m.R
    j = layer // 2
    proj = R["proj"].ap
    H = 16
    with ExitStack() as P:
        pdb = m.pdb

        def bc(name, t, ap_row, n=1024, stk=None):
            x = k.sb(stk if stk is not None else P, [64, n], F32, name, const=True)
            k.ld(x, x[:, :], t, ap_row.to_broadcast([64, n]))
            return x
        mu = [[bc("mu%d%d" % (i, s_), R["rwkv_mu"], R["rwkv_mu"].ap[j, i, s_:s_ + 1, :]) for s_ in range(2)]
              for i in range(3)]
        kk_bc = bc("kkbc", R["rwkv_k_k"], R["rwkv_k_k"].ap[j:j + 1, :])
        ka_bc = bc("kabc", R["rwkv_k_a"], R["rwkv_k_a"].ap[j:j + 1, :])
        rk_bc = bc("rkbc", R["rwkv_r_k"], R["rwkv_r_k"].ap[j:j + 1, :])
        ones_f = k.sb(P, [64, 2], F32, "ones_f2", const=True)
        k.memset("dve", ones_f, ones_f[:, :], 1.0)
        core = ScanCore(k, m, P, 64, 64, H, True, True, True, True, "rw")
        xr = k.sb(P, [64, 1024], F32, "xr")
        xk = k.sb(P, [64, 1024], F32, "xk")
        xv = k.sb(P, [64, 1024], F32, "xv")
        pv_ = k.sb(P, [64, 1024], F32, "pv")
        nx_ = k.sb(P, [64, 1024], F32, "nx")
        kkt = k.sb(P, [64, 1024], F32, "kkt")
        sig = k.sb(P, [64, 1024], F32, "sig")
        A_ = k.sb(P, [64, 1024], F32, "A_")
        kd = k.sb(P, [64, 1024], F32, "kd")
        S1 = k.sb(P, [64, 1024], F32, "S1")
        S2 = k.sb(P, [64, 1024], F32, "S2")
        S3 = k.sb(P, [64, 1024], F32, "S3")
        ssq = k.sb(P, [64, 16], F32, "rssq")
        bs = k.sb(P, [64, 16], F32, "bs")
        bs0 = k.sb(P, [64, 16], F32, "bs0")
        lw = k.sb(P, [64, 64], F32, "lw")
        la_ = k.sb(P, [64, 64], F32, "la")
        lwp = k.sb(P, [64, 128], BF16, "lwp")
        lap = k.sb(P, [64, 128], BF16, "lap")
        for t_ in (lwp, lap):
            k.memset("pool", t_, t_[:, :], 0.0)
            k.memset("pool", t_, t_[:, 0:1], 1.0)
            k.memset("pool", t_, t_[:, 32:33], 1.0)
        lwT = k.sb(P, [128, 64], BF16, "lwT")
        laT = k.sb(P, [128, 64], BF16, "laT")
        zs = k.sb(P, [64, 16], F32, "rzs")
        tl = {nm: k.sb(P, [64, 1024], BF16, "tl" + nm) for nm in ("al", "be", "kt", "rt", "v", "cv")}
        tT = {nm: k.sb(P, [64, H, 64], BF16, "tT" + nm) for nm in ("al", "be", "kt", "rt")}
        st_ = {nm: k.sb(P, [64, H, 64], F32 if nm in ("NN", "NNT") else BF16, "st" + nm)
               for nm in ("NN", "NNT", "CT", "PmT", "QmT")}
        Sst = TV(S1, S1.ap.rearrange("p (h v) -> p h v", v=64))
        for d in range(2):
            with ExitStack() as PD:
                LE, LT, LTo = m.masks["LE%d" % d], m.masks["LT%d" % d], m.masks["LT%d" % (1 - d)]
                LEf = m.masks["LEf%d" % d]
                w2aug = hilo_aug(k, PD, "w2aug", R["rwkv_w2"], R["rwkv_w2"].ap[j, d], R["rwkv_w0"],
                                 R["rwkv_w0"].ap[j, d:d + 1, :])
                a2aug = hilo_aug(k, PD, "a2aug", R["rwkv_a2"], R["rwkv_a2"].ap[j, d], R["rwkv_a0"],
                                 R["rwkv_a0"].ap[j, d:d + 1, :])
                if d == 1:
                    gng = bc("gng", R["rwkv_gn_g"], R["rwkv_gn_g"].ap[j:j + 1, :], stk=PD)
                    gnb = bc("gnb", R["rwkv_gn_b"], R["rwkv_gn_b"].ap[j:j + 1, :], stk=PD)
                    ofw = k.sb(PD, [64, 1024], F32, "rofw")
                    rz = k.sb(PD, [64, 1024], F32, "rz")
                    mo = k.sb(PD, [64, 1024], BF16, "rmo")
                    mean = k.sb(PD, [64, 16], F32, "mean")
                for sq_ in seqs:
                    if sq_["kind"] == "sample":
                        k.ld(Sst, Sst[:, :, :], R["state_rwkv"], R["state_rwkv"].ap[j, d].rearrange("h v k -> v h k"))
                        for g in range(2):
                            p_ = m.pd[g]
                            for hh in range(8):
                                k.tr(p_, p_[0:64, hh * 64:(hh + 1) * 64], Sst, Sst[:, g * 8 + hh, :], m.ident,
                                     m.ident[0:64, 0:64], partial=(hh > 0))
                            k.cp("dve", core.Z, core.Z[:, g * 8:(g + 1) * 8, :], p_,
                                 p_[0:64, :].rearrange("p (h v) -> p h v", v=64), partial=(g > 0))
                        k.cp("act", core.Zb, core.Zb[:, :, :], core.Z, core.Z[:, :, :])
                    else:
                        core.init_state()
                    chs = sq_["chunks"] if d == 0 else sq_["chunks"][::-1]
                    for runs in chs:
                        for qi, (cur, c0) in enumerate(((xr, 3104), (xk, 4128), (xv, 5152))):
                            load_rows(k, m, cur, lambda a, c, t=cur: t[a:c, :], proj, c0, c0 + 1024, runs, 0)
                            load_rows(k, m, pv_, lambda a, c, t=pv_: t[a:c, :], proj, c0, c0 + 1024, runs, -1)
                            load_rows(k, m, nx_, lambda a, c, t=nx_: t[a:c, :], proj, c0, c0 + 1024, runs, +1)
                            k.tt("dve", pv_, pv_[:, :], pv_, pv_[:, :], cur, cur[:, :], ALU.subtract)
                            k.tt("dve", pv_, pv_[:, :], pv_, pv_[:, :], mu[qi][0], mu[qi][0][:, :], ALU.mult)
                            k.tt("dve", nx_, nx_[:, :], nx_, nx_[:, :], cur, cur[:, :], ALU.subtract)
                            k.tt("dve", nx_, nx_[:, :], nx_, nx_[:, :], mu[qi][1], mu[qi][1][:, :], ALU.mult)
                            k.tt("dve", cur, cur[:, :], cur, cur[:, :], pv_, pv_[:, :], ALU.add)
                            k.tt("dve", cur, cur[:, :], cur, cur[:, :], nx_, nx_[:, :], ALU.add)
                        load_rows(k, m, lw, lambda a, c, t=lw: t[a:c, :], proj, 7200 + d * 64, 7264 + d * 64, runs, 0)
                        load_rows(k, m, la_, lambda a, c, t=la_: t[a:c, :], proj, 7328 + d * 64, 7392 + d * 64, runs, 0)
                        k.tt("dve", kkt, kkt[:, :], xk, xk[:, :], kk_bc, kk_bc[:, :], ALU.mult)
                        k.tt("dve", S1, S1[:, :], kkt, kkt[:, :], kkt, kkt[:, :], ALU.mult)

                        def red(e, o=ssq[:, :], i=S1[:, :].rearrange("p (h v) -> p h v", v=64)):
                            return e.tensor_reduce(o, i, AX.X, ALU.add)
                        k.s.op("dve", red, reads=(S1,), writes=(ssq,))
                        k.tsc("dve", ssq, ssq[:, :], ssq, ssq[:, :], EPS, None, ALU.add)
                        k.act(ssq, ssq[:, :], ssq, ssq[:, :], AF.Sqrt)

                        def rec(e, o=ssq[:, :]):
                            return e.reciprocal(o, o)
                        k.s.op("dve", rec, reads=(ssq,), writes=(ssq,))
                        kk3 = kkt[:, :].rearrange("p (h v) -> p h v", v=64)
                        k.tt("dve", kkt, kk3, kkt, kk3, ssq, ssq[:, :].unsqueeze(2).to_broadcast([64, 16, 64]), ALU.mult)
                        k.act(lwp, lwp[:, 64:128], lw, lw[:, :], AF.Tanh, partial=True)
                        k.cp("act", lap, lap[:, 64:128], la_, la_[:, :], partial=True)
                        p0, p1 = m.pd[0], m.pd[1]
                        k.mm(p0, p0[:, 0:64], lwp, lwp[:, :], m.identb, m.identb[0:64, 0:64])
                        k.mm(p0, p0[:, 64:128], lap, lap[:, :], m.identb, m.identb[0:64, 0:64], acc=True)
                        k.cp("dve", lwT, lwT[:, :], p0, p0[:, 0:64])
                        k.cp("dve", laT, laT[:, :], p0, p0[:, 64:128])
                        for hb in range(2):
                            p_ = m.pd[hb]
                            k.mm(p_, p_[0:64, :], lwT, lwT[:, :], w2aug, w2aug[:, hb * 512:(hb + 1) * 512])
                            k.act(sig, sig[:, hb * 512:(hb + 1) * 512], p_, p_[0:64, :], AF.Sigmoid, partial=(hb > 0))
                        for hb in range(2):
                            p_ = m.pd[hb]
                            k.mm(p_, p_[0:64, :], laT, laT[:, :], a2aug, a2aug[:, hb * 512:(hb + 1) * 512])
                            k.act(A_, A_[:, hb * 512:(hb + 1) * 512], p_, p_[0:64, :], AF.Sigmoid, partial=(hb > 0))
                        k.stt(S1, S1[:, :], A_, A_[:, :], -1.0, ka_bc, ka_bc[:, :], ALU.add, ALU.mult)
                        k.stt(kd, kd[:, :], S1, S1[:, :], 1.0, xk, xk[:, :], ALU.add, ALU.mult)
                        k.tt("dve", S1, S1[:, :], xr, xr[:, :], kd, kd[:, :], ALU.mult)
                        k.tt("dve", S1, S1[:, :], S1, S1[:, :], rk_bc, rk_bc[:, :], ALU.mult)

                        def red2(e, o=bs[:, :], i=S1[:, :].rearrange("p (h v) -> p h v", v=64)):
                            return e.tensor_reduce(o, i, AX.X, ALU.add)
                        k.s.op("dve", red2, reads=(S1,), writes=(bs,))
                        for hb in range(2):
                            p_ = m.pd[hb]
                            k.mm(p_, p_[0:64, :], LEf, LEf[:, :], sig, sig[:, hb * 512:(hb + 1) * 512])
                            sl = slice(hb * 512, (hb + 1) * 512)
                            k.act(S2, S2[:, sl], p_, p_[0:64, :], AF.Exp, scale=-C0, partial=(hb > 0))
                            k.act(S3, S3[:, sl], p_, p_[0:64, :], AF.Exp, scale=C0, partial=(hb > 0))
                            k.tt("dve", S1, S1[:, sl], p_, p_[0:64, :], sig, sig[:, sl], ALU.subtract, partial=(hb > 0))
                        k.tt("dve", tl["rt"], tl["rt"][:, :], xr, xr[:, :], S2, S2[:, :], ALU.mult)
                        k.tt("dve", tl["kt"], tl["kt"][:, :], kd, kd[:, :], S3, S3[:, :], ALU.mult)
                        k.tt("dve", S2, S2[:, :], kkt, kkt[:, :], A_, A_[:, :], ALU.mult)
                        k.stt(tl["al"], tl["al"][:, :], S2, S2[:, :], -1.0, S3, S3[:, :], ALU.mult, ALU.mult)
                        k.act(S1, S1[:, :], S1, S1[:, :], AF.Exp, scale=-C0)
                        k.tt("dve", tl["be"], tl["be"][:, :], kkt, kkt[:, :], S1, S1[:, :], ALU.mult)
                        k.cp("act", tl["v"], tl["v"][:, :], xv, xv[:, :])
                        p_ = m.pd[0]
                        for hh in range(H):
                            k.mm(p_, p_[0:64, 2 * hh:2 * hh + 2], sig, sig[:, hh * 64:(hh + 1) * 64], ones_f, ones_f[:, :],
                                 acc=(hh > 0))
                        k.act(zs, zs[:, :], p_, p_[0:64, 0:32].rearrange("p (h t) -> p h t", t=2)[:, :, 0], AF.Exp,
                              scale=-C0)
                        for nm in ("al", "be", "kt", "rt"):
                            for hh in range(H):
                                k.tr(pdb, pdb[0:64, hh, :], tl[nm], tl[nm][:, hh * 64:(hh + 1) * 64], m.identb,
                                     m.identb[0:64, 0:64], partial=(hh > 0))
                            k.cp("act", tT[nm], tT[nm][:, :, :], pdb, pdb[0:64, :, :])
                        prods = (("NN", "al", "be", LT), ("NNT", "be", "al", LTo), ("CT", "kt", "be", LT),
                                 ("PmT", "al", "rt", LE), ("QmT", "kt", "rt", LE))
                        for (dn, ln, rn, mk) in prods:
                            for g in range(2):
                                p_ = m.pd[g]
                                for hh in range(8):
                                    h_ = g * 8 + hh
                                    k.mm(p_, p_[0:64, hh * 64:(hh + 1) * 64], tT[ln], tT[ln][:, h_, :], tT[rn],
                                         tT[rn][:, h_, :], acc=(hh > 0))
                                k.tt("dve", st_[dn], st_[dn][:, g * 8:(g + 1) * 8, :], p_,
                                     p_[0:64, :].rearrange("p (h t) -> p h t", t=64), mk,
                                     mk[0:64, 0:64].unsqueeze(1).to_broadcast([64, 8, 64]), ALU.mult, partial=(g > 0))
                        for g in range(2):
                            p_ = m.pd[g]
                            for hh in range(8):
                                h_ = g * 8 + hh
                                k.mm(p_, p_[0:64, hh * 64:(hh + 1) * 64], st_["CT"], st_["CT"][:, h_, :], tl["v"],
                                     tl["v"][:, h_ * 64:(h_ + 1) * 64], acc=(hh > 0))
                            k.cp("dve", tl["cv"], tl["cv"][:, g * 512:(g + 1) * 512], p_, p_[0:64, :], partial=(g > 0))
                        o_t = core.step(dict(NN=st_["NN"], NNT=st_["NNT"], Bop=tl["be"], U0rhs=tl["cv"], RopT=tT["rt"],
                                             PmT=st_["PmT"], QmT=st_["QmT"], Vtok=tl["v"], Aop=tl["al"], Kop=tl["kt"],
                                             zs=zs))
                        if d == 0:
                            store_rows(k, m, R["orw"].ap, 0, 1024, runs, o_t, lambda a, c, t=o_t: t[a:c, :])
                            store_rows(k, m, R["bsum"].ap, 0, 16, runs, bs, lambda a, c, t=bs: t[a:c, :])
                        else:
                            load_rows(k, m, ofw, lambda a, c, t=ofw: t[a:c, :], R["orw"].ap, 0, 1024, runs)
                            load_rows(k, m, bs0, lambda a, c, t=bs0: t[a:c, :], R["bsum"].ap, 0, 16, runs)
                            load_rows(k, m, rz, lambda a, c, t=rz: t[a:c, :], proj, 6176, 7200, runs)
                            k.tt("dve", ofw, ofw[:, :], ofw, ofw[:, :], o_t, o_t[:, :], ALU.add)
                            k.tt("dve", bs0, bs0[:, :], bs0, bs0[:, :], bs, bs[:, :], ALU.add)
                            o3 = ofw[:, :].rearrange("p (h v) -> p h v", v=64)

                            def red3(e, o=mean[:, :], i=o3):
                                return e.tensor_reduce(o, i, AX.X, ALU.add)
                            k.s.op("dve", red3, reads=(ofw,), writes=(mean,))
                            k.tsc("dve", mean, mean[:, :], mean, mean[:, :], 1.0 / 64.0, None, ALU.mult)
                            k.tt("dve", ofw, o3, ofw, o3, mean, mean[:, :].unsqueeze(2).to_broadcast([64, 16, 64]),
                                 ALU.subtract)
                            k.tt("dve", S1, S1[:, :], ofw, ofw[:, :], ofw, ofw[:, :], ALU.mult)

                            def red4(e, o=ssq[:, :], i=S1[:, :].rearrange("p (h v) -> p h v", v=64)):
                                return e.tensor_reduce(o, i, AX.X, ALU.add)
                            k.s.op("dve", red4, reads=(S1,), writes=(ssq,))
                            k.tsc("dve", ssq, ssq[:, :], ssq, ssq[:, :], 1.0 / 64.0, GN_EPS, ALU.mult, ALU.add)
                            k.act(ssq, ssq[:, :], ssq, ssq[:, :], AF.Sqrt)
                            k.s.op("dve", rec, reads=(ssq,), writes=(ssq,))
                            k.tt("dve", ofw, o3, ofw, o3, ssq, ssq[:, :].unsqueeze(2).to_broadcast([64, 16, 64]), ALU.mult)
                            k.tt("dve", ofw, ofw[:, :], ofw, ofw[:, :], gng, gng[:, :], ALU.mult)
                            k.tt("dve", ofw, ofw[:, :], ofw, ofw[:, :], gnb, gnb[:, :], ALU.add)
                            v3 = xv[:, :].rearrange("p (h v) -> p h v", v=64)
                            k.tt("dve", S1, S1[:, :].rearrange("p (h v) -> p h v", v=64), xv, v3, bs0,
                                 bs0[:, :].unsqueeze(2).to_broadcast([64, 16, 64]), ALU.mult)
                            k.tt("dve", ofw, ofw[:, :], ofw, ofw[:, :], S1, S1[:, :], ALU.add)
                            k.act(S1, S1[:, :], rz, rz[:, :], AF.Sigmoid)
                            k.tt("dve", S1, S1[:, :], S1, S1[:, :], rz, rz[:, :], ALU.mult)
                            k.tt("dve", mo, mo[:, :], ofw, ofw[:, :], S1, S1[:, :], ALU.mult)
                            store_rows(k, m, R["mixd"].ap, 1024, 2048, runs, mo, lambda a, c, t=mo: t[a:c, :])
                    if sq_["kind"] == "prompt":
                        for g in range(2):
                            p_ = m.pd[g]
                            for hh in range(8):
                                k.tr(p_, p_[0:64, hh * 64:(hh + 1) * 64], core.Z, core.Z[:, g * 8 + hh, :], m.ident,
                                     m.ident[0:64, 0:64], partial=(hh > 0))
                            k.cp("dve", Sst, Sst[:, g * 8:(g + 1) * 8, :], p_,
                                 p_[0:64, :].rearrange("p (h v) -> p h v", v=64), partial=(g > 0))
                        k.ld(R["ns_rwkv"], R["ns_rwkv"].ap[sq_["idx"], j, d].rearrange("h v k -> v h k"), Sst,
                             Sst[:, :, :], partial=True)
                k.s.barrier()


def gdn_mixer(k, m, seqs):
    cfg, layer, R = m.cfg, m.layer, m.R
    j = layer // 2
    proj = R["proj"].ap
    HG = 8
    with ExitStack() as P:
        pdb = m.pdb
        ones_f = k.sb(P, [64, 128], F32, "g_ones", const=True)
        k.memset("dve", ones_f, ones_f[:, :], 1.0)
        gg = k.sb(P, [64, 128], F32, "gdng", const=True)
        k.ld(gg, gg[:, :], R["gdn_g"], R["gdn_g"].ap[j:j + 1, :].to_broadcast([64, 128]))
        core = ScanCore(k, m, P, 128, 128, HG, True, False, False, False, "gd")
        cur = [k.sb(P, [64, 1024], F32, "gcur%d" % i) for i in range(3)]
        pv_ = k.sb(P, [64, 1024], F32, "gpv")
        nx_ = k.sb(P, [64, 1024], F32, "gnx")
        S1 = k.sb(P, [64, 1024], F32, "gS1")
        ssq = k.sb(P, [64, 8], F32, "gssq")
        bl = k.sb(P, [64, 8], F32, "gbl")
        al = k.sb(P, [64, 8], F32, "gal")
        beta = k.sb(P, [64, 8], F32, "gbeta")
        gt = k.sb(P, [64, 8], F32, "ggt")
        gc = k.sb(P, [64, 8], F32, "ggc")
        egc = k.sb(P, [64, 8], F32, "gegc")
        coef = k.sb(P, [64, 8], F32, "gcoef")
        edec = k.sb(P, [64, 8], F32, "gedec")
        zs2 = [k.sb(P, [128, 8], F32, "gzs") for _ in range(2)]
        gLE = k.sb(P, [64, 8, 64], F32, "gLE")
        gLT = k.sb(P, [64, 8, 64], F32, "gLT")
        EMs = k.sb(P, [64, 8, 64], F32, "EMs")
        EMe = k.sb(P, [64, 8, 64], F32, "EMe")
        EMt = k.sb(P, [64, 8, 64], F32, "EMt")
        tl2 = [{nm: k.sb(P, [64, 1024], BF16, "gtl" + nm) for nm in ("k", "kb", "q", "qe", "bop", "bv", "aop")}
               for _ in range(2)]
        tT2 = [{nm: k.sb(P, [128, HG, 64], BF16, "gtT" + nm) for nm in ("k", "kb", "q", "qe")} for _ in range(2)]
        st2 = [{nm: k.sb(P, [64, HG, 64], F32 if nm in ("NN", "NNT") else BF16, "gst" + nm)
                for nm in ("NN", "NNT", "PmT")} for _ in range(2)]
        S1e = k.sb(P, [64, 1024], F32, "gS1e")
        ssqe = k.sb(P, [64, 8], F32, "gssqe")
        for d in range(2):
            LE, LT, LTo = m.masks["LE%d" % d], m.masks["LT%d" % d], m.masks["LT%d" % (1 - d)]
            LEf, LTof = m.masks["LEf%d" % d], m.masks["LTf%d" % (1 - d)]
            for g in range(2):
                with ExitStack() as PD:
                    cw = [[k.sb(PD, [64, 1024], F32, "cw%d%d" % (qi, tp), const=True) for tp in range(3)] for qi in range(3)]
                    for qi in range(3):
                        for tp in range(3):
                            c0 = qi * 2048 + g * 1024
                            k.ld(cw[qi][tp], cw[qi][tp][:, :], R["gdn_conv_w"],
                                 R["gdn_conv_w"].ap[j, tp:tp + 1, c0:c0 + 1024].to_broadcast([64, 1024]))
                    negA = k.sb(PD, [64, 8], F32, "negA", const=True)
                    dtb = k.sb(PD, [64, 8], F32, "dtb", const=True)
                    k.ld(negA, negA[:, :], R["gdn_A_log"], R["gdn_A_log"].ap[j, d:d + 1, g * 8:(g + 1) * 8].to_broadcast([64, 8]))
                    k.ld(dtb, dtb[:, :], R["gdn_dt_bias"],
                         R["gdn_dt_bias"].ap[j, d:d + 1, g * 8:(g + 1) * 8].to_broadcast([64, 8]))
                    k.act(negA, negA[:, :], negA, negA[:, :], AF.Exp)
                    k.tsc("dve", negA, negA[:, :], negA, negA[:, :], -1.0, None, ALU.mult)
                    if d == 1:
                        ofw = k.sb(PD, [64, 1024], F32, "gofw")
                        zt = k.sb(PD, [64, 1024], F32, "gz")
                        mo = k.sb(PD, [64, 1024], BF16, "gmo")
                    work = []
                    for sq_ in seqs:
                        chs = sq_["chunks"] if d == 0 else sq_["chunks"][::-1]
                        for ci, runs in enumerate(chs):
                            work.append((sq_, ci, len(chs), runs))
                    pend = [None]
                    nb_ = [0]

                    def front(runs, bs):
                        tl, tT, st_, zs = tl2[bs], tT2[bs], st2[bs], zs2[bs]
                        if True:
                            for qi in range(3):
                                c0 = qi * 2048 + g * 1024
                                X = cur[qi]
                                load_rows(k, m, X, lambda a, c, t=X: t[a:c, :], proj, c0, c0 + 1024, runs, 0)
                                load_rows(k, m, pv_, lambda a, c, t=pv_: t[a:c, :], proj, c0, c0 + 1024, runs, -1)
                                load_rows(k, m, nx_, lambda a, c, t=nx_: t[a:c, :], proj, c0, c0 + 1024, runs, +1)
                                k.tt("dve", X, X[:, :], X, X[:, :], cw[qi][1], cw[qi][1][:, :], ALU.mult)
                                k.tt("dve", pv_, pv_[:, :], pv_, pv_[:, :], cw[qi][0], cw[qi][0][:, :], ALU.mult)
                                k.tt("dve", nx_, nx_[:, :], nx_, nx_[:, :], cw[qi][2], cw[qi][2][:, :], ALU.mult)
                                k.tt("dve", X, X[:, :], X, X[:, :], pv_, pv_[:, :], ALU.add)
                                k.tt("dve", X, X[:, :], X, X[:, :], nx_, nx_[:, :], ALU.add)
                                k.act(S1, S1[:, :], X, X[:, :], AF.Sigmoid)
                                k.tt("dve", X, X[:, :], X, X[:, :], S1, S1[:, :], ALU.mult)
                            load_rows(k, m, bl, lambda a, c, t=bl: t[a:c, :], proj, 8192 + d * 16 + g * 8,
                                      8192 + d * 16 + g * 8 + 8, runs, 0)
                            load_rows(k, m, al, lambda a, c, t=al: t[a:c, :], proj, 8224 + d * 16 + g * 8,
                                      8224 + d * 16 + g * 8 + 8, runs, 0)
                            for qi, scl in ((0, 128.0 ** -0.5), (1, 1.0)):
                                X = cur[qi]
                                k.tt("dve", S1, S1[:, :], X, X[:, :], X, X[:, :], ALU.mult)

                                def red(e, o=ssq[:, :], i=S1[:, :].rearrange("p (h v) -> p h v", v=128)):
                                    return e.tensor_reduce(o, i, AX.X, ALU.add)
                                k.s.op("dve", red, reads=(S1,), writes=(ssq,))
                                k.tsc("dve", ssq, ssq[:, :], ssq, ssq[:, :], EPS, None, ALU.add)
                                k.act(ssq, ssq[:, :], ssq, ssq[:, :], AF.Sqrt)

                                def rec(e, o=ssq[:, :]):
                                    return e.reciprocal(o, o)
                                k.s.op("dve", rec, reads=(ssq,), writes=(ssq,))
                                if scl != 1.0:
                                    k.tsc("dve", ssq, ssq[:, :], ssq, ssq[:, :], scl, None, ALU.mult)
                                x3 = X[:, :].rearrange("p (h v) -> p h v", v=128)
                                k.tt("dve", X, x3, X, x3, ssq, ssq[:, :].unsqueeze(2).to_broadcast([64, 8, 128]), ALU.mult)
                            Q, Kk, Vv = cur[0], cur[1], cur[2]
                            k.act(beta, beta[:, :], bl, bl[:, :], AF.Sigmoid)
                            k.tt("dve", gt, gt[:, :], al, al[:, :], dtb, dtb[:, :], ALU.add)
                            k.act(gt, gt[:, :], gt, gt[:, :], AF.Exp)
                            k.act(gt, gt[:, :], gt, gt[:, :], AF.Ln, bias=1.0)
                            k.tt("dve", gt, gt[:, :], gt, gt[:, :], negA, negA[:, :], ALU.mult)
                            p0 = p1 = m.pf
                            k.mm(p0, p0[0:64, 0:8], LEf, LEf[:, :], gt, gt[:, :])
                            k.mm(p0, p0[0:64, 8:16], LTof, LTof[:, :], gt, gt[:, :], acc=True)
                            k.mm(p0, p0[:, 16:24], ones_f, ones_f[:, :], gt, gt[:, :], acc=True)
                            k.act(egc, egc[:, :], p0, p0[0:64, 0:8], AF.Exp)
                            k.act(edec, edec[:, :], p0, p0[0:64, 8:16], AF.Exp)
                            k.act(zs, zs[:, :], p0, p0[:, 16:24], AF.Exp)
                            k.tt("dve", gLE, gLE[:, :, :], m.masks["LEf%d" % d],
                                 m.masks["LEf%d" % d][0:64, 0:64].unsqueeze(1).to_broadcast([64, 8, 64]), gt,
                                 gt[:, :].unsqueeze(2).to_broadcast([64, 8, 64]), ALU.mult)
                            k.tt("dve", gLT, gLT[:, :, :], LTof, LTof[0:64, 0:64].unsqueeze(1).to_broadcast([64, 8, 64]), gt,
                                 gt[:, :].unsqueeze(2).to_broadcast([64, 8, 64]), ALU.mult)
                            k.mm(p1, p1[0:64, :], LTof, LTof[:, :], gLE, gLE[:, :, :].rearrange("p h t -> p (h t)"))
                            k.act(EMe, EMe[:, :, :], p1, p1[0:64, :].rearrange("p (h t) -> p h t", t=64), AF.Exp)
                            k.mm(p1, p1[0:64, :], LEf, LEf[:, :], gLT, gLT[:, :, :].rearrange("p h t -> p (h t)"))
                            k.act(EMt, EMt[:, :, :], p1, p1[0:64, :].rearrange("p (h t) -> p h t", t=64), AF.Exp)
                            k.tt("dve", EMs, EMs[:, :, :], EMe, EMe[:, :, :], m.masks["LTf%d" % d],
                                 m.masks["LTf%d" % d][0:64, 0:64].unsqueeze(1).to_broadcast([64, 8, 64]), ALU.mult)
                            k.tt("dve", EMe, EMe[:, :, :], EMe, EMe[:, :, :], LEf,
                                 LEf[0:64, 0:64].unsqueeze(1).to_broadcast([64, 8, 64]), ALU.mult)
                            k.tt("dve", EMt, EMt[:, :, :], EMt, EMt[:, :, :], LTof,
                                 LTof[0:64, 0:64].unsqueeze(1).to_broadcast([64, 8, 64]), ALU.mult)
                            k3 = Kk[:, :].rearrange("p (h v) -> p h v", v=128)
                            q3 = Q[:, :].rearrange("p (h v) -> p h v", v=128)
                            v3 = Vv[:, :].rearrange("p (h v) -> p h v", v=128)

                            def t3(nm):
                                return tl[nm][:, :].rearrange("p (h v) -> p h v", v=128)

                            def bcs(t_):
                                return t_[:, :].unsqueeze(2).to_broadcast([64, 8, 128])
                            k.cp("act", tl["k"], tl["k"][:, :], Kk, Kk[:, :])
                            k.cp("act", tl["q"], tl["q"][:, :], Q, Q[:, :])
                            k.tt("dve", tl["kb"], t3("kb"), Kk, k3, beta, bcs(beta), ALU.mult)
                            k.tt("dve", tl["bv"], t3("bv"), Vv, v3, beta, bcs(beta), ALU.mult)
                            k.tt("dve", tl["qe"], t3("qe"), Q, q3, egc, bcs(egc), ALU.mult)
                            k.tt("dve", tl["aop"], t3("aop"), Kk, k3, edec, bcs(edec), ALU.mult)
                            k.stt(coef, coef[:, :], beta, beta[:, :], -1.0, egc, egc[:, :], ALU.mult, ALU.mult)
                            k.tt("dve", tl["bop"], t3("bop"), Kk, k3, coef, bcs(coef), ALU.mult)
                            for ti_, nm in enumerate(("k", "kb", "q", "qe")):
                                for hh in range(HG):
                                    k.tr(pdb, pdb[:, (ti_ % 2) * 8 + hh, :], tl[nm], tl[nm][:, hh * 128:(hh + 1) * 128],
                                         m.identb, m.identb[0:64, 0:64], partial=not (hh == 0 and ti_ % 2 == 0))
                                k.cp("act", tT[nm], tT[nm][:, :, :], pdb, pdb[:, (ti_ % 2) * 8:(ti_ % 2) * 8 + 8, :])
                            for (dn, ln, rn, em, sgn) in (("NN", "k", "kb", EMs, -1.0), ("NNT", "kb", "k", EMt, -1.0),
                                                          ("PmT", "k", "q", EMe, 1.0)):
                                p_ = m.pf
                                for hh in range(HG):
                                    k.mm(p_, p_[0:64, hh * 64:(hh + 1) * 64], tT[ln], tT[ln][:, hh, :], tT[rn], tT[rn][:, hh, :],
                                         acc=(hh > 0))
                                k.stt(st_[dn], st_[dn][:, :, :], p_, p_[0:64, :].rearrange("p (h t) -> p h t", t=64), sgn,
                                      em, em[:, :, :], ALU.mult, ALU.mult)

                    def back(sq_, ci, nch, runs, bs):
                        tl, tT, st_, zs = tl2[bs], tT2[bs], st2[bs], zs2[bs]
                        S1, ssq = S1e, ssqe
                        if ci == 0:
                            if sq_["kind"] == "sample":
                                core.init_state(R["state_gdn"],
                                                R["state_gdn"].ap[j, d, g * 8:(g + 1) * 8].rearrange("h k v -> k h v"))
                            else:
                                core.init_state()
                        if True:
                            o_t = core.step(dict(NN=st_["NN"], NNT=st_["NNT"], Bop=tl["bop"], U0rhs=tl["bv"], RopT=tT["qe"],
                                                 PmT=st_["PmT"], Aop=tl["aop"], zs=zs))
                            oc0 = g * 1024
                            if d == 0:
                                store_rows(k, m, R["ogd"].ap, oc0, oc0 + 1024, runs, o_t, lambda a, c, t=o_t: t[a:c, :])
                            else:
                                load_rows(k, m, ofw, lambda a, c, t=ofw: t[a:c, :], R["ogd"].ap, oc0, oc0 + 1024, runs)
                                load_rows(k, m, zt, lambda a, c, t=zt: t[a:c, :], proj, 6144 + oc0, 6144 + oc0 + 1024, runs)
                                k.tt("dve", ofw, ofw[:, :], ofw, ofw[:, :], o_t, o_t[:, :], ALU.add)
                                k.tt("dve", S1, S1[:, :], ofw, ofw[:, :], ofw, ofw[:, :], ALU.mult)

                                def red5(e, o=ssq[:, :], i=S1[:, :].rearrange("p (h v) -> p h v", v=128)):
                                    return e.tensor_reduce(o, i, AX.X, ALU.add)
                                k.s.op("dve", red5, reads=(S1,), writes=(ssq,))
                                k.tsc("dve", ssq, ssq[:, :], ssq, ssq[:, :], 1.0 / 128.0, EPS, ALU.mult, ALU.add)
                                k.act(ssq, ssq[:, :], ssq, ssq[:, :], AF.Sqrt)

                                def rec2(e, o=ssq[:, :]):
                                    return e.reciprocal(o, o)
                                k.s.op("dve", rec2, reads=(ssq,), writes=(ssq,))
                                o3 = ofw[:, :].rearrange("p (h v) -> p h v", v=128)
                                k.tt("dve", ofw, o3, ofw, o3, ssq, ssq[:, :].unsqueeze(2).to_broadcast([64, 8, 128]), ALU.mult)
                                k.tt("dve", ofw, o3, ofw, o3, gg, gg[:, :].unsqueeze(1).to_broadcast([64, 8, 128]), ALU.mult)
                                k.act(S1, S1[:, :], zt, zt[:, :], AF.Sigmoid)
                                k.tt("dve", S1, S1[:, :], S1, S1[:, :], zt, zt[:, :], ALU.mult)
                                k.tt("dve", mo, mo[:, :], ofw, ofw[:, :], S1, S1[:, :], ALU.mult)
                                store_rows(k, m, R["mixd"].ap, oc0, oc0 + 1024, runs, mo, lambda a, c, t=mo: t[a:c, :])
                        if sq_["kind"] == "prompt" and ci == nch - 1:
                            core.store_state(R["ns_gdn"],
                                             R["ns_gdn"].ap[sq_["idx"], j, d, g * 8:(g + 1) * 8].rearrange("h k v -> k h v"))

                    prev = None
                    for wi, (sq_, ci, nch, runs) in enumerate(work):
                        bs = wi % 2
                        A = k.s.capture(lambda: front(runs, bs))
                        B = k.s.capture(lambda: back(*prev)) if prev is not None else []
                        k.s.append_merged(A, B)
                        prev = (sq_, ci, nch, runs, bs)
                    B = k.s.capture(lambda: back(*prev))
                    k.s.append_merged([], B)
                    k.s.barrier()


_CACHE = {}


def kernel(**inputs):
    cfg = Cfg(depth=4, ts=4096, np_=4, tp=256)
    n = 8
    if "kb" not in _CACHE:
        _CACHE["kb"] = build(cfg)
    kb = _CACHE["kb"]
    consts = make_consts()
    in_maps = [core_inputs(inputs, c, cfg, consts) for c in range(n)]
    res = run_bass_kernel_spmd(kb.nc, in_maps, core_ids=list(range(n))).results
    y_sample = np.stack([res[c]["y"][:cfg.ts] for c in range(n)]).astype(np.float32)
    y_prompt = np.concatenate([res[c]["y"][cfg.ts:].reshape(cfg.np, cfg.tp, D) for c in range(n)]).astype(np.float32)
    ns_gla = np.concatenate([res[c]["ns_gla"] for c in range(n)]).astype(np.float32)
    ns_rwkv = np.concatenate([res[c]["ns_rwkv"] for c in range(n)]).astype(np.float32)
    ns_gdn = np.concatenate([res[c]["ns_gdn"] for c in range(n)]).astype(np.float32)
    return (y_prompt, y_sample, ns_gla, ns_rwkv, ns_gdn)
```

```python
from contextlib import ExitStack
import numpy as np
import concourse.bass as bass
import concourse.mybir as mybir
from concourse.bass_utils import run_bass_kernel_spmd

F32 = mybir.dt.float32
BF16 = mybir.dt.bfloat16
AF = mybir.ActivationFunctionType
ALU = mybir.AluOpType
AX = mybir.AxisListType

import os
DBG = int(os.environ.get("KDBG", "0"))
SELF_SYNC = True
NDMASEM = 12


class T:
    __slots__ = ("ap", "w", "r", "war", "name", "const", "fw")

    def __init__(self, ap, name="", const=False):
        self.ap = ap
        self.w = []
        self.r = []
        self.war = []
        self.name = name
        self.const = const
        self.fw = None

    def __getitem__(self, idx):
        return self.ap[idx]


class TV:
    def __init__(self, base, ap):
        object.__setattr__(self, "base", base)
        object.__setattr__(self, "ap", ap)

    def __getattr__(self, nm):
        return getattr(self.base, nm)

    def __setattr__(self, nm, v):
        setattr(self.base, nm, v)

    def __getitem__(self, idx):
        return self.ap[idx]


class Op:
    __slots__ = ("eng", "fn", "deps", "signal", "is_dma", "sem", "val", "pos", "prewait")

    def __init__(self, eng, fn, is_dma):
        self.eng = eng
        self.fn = fn
        self.is_dma = is_dma
        self.deps = []
        self.signal = is_dma
        self.sem = None
        self.val = 0
        self.pos = 0
        self.prewait = None


class Sched:
    ENGS = ("pe", "dve", "act", "pool", "sp")

    def __init__(self, nc):
        self.nc = nc
        self.q = {e: [] for e in self.ENGS}
        self.last = {e: None for e in self.ENGS}
        self.dmas_since_barrier = []
        self.nops = 0
        self.cap = None

    def op(self, eng, fn, reads=(), writes=(), pwrites=(), dma=False):
        o = Op(eng, fn, dma)
        deps = o.deps
        for t in reads:
            if t.w:
                deps.extend(t.w)
        for t in writes:
            deps.extend(t.w)
            deps.extend(t.r)
            deps.extend(t.war)
        for t in pwrites:
            if t.r:
                t.war = t.r + t.w
                t.r = []
                t.w = []
            deps.extend(t.war)
            if t.fw is not None:
                deps.append(t.fw)
        for t in reads:
            if not t.const:
                t.r.append(o)
        for t in writes:
            t.war = t.r + t.w
            t.w = [o]
            t.r = []
            t.fw = o
        for t in pwrites:
            t.w.append(o)
        self.nops += 1
        if self.cap is not None:
            self.cap.append(o)
            return o
        self._append(o)
        return o

    def _append(self, o):
        o.pos = len(self.q[o.eng])
        self.q[o.eng].append(o)
        self.last[o.eng] = o
        if o.is_dma:
            self.dmas_since_barrier.append(o)

    def capture(self, fn):
        assert self.cap is None
        self.cap = []
        try:
            fn()
        finally:
            ops, self.cap = self.cap, None
        return ops

    def append_merged(self, A, B):
        ia = ib = 0
        la, lb = len(A), len(B)
        while ia < la or ib < lb:
            if ib >= lb or (ia < la and ia * lb <= ib * la):
                self._append(A[ia])
                ia += 1
            else:
                self._append(B[ib])
                ib += 1

    def dma(self, eng, out_t, out_ap, in_t, in_ap, partial=False, **kw):
        def fn(e):
            return e.dma_start(out=out_ap, in_=in_ap, **kw)
        if partial:
            return self.op(eng, fn, reads=(in_t,), pwrites=(out_t,), dma=True)
        return self.op(eng, fn, reads=(in_t,), writes=(out_t,), dma=True)

    def barrier(self):
        lasts = [self.last[e] for e in self.ENGS if self.last[e] is not None]
        dm = list(self.dmas_since_barrier)
        self.dmas_since_barrier = []
        for e in self.ENGS:
            o = Op(e, None, False)
            o.deps = [x for x in lasts if x.eng != e and not x.is_dma] + dm
            o.pos = len(self.q[e])
            self.q[e].append(o)

    def emit(self):
        nc = self.nc
        for e in self.ENGS:
            for o in self.q[e]:
                for d in o.deps:
                    if d.is_dma:
                        continue
                    if d.eng != o.eng:
                        d.signal = True
                    elif o.eng != "pe" and (SELF_SYNC or o.is_dma):
                        d.signal = True
        esem = {e: nc.alloc_semaphore("sem_" + e) for e in self.ENGS}
        dsem = {e: [nc.alloc_semaphore("dsem_%s_%d" % (e, i)) for i in range(NDMASEM)]
                for e in self.ENGS if any(o.is_dma for o in self.q[e])}
        for e in self.ENGS:
            cnt = 0
            nd = 0
            for o in self.q[e]:
                if o.is_dma:
                    o.sem = dsem[e][nd % NDMASEM]
                    o.val = 16 * (nd // NDMASEM + 1)
                    if nd >= NDMASEM:
                        o.prewait = (o.sem, 16 * (nd // NDMASEM))
                    nd += 1
                elif o.signal:
                    cnt += 1
                    o.sem = esem[e]
                    o.val = cnt
        engobj = {"pe": "tensor", "dve": "vector", "act": "scalar", "pool": "gpsimd", "sp": "sync"}
        with nc.Block() as block:
            for e in self.ENGS:
                ops = self.q[e]
                if not ops:
                    continue

                def body(eng, ops=ops, e=e):
                    waited = {}
                    for o in ops:
                        need = {}
                        if o.prewait is not None:
                            need[id(o.prewait[0])] = (o.prewait[0], o.prewait[1])
                        for d in o.deps:
                            if (not d.is_dma) and d.eng == e:
                                if e == "pe" or not d.signal:
                                    continue
                            k = id(d.sem)
                            if k not in need or need[k][1] < d.val:
                                need[k] = (d.sem, d.val)
                        for k, (sem, val) in need.items():
                            if waited.get(k, 0) < val:
                                eng.wait_ge(sem, val)
                                waited[k] = val
                        if o.fn is None:
                            continue
                        ins = o.fn(eng)
                        if o.is_dma:
                            ins.then_inc(o.sem, 16)
                        elif o.signal:
                            ins.then_inc(o.sem, 1)
                    fin = {}
                    for o in ops:
                        if o.is_dma:
                            fin[id(o.sem)] = (o.sem, o.val)
                    for k, (sem, val) in fin.items():
                        if waited.get(k, 0) < val:
                            eng.wait_ge(sem, val)
                            waited[k] = val

                getattr(block, engobj[e])(body)


D = 2048
GRID_W = 64
CH = 64
EVEN_SIZES = (512, 512, 1024, 1024, 32, 1024, 1024, 1024, 1024, 128, 128)
ODD_SIZES = (2048, 2048, 2048, 2048, 32, 32)
EVEN_IN = sum(EVEN_SIZES)
ODD_IN = sum(ODD_SIZES)
EPS = 1e-6
GN_EPS = 64e-5


class Cfg:
    def __init__(self, depth=4, ts=4096, np_=4, tp=256):
        self.depth = depth
        self.ts = ts
        self.np = np_
        self.tp = tp
        self.ntok = ts + np_ * tp
        assert self.ntok % 128 == 0 and ts % 128 == 0
        self.ntile = self.ntok // 128
        self.n_even = (depth + 1) // 2
        self.n_odd = depth // 2


def make_consts():
    c = {}
    c["ident"] = np.eye(128, dtype=np.float32)
    sel = np.zeros((2, 2, 128), np.float32)
    sel[0, 0, :] = 1.0
    sel[1, 1, :] = 1.0
    c["sel"] = sel
    i = np.arange(64)
    le0 = (i[:, None] <= i[None, :]).astype(np.float32)
    lt0 = (i[:, None] < i[None, :]).astype(np.float32)
    c["masks"] = np.stack([le0, le0.T.copy(), lt0, lt0.T.copy()]).astype(np.float32)
    c["zeros"] = np.zeros((1, ODD_IN), np.float32)
    blk = lambda n: ((i[:, None] // n) == (i[None, :] // n)).astype(np.float32)
    c["bmask"] = np.stack([blk(8), blk(16) - blk(8), blk(32) - blk(16), 1.0 - blk(32)]).astype(np.float32)
    return c


class K:
    def __init__(self, cfg):
        self.cfg = cfg
        self.nc = bass.Bass("TRN2", target_bir_lowering=False)
        self.s = Sched(self.nc)
        self.es = ExitStack()
        self.uid = 0

    def dram(self, name, shape, kind, dt=F32):
        h = self.nc.dram_tensor(name, list(shape), dt, kind=kind)
        return T(h.ap(), name)

    def sb(self, stack, shape, dt=F32, name=None, const=False):
        self.uid += 1
        nm = "%s_%d" % (name or "t", self.uid)
        ap = stack.enter_context(self.nc.sbuf_tensor(nm, list(shape), dt))
        return T(ap, nm, const=const)

    def ps(self, stack, shape, dt=F32, name=None):
        self.uid += 1
        nm = "%s_%d" % (name or "p", self.uid)
        ap = stack.enter_context(self.nc.psum_tensor(nm, list(shape), dt))
        return T(ap, nm)

    def mm(self, out_t, out_ap, l_t, l_ap, r_t, r_ap, start=True, stop=True, acc=False):
        def fn(e):
            return e.matmul(out_ap, l_ap, r_ap, start=start, stop=stop)
        return self.s.op("pe", fn, reads=(l_t, r_t), writes=(out_t,)) if not acc else \
            self.s.op("pe", fn, reads=(l_t, r_t), pwrites=(out_t,))

    def tr(self, out_t, out_ap, in_t, in_ap, id_t, id_ap, partial=True):
        def fn(e):
            return e.transpose(out_ap, in_ap, id_ap)
        if partial:
            return self.s.op("pe", fn, reads=(in_t, id_t), pwrites=(out_t,))
        return self.s.op("pe", fn, reads=(in_t, id_t), writes=(out_t,))

    def act(self, out_t, out_ap, in_t, in_ap, func, bias=None, scale=None, accum=None, extra_reads=(),
            partial=False, eng="act"):
        kw = {}
        if bias is not None:
            kw["bias"] = bias
        if scale is not None:
            kw["scale"] = scale
        wr = [out_t]
        if accum is not None:
            kw["accum_out"] = accum[1]
            wr.append(accum[0])

        def fn(e):
            return e.activation(out_ap, in_ap, func, **kw)
        if partial:
            return self.s.op(eng, fn, reads=(in_t,) + tuple(extra_reads), pwrites=tuple(wr))
        return self.s.op(eng, fn, reads=(in_t,) + tuple(extra_reads), writes=tuple(wr))

    def tsc(self, eng, out_t, out_ap, in_t, in_ap, s1, s2, op0, op1=None, extra_reads=(), partial=False):
        def fn(e):
            if op1 is None:
                return e.tensor_scalar(out_ap, in_ap, s1, None, op0)
            return e.tensor_scalar(out_ap, in_ap, s1, s2, op0, op1)
        if partial:
            return self.s.op(eng, fn, reads=(in_t,) + tuple(extra_reads), pwrites=(out_t,))
        return self.s.op(eng, fn, reads=(in_t,) + tuple(extra_reads), writes=(out_t,))

    def tt(self, eng, out_t, out_ap, a_t, a_ap, b_t, b_ap, op, partial=False):
        def fn(e):
            return e.tensor_tensor(out_ap, a_ap, b_ap, op)
        if partial:
            return self.s.op(eng, fn, reads=(a_t, b_t), pwrites=(out_t,))
        return self.s.op(eng, fn, reads=(a_t, b_t), writes=(out_t,))

    def stt(self, out_t, out_ap, a_t, a_ap, scalar, b_t, b_ap, op0, op1, extra_reads=(), partial=False):
        def fn(e):
            return e.scalar_tensor_tensor(out_ap, a_ap, scalar, b_ap, op0, op1)
        if partial:
            return self.s.op("dve", fn, reads=(a_t, b_t) + tuple(extra_reads), pwrites=(out_t,))
        return self.s.op("dve", fn, reads=(a_t, b_t) + tuple(extra_reads), writes=(out_t,))

    def cp(self, eng, out_t, out_ap, in_t, in_ap, partial=False):
        if eng == "act":
            def fn(e):
                return e.copy(out_ap, in_ap)
        else:
            def fn(e):
                return e.tensor_copy(out_ap, in_ap)
        if partial:
            return self.s.op(eng, fn, reads=(in_t,), pwrites=(out_t,))
        return self.s.op(eng, fn, reads=(in_t,), writes=(out_t,))

    def memset(self, eng, t, ap, val):
        def fn(e):
            return e.memset(ap, val)
        return self.s.op(eng, fn, writes=(t,))

    def ld(self, out_t, out_ap, in_t, in_ap, eng="sp", partial=False, **kw):
        return self.s.dma(eng, out_t, out_ap, in_t, in_ap, partial=partial, **kw)


class StopBuild(Exception):
    pass


_CKPT = [0]
KSTOPG = int(os.environ.get("KSTOPG", "0"))


def ckpt():
    _CKPT[0] += 1
    if KSTOPG and _CKPT[0] >= KSTOPG:
        raise StopBuild()


def build(cfg, mixer_mode="full", stop=None):
    k = K(cfg)
    try:
        _build(k, cfg, mixer_mode, stop)
    except StopBuild:
        pass
    k.s.emit()
    return k


def _build(k, cfg, mixer_mode, stop):
    def stage(name):
        if stop == name:
            raise StopBuild()
    nc = k.nc
    s = k.s
    NT = cfg.ntile
    depth = cfg.depth
    xin = k.dram("xin", [cfg.ntok, D], "ExternalInput")
    cvec = k.dram("cvec", [2, D], "ExternalInput")
    norm_g = k.dram("norm_g", [depth, D], "ExternalInput")
    w_ada = k.dram("w_ada", [depth, D, 3 * D], "ExternalInput")
    b_ada = k.dram("b_ada", [depth, 3 * D], "ExternalInput")
    w_in_even = k.dram("w_in_even", [cfg.n_even, D, EVEN_IN], "ExternalInput")
    w_in_odd = k.dram("w_in_odd", [max(cfg.n_odd, 1), D, ODD_IN], "ExternalInput")
    w_out = k.dram("w_out", [depth, D, D], "ExternalInput")
    final_g = k.dram("final_g", [1, D], "ExternalInput")
    ident_d = k.dram("ident", [128, 128], "ExternalInput")
    sel_d = k.dram("sel", [2, 2, 128], "ExternalInput")
    masks_d = k.dram("masks", [4, 64, 64], "ExternalInput")
    bmask_d = k.dram("bmask", [4, 64, 64], "ExternalInput")
    zeros = k.dram("zeros", [1, ODD_IN], "ExternalInput")
    ne, no_ = cfg.n_even, max(cfg.n_odd, 1)
    P_ = {}
    for nm, shp in (("state_gla", [ne, 2, 4, 128, 256]), ("state_rwkv", [ne, 2, 16, 64, 64]),
                    ("state_gdn", [no_, 2, 16, 128, 128]),
                    ("gla_w2", [ne, 2, 16, 512]), ("gla_b", [ne, 2, 512]), ("gla_g", [ne, 256]),
                    ("rwkv_mu", [ne, 3, 2, 1024]), ("rwkv_w0", [ne, 2, 1024]), ("rwkv_w2", [ne, 2, 64, 1024]),
                    ("rwkv_a0", [ne, 2, 1024]), ("rwkv_a2", [ne, 2, 64, 1024]), ("rwkv_k_k", [ne, 1024]),
                    ("rwkv_k_a", [ne, 1024]), ("rwkv_r_k", [ne, 1024]), ("rwkv_gn_g", [ne, 1024]),
                    ("rwkv_gn_b", [ne, 1024]), ("gdn_conv_w", [no_, 3, 6144]), ("gdn_A_log", [no_, 2, 16]),
                    ("gdn_dt_bias", [no_, 2, 16]), ("gdn_g", [no_, 128])):
        P_[nm] = k.dram(nm, shp, "ExternalInput")
    P_["ns_gla"] = k.dram("ns_gla", [cfg.np, ne, 2, 4, 128, 256], "ExternalOutput")
    P_["ns_rwkv"] = k.dram("ns_rwkv", [cfg.np, ne, 2, 16, 64, 64], "ExternalOutput")
    P_["ns_gdn"] = k.dram("ns_gdn", [cfg.np, no_, 2, 16, 128, 128], "ExternalOutput")
    P_["ogla"] = k.dram("ogla", [cfg.ntok, 1024], "Internal")
    P_["orw"] = k.dram("orw", [cfg.ntok, 1024], "Internal")
    P_["ogd"] = k.dram("ogd", [cfg.ntok, 2048], "Internal")
    P_["bsum"] = k.dram("bsum", [cfg.ntok, 16], "Internal")
    P_["zeros"] = zeros
    yout = k.dram("y", [cfg.ntok, D], "ExternalOutput")
    xcur = [k.dram("xcur%d" % i, [cfg.ntok, D], "Internal") for i in range(2)]
    proj = k.dram("proj", [cfg.ntok, ODD_IN], "Internal")
    mixd = k.dram("mixd", [cfg.ntok, D], "Internal", BF16)
    def rowtiles(t, n):
        return [T(t.ap, "%s_r%d" % (t.name, i)) for i in range(n)]
    xcur_rt = [rowtiles(x, NT) for x in xcur]
    proj_rt = rowtiles(proj, NT)
    mix_rt = rowtiles(mixd, NT)
    yout_rt = rowtiles(yout, NT)

    def cond_of_tile(tt):
        return 0 if tt * 128 < cfg.ts else 1

    with ExitStack() as glob:
        ident = k.sb(glob, [128, 128], F32, "ident", const=True)
        k.ld(ident, ident[:, :], ident_d, ident_d[:, :])
        gT = k.sb(glob, [128, depth * 16], F32, "gT")
        bT = k.sb(glob, [128, depth * 48], F32, "bT")
        sT = k.sb(glob, [128, 16, 2], F32, "sT")
        fg_bc = k.sb(glob, [128, D], F32, "fg_bc")
        k.ld(fg_bc, fg_bc[:, :], final_g, final_g.ap[0:1, :].to_broadcast([128, D]))
        sel = k.sb(glob, [2, 2, 128], F32, "sel", const=True)
        k.ld(sel, sel[:, :, :], sel_d, sel_d[:, :, :])
        identb = k.sb(glob, [128, 128], BF16, "identb", const=True)
        k.cp("dve", identb, identb[:, :], ident, ident[:, :])
        maskf = k.sb(glob, [64, 4, 64], F32, "maskf", const=True)
        k.ld(maskf, maskf[:, :, :], masks_d, masks_d.ap.rearrange("m s t -> s m t"))
        maskb = k.sb(glob, [64, 4, 64], BF16, "maskb", const=True)
        k.cp("dve", maskb, maskb[:, :, :], maskf, maskf[:, :, :])
        bmaskf = k.sb(glob, [64, 4, 64], F32, "bmaskf", const=True)
        k.ld(bmaskf, bmaskf[:, :, :], bmask_d, bmask_d.ap.rearrange("m s t -> s m t"))
        masks = {}
        masks["bmask"] = [T(bmaskf.ap[:, mi, :], "bm%d" % mi, const=True) for mi in range(4)]
        masks["identf64"] = T(ident.ap[0:64, 0:64], "identf64", const=True)
        for mi, mn in enumerate(("LE0", "LE1", "LT0", "LT1")):
            masks[mn] = T(maskb.ap[:, mi, :], mn, const=True)
            masks[mn[0:2] + "f" + mn[2]] = T(maskf.ap[:, mi, :], mn + "f", const=True)
        with ExitStack() as st0:
            rows = k.sb(st0, [64, 128], F32, "rows")
            pst = k.ps(st0, [128, 512], F32, "pst")
            n_r = depth * 16
            k.ld(rows, rows[0:n_r, :], norm_g, norm_g.ap.rearrange("l (c p) -> (l c) p", p=128))
            k.tr(pst, pst[:, 0:n_r], rows, rows[0:n_r, :], ident, ident[0:n_r, 0:n_r], partial=False)
            k.cp("dve", gT, gT[:, :], pst, pst[:, 0:n_r])
            for l in range(depth):
                k.ld(rows, rows[0:48, :], b_ada, b_ada.ap[l:l + 1, :].rearrange("o (c p) -> (o c) p", p=128))
                k.tr(pst, pst[:, 0:48], rows, rows[0:48, :], ident, ident[0:48, 0:48], partial=False)
                k.cp("dve", bT, bT[:, l * 48:(l + 1) * 48], pst, pst[:, 0:48], partial=True)
            cv = k.sb(st0, [2, D], F32, "cv")
            sg = k.sb(st0, [2, D], F32, "sg")
            k.ld(cv, cv[:, :], cvec, cvec[:, :])
            k.act(sg, sg[:, :], cv, cv[:, :], AF.Sigmoid)
            k.tt("dve", sg, sg[:, :], sg, sg[:, :], cv, cv[:, :], ALU.mult)
            for c in range(16):
                k.tr(pst, pst[:, 64 + 2 * c:64 + 2 * c + 2], sg, sg[0:2, c * 128:(c + 1) * 128], ident, ident[0:2, 0:2],
                     partial=(c > 0))
            k.cp("dve", sT, sT[:, :, :], pst, pst[:, 64:96].rearrange("p (c t) -> p c t", t=2))
        s.barrier()
        stage("consts")

        for layer in range(depth + 1):
            last = layer == depth
            with ExitStack() as L:
                gate_bc = None
                A_T = B_T = None
                if not last:
                    A_T = k.sb(L, [128, 2, 16], F32, "A_T")
                    B_T = k.sb(L, [128, 2, 16], F32, "B_T")
                if layer > 0:
                    gate_bc = k.sb(L, [128, 2, D], F32, "gate_bc")
                with ExitStack() as M:
                    slab = [k.sb(M, [128, 16, 512], F32, "adaslab") for _ in range(2)]
                    psm = k.ps(M, [128, 512], F32, "psm")
                    psg = k.ps(M, [128, 512], F32, "psg")
                    nsl = 0
                    if not last:
                        modT = k.sb(M, [128, 32, 2], F32, "modT")
                        for sl in range(8):
                            sb_ = slab[nsl % 2]
                            nsl += 1
                            k.ld(sb_, sb_[:, :, :], w_ada,
                                 w_ada.ap[layer, :, sl * 512:(sl + 1) * 512].rearrange("(c p) n -> p c n", p=128))
                            for j in range(4):
                                blk = sl * 4 + j
                                for kc in range(16):
                                    k.mm(psm, psm[:, 2 * blk:2 * blk + 2], sb_, sb_[:, kc, j * 128:(j + 1) * 128],
                                         sT, sT[:, kc, :], start=(kc == 0), stop=(kc == 15),
                                         acc=not (blk == 0 and kc == 0))
                        k.cp("dve", modT, modT[:, :, :], psm, psm[:, 0:64].rearrange("p (b t) -> p b t", t=2))
                        for cnd in range(2):
                            k.tt("dve", B_T, B_T[:, cnd, :], modT, modT[:, 0:16, cnd], bT,
                                 bT[:, layer * 48:layer * 48 + 16], ALU.add, partial=(cnd > 0))
                            k.tt("dve", A_T, A_T[:, cnd, :], modT, modT[:, 16:32, cnd], bT,
                                 bT[:, layer * 48 + 16:layer * 48 + 32], ALU.add, partial=(cnd > 0))
                        for cnd in range(2):
                            k.stt(A_T, A_T[:, cnd, :], A_T, A_T[:, cnd, :], 1.0, gT, gT[:, layer * 16:(layer + 1) * 16],
                                  ALU.add, ALU.mult, partial=True)
                    if layer > 0:
                        pl = layer - 1
                        grow = k.sb(M, [2, D], F32, "grow")
                        brow = k.sb(M, [2, D], F32, "brow")
                        for r in range(2):
                            k.ld(brow, brow[r:r + 1, :], b_ada, b_ada.ap[pl:pl + 1, 2 * D:3 * D], partial=(r > 0))
                        for sl in range(4):
                            sb_ = slab[nsl % 2]
                            nsl += 1
                            k.ld(sb_, sb_[:, :, :], w_ada,
                                 w_ada.ap[pl, :, 2 * D + sl * 512:2 * D + (sl + 1) * 512].rearrange("(c p) n -> p c n", p=128))
                            for kc in range(16):
                                k.mm(psg, psg[0:2, :], sT, sT[:, kc, :], sb_, sb_[:, kc, :],
                                     start=(kc == 0), stop=(kc == 15), acc=(kc > 0))
                            k.tt("dve", grow, grow[:, sl * 512:(sl + 1) * 512], psg, psg[0:2, :], brow,
                                 brow[:, sl * 512:(sl + 1) * 512], ALU.add, partial=(sl > 0))
                        for cnd in range(2):
                            for sl in range(4):
                                k.mm(psg, psg[:, :], sel, sel[:, cnd, :], grow, grow[:, sl * 512:(sl + 1) * 512])
                                k.cp("dve", gate_bc, gate_bc[:, cnd, sl * 512:(sl + 1) * 512], psg, psg[:, :],
                                     partial=not (cnd == 0 and sl == 0))
                s.barrier()
                stage("mod%d" % layer)
                lin_phase(k, cfg, layer, L, dict(
                    xin=xin, xcur=xcur, xcur_rt=xcur_rt, proj=proj, proj_rt=proj_rt, mixd=mixd, mix_rt=mix_rt,
                    yout=yout, yout_rt=yout_rt, w_in_even=w_in_even, w_in_odd=w_in_odd, w_out=w_out,
                    ident=ident, identb=identb, A_T=A_T, B_T=B_T, gate_bc=gate_bc, fg_bc=fg_bc, cond_of_tile=cond_of_tile))
                s.barrier()
                stage("lin%d" % layer)
            if True:
                if not last:
                    RR = dict(proj=proj, proj_rt=proj_rt, mixd=mixd, mix_rt=mix_rt, ident=ident, identb=identb,
                              masks=masks)
                    RR.update(P_)
                    mixer_phase(k, cfg, layer, RR, mixer_mode)
                    s.barrier()


def lin_phase(k, cfg, layer, L, R):
    s = k.s
    depth = cfg.depth
    last = layer == depth
    first = layer == 0
    NT = cfg.ntile
    ident = R["ident"]
    identb = R["identb"]
    x_src = R["xin"] if layer <= 1 else R["xcur"][(layer - 1) % 2]
    x_src_rt = None if layer <= 1 else R["xcur_rt"][(layer - 1) % 2]
    x_dst = R["xcur"][layer % 2]
    x_dst_rt = R["xcur_rt"][layer % 2]
    if not last:
        even = layer % 2 == 0
        win = R["w_in_even"] if even else R["w_in_odd"]
        NOUT = EVEN_IN if even else ODD_IN
        nslab = (NOUT + 511) // 512
    GT = 8
    groups = []
    t0 = 0
    nts = cfg.ts // 128
    while t0 < nts:
        g = min(GT, nts - t0)
        groups.append((t0, g))
        t0 += g
    while t0 < NT:
        g = min(GT, NT - t0)
        groups.append((t0, g))
        t0 += g
    with ExitStack() as P:
        hT = k.sb(P, [128, 16, GT * 128], BF16, "hT") if not last else None
        xt = [k.sb(P, [128, D], F32, "xt") for _ in range(2)]
        xn = [k.sb(P, [128, 512], F32, "xn") for _ in range(2)]
        ss = [k.sb(P, [128, 1], F32, "ss") for _ in range(2)]
        rstd = [k.sb(P, [128, 1], F32, "rstd") for _ in range(2)]
        junk = k.sb(P, [128, D], BF16, "junk")
        pt = [k.ps(P, [128, 512], F32, "pt") for _ in range(2)]
        pm = [k.ps(P, [128, 512], F32, "pm") for _ in range(4)]
        if not last:
            wsl = [k.sb(P, [128, 16, 512], BF16, "wsl") for _ in range(2)]
            po = [k.sb(P, [128, 512], F32, "po") for _ in range(2)]
        if last:
            yb = [k.sb(P, [128, D], F32, "yb") for _ in range(2)]
        if not first:
            ptb = [k.ps(P, [128, 8, 128], BF16, "ptb") for _ in range(2)]
            mt = [k.sb(P, [128, D], BF16, "mt") for _ in range(2)]
            mixT = [k.sb(P, [128, 16, 128], BF16, "mixT") for _ in range(2)]
            tmp = [k.sb(P, [128, 512], F32, "tmp") for _ in range(2)]
            wout = [k.sb(P, [128, 16, 512], BF16, "wout") for _ in range(4)]
            for oc in range(4):
                k.ld(wout[oc], wout[oc][:, :, :], R["w_out"],
                     R["w_out"].ap[layer - 1, :, oc * 512:(oc + 1) * 512].rearrange("(c p) n -> p c n", p=128),
                     eng="pool")
        nws = 0
        npm = 0
        npo = 0
        ntl = 0
        nxn = 0
        for (g0, gn) in groups:
            for ti in range(gn):
                tt = g0 + ti
                cnd = R["cond_of_tile"](tt)
                b = ntl % 2
                ntl += 1
                X = xt[b]
                rows = slice(tt * 128, (tt + 1) * 128)
                if first:
                    k.ld(X, X[:, :], x_src, x_src.ap[rows, :])
                else:
                    k.ld(X, X[:, :], x_src_rt[tt] if x_src_rt is not None else x_src, x_src.ap[rows, :])
                    M_ = mt[b]
                    k.ld(M_, M_[:, :], R["mix_rt"][tt], R["mixd"].ap[rows, :])
                    MT = mixT[b]
                    for h2 in range(2):
                        p_ = ptb[h2]
                        for j in range(8):
                            c = h2 * 8 + j
                            k.tr(p_, p_[:, j, :], M_, M_[:, c * 128:(c + 1) * 128], identb, identb[:, :],
                                 partial=(j > 0))
                        k.cp("act", MT, MT[:, h2 * 8:(h2 + 1) * 8, :], p_, p_[:, :, :], partial=(h2 > 0))
                    for oc in range(4):
                        w = wout[oc]
                        p_ = pm[npm % 4]
                        npm += 1
                        for kc in range(16):
                            k.mm(p_, p_[:, :], MT, MT[:, kc, :], w, w[:, kc, :], start=(kc == 0), stop=(kc == 15),
                                 acc=(kc > 0))
                        tm = tmp[oc % 2]
                        k.tt("dve", tm, tm[:, :], p_, p_[:, :], R["gate_bc"], R["gate_bc"][:, cnd, oc * 512:(oc + 1) * 512],
                             ALU.mult)
                        k.tt("pool", X, X[:, oc * 512:(oc + 1) * 512], tm, tm[:, :], X, X[:, oc * 512:(oc + 1) * 512],
                             ALU.add, partial=True)
                    if not last:
                        k.ld(x_dst_rt[tt], x_dst.ap[rows, :], X, X[:, :], eng="sp")
                k.act(junk, junk[:, :], X, X[:, :], AF.Square, accum=(ss[b], ss[b][:, :]))
                k.tsc("dve", rstd[b], rstd[b][:, :], ss[b], ss[b][:, :], 1.0 / D, EPS, ALU.mult, ALU.add)
                k.act(rstd[b], rstd[b][:, :], rstd[b], rstd[b][:, :], AF.Sqrt)

                def rec(e, o=rstd[b][:, :]):
                    return e.reciprocal(o, o)
                s.op("dve", rec, reads=(rstd[b],), writes=(rstd[b],))
                if last:
                    k.stt(yb[b], yb[b][:, :], X, X[:, :], rstd[b][:, 0:1], R["fg_bc"], R["fg_bc"][:, :], ALU.mult,
                          ALU.mult, extra_reads=(rstd[b],))
                    k.ld(R["yout_rt"][tt], R["yout"].ap[rows, :], yb[b], yb[b][:, :], eng="sp")
                    continue
                if DBG & 4:
                    continue
                for q4 in range(4):
                    xq = xn[nxn % 2]
                    nxn += 1
                    k.tsc("dve" if (q4 % 2 == 0 or DBG & 1) else "pool", xq, xq[:, :], X, X[:, q4 * 512:(q4 + 1) * 512],
                          rstd[b][:, 0:1], None, ALU.mult, extra_reads=(rstd[b],))
                    p_ = pt[q4 % 2]
                    for j in range(4):
                        k.tr(p_, p_[:, j * 128:(j + 1) * 128], xq, xq[:, j * 128:(j + 1) * 128], ident, ident[:, :],
                             partial=(j > 0))
                    for j in range(4):
                        c = q4 * 4 + j
                        eng = "act" if (j % 2 == 0 and DBG & 8) else "dve"
                        if DBG & 16:
                            continue
                        if eng == "act":
                            k.act(hT, hT[:, c, ti * 128:(ti + 1) * 128], p_, p_[:, j * 128:(j + 1) * 128], AF.Identity,
                                  bias=R["B_T"][:, cnd, c:c + 1], scale=R["A_T"][:, cnd, c:c + 1],
                                  extra_reads=(R["A_T"], R["B_T"]), partial=True)
                        else:
                            k.tsc("dve", hT, hT[:, c, ti * 128:(ti + 1) * 128], p_, p_[:, j * 128:(j + 1) * 128],
                                  R["A_T"][:, cnd, c:c + 1], R["B_T"][:, cnd, c:c + 1], ALU.mult, ALU.add,
                                  extra_reads=(R["A_T"], R["B_T"]), partial=True)
            if last or DBG & 2:
                continue
            for sl in range(nslab):
                c0 = sl * 512
                cw = min(512, NOUT - c0)
                w = wsl[nws % 2]
                nws += 1
                k.ld(w, w[:, :, 0:cw], win, win.ap[layer // 2, :, c0:c0 + cw].rearrange("(c p) n -> p c n", p=128),
                     eng="pool")
                for ti in range(gn):
                    tt = g0 + ti
                    p_ = pm[npm % 4]
                    npm += 1
                    for kc in range(16):
                        k.mm(p_, p_[:, 0:cw], hT, hT[:, kc, ti * 128:(ti + 1) * 128], w, w[:, kc, 0:cw],
                             start=(kc == 0), stop=(kc == 15), acc=(kc > 0))
                    o_ = po[npo % 2]
                    k.cp("act" if npo % 2 == 0 else "dve", o_, o_[:, 0:cw], p_, p_[:, 0:cw])
                    npo += 1
                    k.ld(R["proj_rt"][tt], R["proj"].ap[tt * 128:(tt + 1) * 128, c0:c0 + cw], o_, o_[:, 0:cw],
                         eng="sp", partial=True)


def seq_list(cfg, layer):
    even = layer % 2 == 0
    seqs = []
    nch = cfg.ts // 64
    R = cfg.ts // 64
    chunks = []
    for j in range(nch):
        if even:
            chunks.append([(0, 64, j * 64, 1, True, True)])
        elif R >= 64:
            assert R == 64
            chunks.append([(0, 64, j, 64, True, True)])
        else:
            cpc = 64 // R
            chunks.append([(m * R, R, j * cpc + m, 64, True, True) for m in range(cpc)])
    seqs.append(dict(kind="sample", idx=0, chunks=chunks))
    ncp = cfg.tp // 64
    for q in range(cfg.np):
        b = cfg.ts + q * cfg.tp
        seqs.append(dict(kind="prompt", idx=q,
                         chunks=[[(0, 64, b + j * 64, 1, j == 0, j == ncp - 1)] for j in range(ncp)]))
    return seqs


class MX:
    pass


def load_rows(k, m, dst_t, rowfn, src_ap, c0, c1, runs, shift=0):
    for (p0, n, r0, st, s0, s1) in runs:
        if shift == 0:
            k.ld(dst_t, rowfn(p0, p0 + n), m.dr, src_ap[r0:r0 + st * (n - 1) + 1:st, c0:c1], partial=True)
        elif shift == -1:
            if s0:
                k.ld(dst_t, rowfn(p0, p0 + 1), m.dr, m.zeros.ap[0:1, 0:c1 - c0], partial=True)
                if n > 1:
                    k.ld(dst_t, rowfn(p0 + 1, p0 + n), m.dr, src_ap[r0:r0 + st * (n - 2) + 1:st, c0:c1], partial=True)
            else:
                k.ld(dst_t, rowfn(p0, p0 + n), m.dr, src_ap[r0 - st:r0 - st + st * (n - 1) + 1:st, c0:c1], partial=True)
        else:
            if s1:
                k.ld(dst_t, rowfn(p0 + n - 1, p0 + n), m.dr, m.zeros.ap[0:1, 0:c1 - c0], partial=True)
                if n > 1:
                    k.ld(dst_t, rowfn(p0, p0 + n - 1), m.dr, src_ap[r0 + st:r0 + st + st * (n - 2) + 1:st, c0:c1],
                         partial=True)
            else:
                k.ld(dst_t, rowfn(p0, p0 + n), m.dr, src_ap[r0 + st:r0 + st + st * (n - 1) + 1:st, c0:c1], partial=True)


def store_rows(k, m, dst_ap, c0, c1, runs, src_t, rowfn):
    for (p0, n, r0, st, s0, s1) in runs:
        k.ld(m.dw, dst_ap[r0:r0 + st * (n - 1) + 1:st, c0:c1], src_t, rowfn(p0, p0 + n), partial=True)


class ScanCore:
    def __init__(self, k, m, P, K, V, H, has_u, has_q, has_k2, post_scale, name):
        self.k, self.m = k, m
        self.K, self.V, self.H = K, V, H
        self.has_u, self.has_q, self.has_k2, self.post_scale = has_u, has_q, has_k2, post_scale
        self.Z = k.sb(P, [K, H, V], F32, name + "Z")
        self.Zb = k.sb(P, [K, H, V], BF16, name + "Zb")
        self.o = [k.sb(P, [64, H * V], F32, name + "o") for _ in range(1)]
        self.no = 0
        if has_u:
            self.iv = {nm: k.sb(P, [64, 8, 64], F32, name + nm) for nm in
                       ("N0", "NT0", "E0", "E1", "E2", "Xa", "Xb", "XTa", "XTb", "N2", "NT2", "N4")}
            self.iv["Y"] = self.iv["N2"]
            self.Xf = k.sb(P, [64, H, 64], BF16, name + "Xf")
            self.BhT = k.sb(P, [K, H, 64], BF16, name + "BhT")
            self.U0 = k.sb(P, [64, H * V], F32, name + "U0")
            self.U = k.sb(P, [64, H * V], BF16, name + "U")
        self.ztmp = k.sb(P, [K, H, V], F32, name + "zt")

    def init_state(self, src_t=None, src_ap=None):
        k = self.k
        if src_ap is None:
            k.memset("pool", self.Z, self.Z[:, :, :], 0.0)
        else:
            k.ld(self.Z, self.Z[:, :, :], src_t, src_ap)
        k.cp("act", self.Zb, self.Zb[:, :, :], self.Z, self.Z[:, :, :])

    def store_state(self, dst_t, dst_ap):
        self.k.ld(dst_t, dst_ap, self.Z, self.Z[:, :, :], partial=True)

    def _banks(self, width):
        return (width + 511) // 512

    def step(self, I):
        k, m = self.k, self.m
        K, V, H = self.K, self.V, self.H
        HV = H * V
        nb = self._banks(HV)
        hpb = 512 // V
        pd = m.pd
        evn = [0]

        def evac_eng():
            evn[0] += 1
            return "act" if evn[0] % 2 == 0 else "dve"

        if self.has_u:
            NN, NNT = I["NN"], I["NNT"]
            ident = m.identf64
            iv = self.iv
            for hg in range(0, H, 8):
                nh = min(8, H - hg)
                hs = slice(hg, hg + nh)

                def bcm(mt):
                    return mt[0:64, 0:64].unsqueeze(1).to_broadcast([64, nh, 64])

                def v(t_):
                    return t_[:, 0:nh, :]
                k.tt("dve", iv["N0"], v(iv["N0"]), NN, NN[:, hs, :], m.bmask[0], bcm(m.bmask[0]), ALU.mult)
                k.tt("dve", iv["NT0"], v(iv["NT0"]), NNT, NNT[:, hs, :], m.bmask[0], bcm(m.bmask[0]), ALU.mult)
                for l in range(3):
                    k.tt("dve", iv["E%d" % l], v(iv["E%d" % l]), NN, NN[:, hs, :], m.bmask[l + 1], bcm(m.bmask[l + 1]),
                         ALU.mult)
                X, XT = iv["Xa"], iv["XTa"]
                Xn, XTn = iv["Xb"], iv["XTb"]
                k.tt("dve", X, v(X), iv["N0"], v(iv["N0"]), ident, bcm(ident), ALU.add)
                k.tt("dve", XT, v(XT), iv["NT0"], v(iv["NT0"]), ident, bcm(ident), ALU.add)
                pc = [0]

                def mmh(L_, R_):
                    p_ = m.pd[pc[0] % 2]
                    pc[0] += 1
                    for hh in range(nh):
                        k.mm(p_, p_[0:64, hh * 64:(hh + 1) * 64], L_, L_[:, hh, :], R_, R_[:, hh, :], acc=(hh > 0))
                    return p_, p_[0:64, 0:nh * 64].rearrange("p (h t) -> p h t", t=64)

                def evc(dst, L_, R_):
                    p_, pap = mmh(L_, R_)
                    k.cp("act", dst, v(dst), p_, pap)

                def eva(dst, base, L_, R_):
                    p_, pap = mmh(L_, R_)
                    k.tt("dve", dst, v(dst), p_, pap, base, v(base), ALU.add)
                evc(iv["N2"], iv["NT0"], iv["N0"])
                evc(iv["NT2"], iv["N0"], iv["NT0"])
                eva(Xn, X, XT, iv["N2"])
                eva(XTn, XT, iv["N2"], XT)
                X, XT, Xn, XTn = Xn, XTn, X, XT
                evc(iv["N4"], iv["NT2"], iv["N2"])
                eva(Xn, X, XT, iv["N4"])
                eva(XTn, XT, iv["N4"], XT)
                X, XT, Xn, XTn = Xn, XTn, X, XT
                for l in range(3):
                    evc(iv["Y"], iv["E%d" % l], XT)
                    eva(Xn, X, iv["Y"], X)
                    if l < 2:
                        eva(XTn, XT, X, iv["Y"])
                    X, XT, Xn, XTn = Xn, XTn, X, XT
                k.cp("act", self.Xf, self.Xf[:, hs, :], X, v(X), partial=(hg > 0))
            X = self.Xf
            Bop = I["Bop"]
            kpb = 512 // 64
            for g in range(0, H, kpb):
                p_ = pd[(g // kpb) % 2]
                for hh in range(g, min(H, g + kpb)):
                    k.mm(p_, p_[0:K, (hh - g) * 64:(hh - g + 1) * 64], Bop, Bop[:, hh * K:(hh + 1) * K], X, X[:, hh, :],
                         acc=(hh > g))
                nh = min(H, g + kpb) - g
                k.cp("dve", self.BhT, self.BhT[:, g:g + nh, :], p_,
                     p_[0:K, 0:nh * 64].rearrange("p (h t) -> p h t", t=64), partial=(g > 0))
            U0r = I["U0rhs"]
            for b in range(nb):
                p_ = pd[b % 2]
                for hh in range(b * hpb, min(H, (b + 1) * hpb)):
                    k.mm(p_, p_[0:64, (hh - b * hpb) * V:(hh - b * hpb + 1) * V], X, X[:, hh, :], U0r,
                         U0r[:, hh * V:(hh + 1) * V], acc=(hh > b * hpb))
                w = min(HV, (b + 1) * 512) - b * 512
                k.cp(evac_eng(), self.U0, self.U0[:, b * 512:b * 512 + w], p_, p_[0:64, 0:w], partial=(b > 0))
            for b in range(nb):
                p_ = m.pU[b]
                for hh in range(b * hpb, min(H, (b + 1) * hpb)):
                    k.mm(p_, p_[0:64, (hh - b * hpb) * V:(hh - b * hpb + 1) * V], self.BhT, self.BhT[:, hh, :], self.Zb,
                         self.Zb[:, hh, :], acc=(hh > b * hpb))
                w = min(HV, (b + 1) * 512) - b * 512
                k.tt("dve", self.U, self.U[:, b * 512:b * 512 + w], p_, p_[0:64, 0:w], self.U0,
                     self.U0[:, b * 512:b * 512 + w], ALU.add, partial=(b > 0))
        RopT = I["RopT"]
        o_t = self.o[0]
        self.no += 1
        for b in range(nb):
            p_ = m.pO[b]
            for hh in range(b * hpb, min(H, (b + 1) * hpb)):
                oap = p_[0:64, (hh - b * hpb) * V:(hh - b * hpb + 1) * V]
                terms = [(RopT, RopT[:, hh, :], self.Zb, self.Zb[:, hh, :])]
                if self.has_u:
                    terms.append((I["PmT"], I["PmT"][:, hh, :], self.U, self.U[:, hh * V:(hh + 1) * V]))
                if self.has_q:
                    terms.append((I["QmT"], I["QmT"][:, hh, :], I["Vtok"], I["Vtok"][:, hh * V:(hh + 1) * V]))
                for ti, (lt, lap, rt, rap) in enumerate(terms):
                    k.mm(p_, oap, lt, lap, rt, rap, start=(ti == 0), stop=(ti == len(terms) - 1),
                         acc=not (hh == b * hpb and ti == 0))
            w = min(HV, (b + 1) * 512) - b * 512
            k.cp(evac_eng(), o_t, o_t[:, b * 512:b * 512 + w], p_, p_[0:64, 0:w], partial=(b > 0))
        zs = I["zs"]
        for b in range(nb):
            p_ = m.pZ[b]
            for hh in range(b * hpb, min(H, (b + 1) * hpb)):
                zap = p_[0:K, (hh - b * hpb) * V:(hh - b * hpb + 1) * V]
                terms = []
                if self.has_u:
                    terms.append((I["Aop"], I["Aop"][:, hh * K:(hh + 1) * K], self.U, self.U[:, hh * V:(hh + 1) * V]))
                if self.has_k2:
                    terms.append((I["Kop"], I["Kop"][:, hh * K:(hh + 1) * K], I["Vtok"], I["Vtok"][:, hh * V:(hh + 1) * V]))
                for ti, (lt, lap, rt, rap) in enumerate(terms):
                    k.mm(p_, zap, lt, lap, rt, rap, start=(ti == 0), stop=(ti == len(terms) - 1),
                         acc=not (hh == b * hpb and ti == 0))
            h0 = b * hpb
            nh = min(H, (b + 1) * hpb) - h0
            zsb = zs[0:K, h0:h0 + nh].unsqueeze(2).to_broadcast([K, nh, V])
            pz = p_[0:K, 0:nh * V].rearrange("p (h v) -> p h v", v=V)
            if self.post_scale:
                k.tt("dve", self.ztmp, self.ztmp[:, h0:h0 + nh, :], p_, pz, self.Z, self.Z[:, h0:h0 + nh, :], ALU.add,
                     partial=(b > 0))
                k.tt("dve", self.Z, self.Z[:, h0:h0 + nh, :], self.ztmp, self.ztmp[:, h0:h0 + nh, :], zs, zsb, ALU.mult,
                     partial=(b > 0))
            else:
                k.tt("dve", self.ztmp, self.ztmp[:, h0:h0 + nh, :], self.Z, self.Z[:, h0:h0 + nh, :], zs, zsb, ALU.mult,
                     partial=(b > 0))
                k.tt("dve", self.Z, self.Z[:, h0:h0 + nh, :], p_, pz, self.ztmp, self.ztmp[:, h0:h0 + nh, :], ALU.add,
                     partial=(b > 0))
            k.cp("act", self.Zb, self.Zb[:, h0:h0 + nh, :], self.Z, self.Z[:, h0:h0 + nh, :], partial=(b > 0))
        return o_t


def mixer_phase(k, cfg, layer, R, mode):
    m = MX()
    m.cfg = cfg
    m.layer = layer
    m.R = R
    m.dr = T(None, "dram_read", const=True)
    m.dw = T(None, "dram_write")
    m.zeros = R["zeros"]
    m.ident = R["ident"]
    m.identb = R["identb"]
    m.masks = R["masks"]
    m.bmask = R["masks"]["bmask"]
    m.identf64 = R["masks"]["identf64"]
    seqs = seq_list(cfg, layer)
    with ExitStack() as P:
        m.pd = [k.ps(P, [128, 512], F32, "pd") for _ in range(2)]
        m.pdb = k.ps(P, [128, 16, 64], BF16, "pdb")
        m.pf = k.ps(P, [128, 512], F32, "pf")
        m.pU = [k.ps(P, [128, 512], F32, "pU") for _ in range(2)]
        m.pO = [k.ps(P, [128, 512], F32, "pO") for _ in range(2)]
        m.pZ = m.pU
        if layer % 2 == 0:
            if mode in ("full", "gla"):
                gla_mixer(k, m, seqs)
            if mode in ("full", "rwkv"):
                rwkv_mixer(k, m, seqs)
            if mode in ("gla", "rwkv"):
                zt = k.sb(P, [128, 1024], BF16, "zt")
                k.memset("dve", zt, zt[:, :], 0.0)
                c0 = 1024 if mode == "gla" else 0
                for tt in range(cfg.ntile):
                    k.ld(m.dw, R["mixd"].ap[tt * 128:(tt + 1) * 128, c0:c0 + 1024], zt, zt[:, :], partial=True)
        else:
            gdn_mixer(k, m, seqs)
    return


def gla_mixer(k, m, seqs):
    cfg, layer, R = m.cfg, m.layer, m.R
    j = layer // 2
    proj = R["proj"].ap
    with ExitStack() as P:
        w2 = k.sb(P, [16, 2, 512], F32, "gw2", const=True)
        k.ld(w2, w2[:, :, :], R["gla_w2"], R["gla_w2"].ap[j].rearrange("d l n -> l d n"))
        gb = k.sb(P, [1, 2, 512], F32, "gb", const=True)
        k.ld(gb, gb[:, :, :], R["gla_b"], R["gla_b"].ap[j:j + 1, :, :])
        gg = k.sb(P, [64, 256], F32, "gg", const=True)
        k.ld(gg, gg[:, :], R["gla_g"], R["gla_g"].ap[j:j + 1, :].to_broadcast([64, 256]))
        ones_f = k.sb(P, [64, 64], F32, "ones_f", const=True)
        k.memset("dve", ones_f, ones_f[:, :], 1.0)
        core = ScanCore(k, m, P, 128, 256, 4, False, True, True, True, "gla")
        NB = 2
        gq = [k.sb(P, [64, 512], F32, "gq") for _ in range(NB)]
        gk = [k.sb(P, [64, 512], F32, "gk") for _ in range(NB)]
        gv = [k.sb(P, [64, 1024], F32, "gv") for _ in range(NB)]
        gl = [k.sb(P, [64, 16], F32, "gl") for _ in range(NB)]
        glT = [k.sb(P, [16, 64], F32, "glT") for _ in range(NB)]
        la = [k.sb(P, [64, 512], F32, "la") for _ in range(NB)]
        eW = [k.sb(P, [64, 512], F32, "eW") for _ in range(NB)]
        eWi = [k.sb(P, [64, 512], F32, "eWi") for _ in range(NB)]
        qtb = [k.sb(P, [64, 512], BF16, "qtb") for _ in range(NB)]
        ktb = [k.sb(P, [64, 512], BF16, "ktb") for _ in range(NB)]
        vt = [k.sb(P, [64, 1024], BF16, "vt") for _ in range(NB)]
        qT = [k.sb(P, [128, 4, 64], BF16, "qT") for _ in range(NB)]
        kT = [k.sb(P, [128, 4, 64], BF16, "kT") for _ in range(NB)]
        QmT = [k.sb(P, [64, 4, 64], BF16, "QmT") for _ in range(NB)]
        zs = [k.sb(P, [128, 4], F32, "zs") for _ in range(NB)]
        ofw = [k.sb(P, [64, 1024], F32, "ofw") for _ in range(NB)]
        gz = [k.sb(P, [64, 1024], F32, "gz") for _ in range(NB)]
        sq = k.sb(P, [64, 1024], F32, "sq")
        ssq = [k.sb(P, [64, 4], F32, "ssq") for _ in range(NB)]
        mo = [k.sb(P, [64, 1024], BF16, "mo") for _ in range(NB)]
        n = 0
        for d in range(2):
            LE = m.masks["LE%d" % d]
            LEf = m.masks["LEf%d" % d]
            work = []
            for sq_ in seqs:
                chs_ = sq_["chunks"] if d == 0 else sq_["chunks"][::-1]
                for ci_, runs_ in enumerate(chs_):
                    work.append((sq_, ci_, len(chs_), runs_))

            def front(runs, b):
                if True:
                    load_rows(k, m, gq[b], lambda a, c, t=gq[b]: t[a:c, :], proj, 0, 512, runs)
                    load_rows(k, m, gk[b], lambda a, c, t=gk[b]: t[a:c, :], proj, 512, 1024, runs)
                    load_rows(k, m, gv[b], lambda a, c, t=gv[b]: t[a:c, :], proj, 1024, 2048, runs)
                    load_rows(k, m, gl[b], lambda a, c, t=gl[b]: t[a:c, :], proj, 3072 + d * 16, 3088 + d * 16, runs)
                    p0 = m.pd[0]
                    k.tr(p0, p0[0:16, 0:64], gl[b], gl[b][:, :], m.ident, m.ident[0:64, 0:64], partial=False)
                    k.cp("act", glT[b], glT[b][:, :], p0, p0[0:16, 0:64])
                    p1 = m.pd[1]
                    k.mm(p1, p1[0:64, :], glT[b], glT[b][:, :], w2, w2[:, d, :], start=True, stop=False)
                    k.mm(p1, p1[0:64, :], ones_f, ones_f[0:1, 0:64], gb, gb[0:1, d, :], start=False, stop=True, acc=True)
                    ckpt()
                    k.act(la[b], la[b][:, :], p1, p1[0:64, :], AF.Exp, scale=-1.0)
                    k.act(la[b], la[b][:, :], la[b], la[b][:, :], AF.Ln, bias=1.0)
                    k.tsc("dve", la[b], la[b][:, :], la[b], la[b][:, :], -1.0 / 16.0, None, ALU.mult)
                    ckpt()
                    k.mm(p0, p0[0:64, :], LEf, LEf[:, :], la[b], la[b][:, :])
                    k.act(eW[b], eW[b][:, :], p0, p0[0:64, :], AF.Exp)
                    k.act(eWi[b], eWi[b][:, :], p0, p0[0:64, :], AF.Exp, scale=-1.0)
                    for hh in range(4):
                        k.mm(p1, p1[:, 2 * hh:2 * hh + 2], la[b], la[b][:, hh * 128:(hh + 1) * 128], ones_f, ones_f[:, 0:2],
                             acc=(hh > 0))
                    k.act(zs[b], zs[b][:, :], p1, p1[:, 0:8].rearrange("p (h t) -> p h t", t=2)[:, :, 0], AF.Exp)
                    ckpt()
                    k.stt(qtb[b], qtb[b][0:64, :], gq[b], gq[b][:, :], 128.0 ** -0.5, eW[b], eW[b][:, :], ALU.mult, ALU.mult)
                    k.tt("dve" if DBG & 64 else "pool", ktb[b], ktb[b][0:64, :], gk[b], gk[b][:, :], eWi[b], eWi[b][:, :], ALU.mult)
                    k.cp("act", vt[b], vt[b][:, :], gv[b], gv[b][:, :])
                    ckpt()
                    pb = m.pO[0] if DBG & 128 else m.pd[1]
                    for hh in range(4):
                        k.mm(pb, pb[:, hh * 64:(hh + 1) * 64], qtb[b], qtb[b][:, hh * 128:(hh + 1) * 128], m.identb,
                             m.identb[0:64, 0:64], acc=(hh > 0))
                    for hh in range(4 if not DBG & 32 else 0):
                        k.mm(pb, pb[:, (4 + hh) * 64:(5 + hh) * 64], ktb[b], ktb[b][:, hh * 128:(hh + 1) * 128], m.identb,
                             m.identb[0:64, 0:64], acc=True)
                    if DBG & 256:
                        ckpt()
                    k.cp("dve", qT[b], qT[b][:, :, :], pb, pb[:, 0:256].rearrange("p (h t) -> p h t", t=64))
                    if DBG & 512:
                        ckpt()
                    k.cp("dve", kT[b], kT[b][:, :, :], pb, pb[:, 256:512].rearrange("p (h t) -> p h t", t=64))
                    ckpt()
                    for hh in range(4):
                        k.mm(p0, p0[0:64, hh * 64:(hh + 1) * 64], kT[b], kT[b][:, hh, :], qT[b], qT[b][:, hh, :], acc=(hh > 0))
                    k.tt("dve", QmT[b], QmT[b][:, :, :], p0, p0[0:64, 0:256].rearrange("p (h t) -> p h t", t=64), LE,
                         LE[0:64, 0:64].unsqueeze(1).to_broadcast([64, 4, 64]), ALU.mult)

            def back(sq_, ci, nch, runs, b):
                if ci == 0:
                    if sq_["kind"] == "sample":
                        core.init_state(R["state_gla"], R["state_gla"].ap[j, d].rearrange("h k v -> k h v"))
                    else:
                        core.init_state()
                if True:
                    o_t = core.step(dict(RopT=qT[b], QmT=QmT[b], Vtok=vt[b], Kop=ktb[b], zs=zs[b]))
                    if d == 0:
                        store_rows(k, m, R["ogla"].ap, 0, 1024, runs, o_t, lambda a, c, t=o_t: t[a:c, :])
                    else:
                        load_rows(k, m, ofw[b], lambda a, c, t=ofw[b]: t[a:c, :], R["ogla"].ap, 0, 1024, runs)
                        load_rows(k, m, gz[b], lambda a, c, t=gz[b]: t[a:c, :], proj, 2048, 3072, runs)
                        k.tt("dve", ofw[b], ofw[b][:, :], ofw[b], ofw[b][:, :], o_t, o_t[:, :], ALU.add)
                        k.tt("dve", sq, sq[:, :], ofw[b], ofw[b][:, :], ofw[b], ofw[b][:, :], ALU.mult)

                        def red(e, o=ssq[b][:, :], i=sq[:, :].rearrange("p (h v) -> p h v", v=256)):
                            return e.tensor_reduce(o, i, AX.X, ALU.add)
                        k.s.op("dve", red, reads=(sq,), writes=(ssq[b],))
                        k.tsc("dve", ssq[b], ssq[b][:, :], ssq[b], ssq[b][:, :], 1.0 / 256.0, EPS, ALU.mult, ALU.add)
                        k.act(ssq[b], ssq[b][:, :], ssq[b], ssq[b][:, :], AF.Sqrt)

                        def rec(e, o=ssq[b][:, :]):
                            return e.reciprocal(o, o)
                        k.s.op("dve", rec, reads=(ssq[b],), writes=(ssq[b],))
                        o3 = ofw[b][:, :].rearrange("p (h v) -> p h v", v=256)
                        k.tt("dve", ofw[b], o3, ofw[b], o3, ssq[b], ssq[b][:, :].unsqueeze(2).to_broadcast([64, 4, 256]), ALU.mult)
                        k.tt("dve", ofw[b], o3, ofw[b], o3, gg, gg[:, :].unsqueeze(1).to_broadcast([64, 4, 256]), ALU.mult)
                        k.act(sq, sq[:, :], gz[b], gz[b][:, :], AF.Sigmoid)
                        k.tt("dve", sq, sq[:, :], sq, sq[:, :], gz[b], gz[b][:, :], ALU.mult)
                        k.tt("dve", mo[b], mo[b][:, :], ofw[b], ofw[b][:, :], sq, sq[:, :], ALU.mult)
                        store_rows(k, m, R["mixd"].ap, 0, 1024, runs, mo[b], lambda a, c, t=mo[b]: t[a:c, :])
                if sq_["kind"] == "prompt" and ci == nch - 1:
                    core.store_state(R["ns_gla"], R["ns_gla"].ap[sq_["idx"], j, d].rearrange("h k v -> k h v"))

            prev = None
            for wi, (sq_, ci, nch, runs) in enumerate(work):
                b = wi % NB
                A = k.s.capture(lambda: front(runs, b))
                B = k.s.capture(lambda: back(*prev)) if prev is not None else []
                k.s.append_merged(A, B)
                prev = (sq_, ci, nch, runs, b)
            B = k.s.capture(lambda: back(*prev))
            k.s.append_merged([], B)
            k.s.barrier()


def core_inputs(inp, core, cfg, consts):
    f = lambda a: np.ascontiguousarray(np.asarray(a, dtype=np.float32))
    ne, no_ = cfg.n_even, max(cfg.n_odd, 1)
    m = {}
    xs = f(inp["x_sample"])[core]
    xp = f(inp["x_prompt"])[core * cfg.np:(core + 1) * cfg.np].reshape(-1, D)
    m["xin"] = np.ascontiguousarray(np.concatenate([xs, xp], axis=0))
    m["cvec"] = np.ascontiguousarray(np.stack([f(inp["c"])[core], f(inp["c_ctx"])]))
    m["state_gla"] = f(inp["state_gla"])[core]
    m["state_rwkv"] = f(inp["state_rwkv"])[core]
    sg = f(inp["state_gdn"])[core]
    m["state_gdn"] = sg if cfg.n_odd > 0 else np.zeros((1, 2, 16, 128, 128), np.float32)
    for nm in ("norm_g", "w_ada", "b_ada", "w_in_even", "w_out", "gla_w2", "gla_b", "gla_g", "rwkv_mu", "rwkv_w0",
               "rwkv_w2", "rwkv_a0", "rwkv_a2", "rwkv_k_k", "rwkv_k_a", "rwkv_gn_g", "rwkv_gn_b"):
        m[nm] = f(inp[nm])
    m["rwkv_r_k"] = f(inp["rwkv_r_k"]).reshape(ne, 1024)
    m["final_g"] = f(inp["final_g"]).reshape(1, D)
    if cfg.n_odd > 0:
        for nm in ("w_in_odd", "gdn_conv_w", "gdn_A_log", "gdn_dt_bias", "gdn_g"):
            m[nm] = f(inp[nm])
    else:
        m["w_in_odd"] = np.zeros((1, D, ODD_IN), np.float32)
        m["gdn_conv_w"] = np.zeros((1, 3, 6144), np.float32)
        m["gdn_A_log"] = np.zeros((1, 2, 16), np.float32)
        m["gdn_dt_bias"] = np.zeros((1, 2, 16), np.float32)
        m["gdn_g"] = np.zeros((1, 128), np.float32)
    m.update(consts)
    return m


C0 = 0.6065306597126334


def hilo_aug(k, P, name, w_t, w_ap64, row_t, row_ap):
    aug = k.sb(P, [128, 1024], BF16, name, const=True)
    k.memset("pool", aug, aug[:, :], 0.0)
    k.ld(aug, aug[64:128, :], w_t, w_ap64, eng="pool", partial=True)
    with ExitStack() as tmp:
        st = k.sb(tmp, [33, 1024], F32, name + "st")
        k.ld(st, st[0:1, :], row_t, row_ap)
        k.ld(st, st[32:33, :], row_t, row_ap, partial=True)
        k.cp("dve", aug, aug[0:1, :], st, st[0:1, :], partial=True)
        k.cp("dve", aug, aug[32:33, :], st, st[32:33, :], partial=True)
        k.tt("dve", aug, aug[32:33, :], st, st[32:33, :], aug, aug[32:33, :], ALU.subtract, partial=True)
    k.s.barrier()
    return aug


def rwkv_mixer(k, m, seqs):
    cfg, layer, R = m.cfg, m.layer, m.R
    j = layer // 2
    proj = R["proj"].ap
    H = 16
    with ExitStack() as P:
        pdb = m.pdb

        def bc(name, t, ap_row, n=1024, stk=None):
            x = k.sb(stk if stk is not None else P, [64, n], F32, name, const=True)
            k.ld(x, x[:, :], t, ap_row.to_broadcast([64, n]))
            return x
        mu = [[bc("mu%d%d" % (i, s_), R["rwkv_mu"], R["rwkv_mu"].ap[j, i, s_:s_ + 1, :]) for s_ in range(2)]
              for i in range(3)]
        kk_bc = bc("kkbc", R["rwkv_k_k"], R["rwkv_k_k"].ap[j:j + 1, :])
        ka_bc = bc("kabc", R["rwkv_k_a"], R["rwkv_k_a"].ap[j:j + 1, :])
        rk_bc = bc("rkbc", R["rwkv_r_k"], R["rwkv_r_k"].ap[j:j + 1, :])
        ones_f = k.sb(P, [64, 2], F32, "ones_f2", const=True)
        k.memset("dve", ones_f, ones_f[:, :], 1.0)
        core = ScanCore(k, m, P, 64, 64, H, True, True, True, True, "rw")
        xr = k.sb(P, [64, 1024], F32, "xr")
        xk = k.sb(P, [64, 1024], F32, "xk")
        xv2 = [k.sb(P, [64, 1024], F32, "xv") for _ in range(2)]
        pv_ = k.sb(P, [64, 1024], F32, "pv")
        nx_ = k.sb(P, [64, 1024], F32, "nx")
        kkt = k.sb(P, [64, 1024], F32, "kkt")
        sig = k.sb(P, [64, 1024], F32, "sig")
        A_ = k.sb(P, [64, 1024], F32, "A_")
        kd = k.sb(P, [64, 1024], F32, "kd")
        S1 = k.sb(P, [64, 1024], F32, "S1")
        S2 = k.sb(P, [64, 1024], F32, "S2")
        S3 = k.sb(P, [64, 1024], F32, "S3")
        ssq = k.sb(P, [64, 16], F32, "rssq")
        bs2 = [k.sb(P, [64, 16], F32, "bs") for _ in range(2)]
        ssqe = k.sb(P, [64, 16], F32, "rssqe")
        bs0 = k.sb(P, [64, 16], F32, "bs0")
        lw = k.sb(P, [64, 64], F32, "lw")
        la_ = k.sb(P, [64, 64], F32, "la")
        lwp = k.sb(P, [64, 128], BF16, "lwp")
        lap = k.sb(P, [64, 128], BF16, "lap")
        for t_ in (lwp, lap):
            k.memset("pool", t_, t_[:, :], 0.0)
            k.memset("pool", t_, t_[:, 0:1], 1.0)
            k.memset("pool", t_, t_[:, 32:33], 1.0)
        lwT = k.sb(P, [128, 64], BF16, "lwT")
        laT = k.sb(P, [128, 64], BF16, "laT")
        zs = k.sb(P, [64, 16], F32, "rzs")
        tl = {nm: k.sb(P, [64, 1024], BF16, "tl" + nm) for nm in ("al", "be", "kt", "rt", "v", "cv")}
        tT = {nm: k.sb(P, [64, H, 64], BF16, "tT" + nm) for nm in ("al", "be", "kt", "rt")}
        st_ = {nm: k.sb(P, [64, H, 64], F32 if nm in ("NN", "NNT") else BF16, "st" + nm)
               for nm in ("NN", "NNT", "CT", "PmT", "QmT")}
        Sst = TV(S2, S2.ap.rearrange("p (h v) -> p h v", v=64))
        for d in range(2):
            with ExitStack() as PD:
                LE, LT, LTo = m.masks["LE%d" % d], m.masks["LT%d" % d], m.masks["LT%d" % (1 - d)]
                LEf = m.masks["LEf%d" % d]
                w2aug = hilo_aug(k, PD, "w2aug", R["rwkv_w2"], R["rwkv_w2"].ap[j, d], R["rwkv_w0"],
                                 R["rwkv_w0"].ap[j, d:d + 1, :])
                a2aug = hilo_aug(k, PD, "a2aug", R["rwkv_a2"], R["rwkv_a2"].ap[j, d], R["rwkv_a0"],
                                 R["rwkv_a0"].ap[j, d:d + 1, :])
                if d == 1:
                    gng = bc("gng", R["rwkv_gn_g"], R["rwkv_gn_g"].ap[j:j + 1, :], stk=PD)
                    gnb = bc("gnb", R["rwkv_gn_b"], R["rwkv_gn_b"].ap[j:j + 1, :], stk=PD)
                    ofw = k.sb(PD, [64, 1024], F32, "rofw")
                    rz = k.sb(PD, [64, 1024], F32, "rz")
                    mo = k.sb(PD, [64, 1024], BF16, "rmo")
                    mean = k.sb(PD, [64, 16], F32, "mean")
                work = []
                for sq_ in seqs:
                    chs_ = sq_["chunks"] if d == 0 else sq_["chunks"][::-1]
                    for ci_, runs_ in enumerate(chs_):
                        work.append((sq_, ci_, len(chs_), runs_))

                def init_state(sq_):
                    if sq_["kind"] == "sample":
                        k.ld(Sst, Sst[:, :, :], R["state_rwkv"], R["state_rwkv"].ap[j, d].rearrange("h v k -> v h k"))
                        for g in range(2):
                            p_ = m.pd[g]
                            for hh in range(8):
                                k.tr(p_, p_[0:64, hh * 64:(hh + 1) * 64], Sst, Sst[:, g * 8 + hh, :], m.ident,
                                     m.ident[0:64, 0:64], partial=(hh > 0))
                            k.cp("dve", core.Z, core.Z[:, g * 8:(g + 1) * 8, :], p_,
                                 p_[0:64, :].rearrange("p (h v) -> p h v", v=64), partial=(g > 0))
                        k.cp("act", core.Zb, core.Zb[:, :, :], core.Z, core.Z[:, :, :])
                    else:
                        core.init_state()

                def partA(runs, bsi):
                    xv, bs = xv2[bsi], bs2[bsi]
                    if True:
                        for qi, (cur, c0) in enumerate(((xr, 3104), (xk, 4128), (xv, 5152))):
                            load_rows(k, m, cur, lambda a, c, t=cur: t[a:c, :], proj, c0, c0 + 1024, runs, 0)
                            load_rows(k, m, pv_, lambda a, c, t=pv_: t[a:c, :], proj, c0, c0 + 1024, runs, -1)
                            load_rows(k, m, nx_, lambda a, c, t=nx_: t[a:c, :], proj, c0, c0 + 1024, runs, +1)
                            k.tt("dve", pv_, pv_[:, :], pv_, pv_[:, :], cur, cur[:, :], ALU.subtract)
                            k.tt("dve", pv_, pv_[:, :], pv_, pv_[:, :], mu[qi][0], mu[qi][0][:, :], ALU.mult)
                            k.tt("dve", nx_, nx_[:, :], nx_, nx_[:, :], cur, cur[:, :], ALU.subtract)
                            k.tt("dve", nx_, nx_[:, :], nx_, nx_[:, :], mu[qi][1], mu[qi][1][:, :], ALU.mult)
                            k.tt("dve", cur, cur[:, :], cur, cur[:, :], pv_, pv_[:, :], ALU.add)
                            k.tt("dve", cur, cur[:, :], cur, cur[:, :], nx_, nx_[:, :], ALU.add)
                        load_rows(k, m, lw, lambda a, c, t=lw: t[a:c, :], proj, 7200 + d * 64, 7264 + d * 64, runs, 0)
                        load_rows(k, m, la_, lambda a, c, t=la_: t[a:c, :], proj, 7328 + d * 64, 7392 + d * 64, runs, 0)
                        k.tt("dve", kkt, kkt[:, :], xk, xk[:, :], kk_bc, kk_bc[:, :], ALU.mult)
                        k.tt("dve", S1, S1[:, :], kkt, kkt[:, :], kkt, kkt[:, :], ALU.mult)

                        def red(e, o=ssq[:, :], i=S1[:, :].rearrange("p (h v) -> p h v", v=64)):
                            return e.tensor_reduce(o, i, AX.X, ALU.add)
                        k.s.op("dve", red, reads=(S1,), writes=(ssq,))
                        k.tsc("dve", ssq, ssq[:, :], ssq, ssq[:, :], EPS, None, ALU.add)
                        k.act(ssq, ssq[:, :], ssq, ssq[:, :], AF.Sqrt)

                        def rec(e, o=ssq[:, :]):
                            return e.reciprocal(o, o)
                        k.s.op("dve", rec, reads=(ssq,), writes=(ssq,))
                        kk3 = kkt[:, :].rearrange("p (h v) -> p h v", v=64)
                        k.tt("dve", kkt, kk3, kkt, kk3, ssq, ssq[:, :].unsqueeze(2).to_broadcast([64, 16, 64]), ALU.mult)
                        k.act(lwp, lwp[:, 64:128], lw, lw[:, :], AF.Tanh, partial=True)
                        k.cp("act", lap, lap[:, 64:128], la_, la_[:, :], partial=True)
                        p0 = p1 = m.pf
                        k.mm(p0, p0[:, 0:64], lwp, lwp[:, :], m.identb, m.identb[0:64, 0:64])
                        k.mm(p0, p0[:, 64:128], lap, lap[:, :], m.identb, m.identb[0:64, 0:64], acc=True)
                        k.cp("dve", lwT, lwT[:, :], p0, p0[:, 0:64])
                        k.cp("dve", laT, laT[:, :], p0, p0[:, 64:128])
                        for hb in range(2):
                            p_ = m.pf
                            k.mm(p_, p_[0:64, :], lwT, lwT[:, :], w2aug, w2aug[:, hb * 512:(hb + 1) * 512])
                            k.act(sig, sig[:, hb * 512:(hb + 1) * 512], p_, p_[0:64, :], AF.Sigmoid, partial=(hb > 0))
                        for hb in range(2):
                            p_ = m.pf
                            k.mm(p_, p_[0:64, :], laT, laT[:, :], a2aug, a2aug[:, hb * 512:(hb + 1) * 512])
                            k.act(A_, A_[:, hb * 512:(hb + 1) * 512], p_, p_[0:64, :], AF.Sigmoid, partial=(hb > 0))
                        k.stt(S1, S1[:, :], A_, A_[:, :], -1.0, ka_bc, ka_bc[:, :], ALU.add, ALU.mult)
                        k.stt(kd, kd[:, :], S1, S1[:, :], 1.0, xk, xk[:, :], ALU.add, ALU.mult)
                        k.tt("dve", S1, S1[:, :], xr, xr[:, :], kd, kd[:, :], ALU.mult)
                        k.tt("dve", S1, S1[:, :], S1, S1[:, :], rk_bc, rk_bc[:, :], ALU.mult)

                        def red2(e, o=bs[:, :], i=S1[:, :].rearrange("p (h v) -> p h v", v=64)):
                            return e.tensor_reduce(o, i, AX.X, ALU.add)
                        k.s.op("dve", red2, reads=(S1,), writes=(bs,))

                def partB(runs, bsi):
                    xv, bs = xv2[bsi], bs2[bsi]
                    if True:
                        for hb in range(2):
                            p_ = m.pd[hb]
                            k.mm(p_, p_[0:64, :], LEf, LEf[:, :], sig, sig[:, hb * 512:(hb + 1) * 512])
                            sl = slice(hb * 512, (hb + 1) * 512)
                            k.act(S2, S2[:, sl], p_, p_[0:64, :], AF.Exp, scale=-C0, partial=(hb > 0))
                            k.act(S3, S3[:, sl], p_, p_[0:64, :], AF.Exp, scale=C0, partial=(hb > 0))
                            k.tt("dve", S1, S1[:, sl], p_, p_[0:64, :], sig, sig[:, sl], ALU.subtract, partial=(hb > 0))
                        k.tt("dve", tl["rt"], tl["rt"][:, :], xr, xr[:, :], S2, S2[:, :], ALU.mult)
                        k.tt("dve", tl["kt"], tl["kt"][:, :], kd, kd[:, :], S3, S3[:, :], ALU.mult)
                        k.tt("dve", S2, S2[:, :], kkt, kkt[:, :], A_, A_[:, :], ALU.mult)
                        k.stt(tl["al"], tl["al"][:, :], S2, S2[:, :], -1.0, S3, S3[:, :], ALU.mult, ALU.mult)
                        k.act(S1, S1[:, :], S1, S1[:, :], AF.Exp, scale=-C0)
                        k.tt("dve", tl["be"], tl["be"][:, :], kkt, kkt[:, :], S1, S1[:, :], ALU.mult)
                        k.cp("act", tl["v"], tl["v"][:, :], xv, xv[:, :])
                        p_ = m.pd[0]
                        for hh in range(H):
                            k.mm(p_, p_[0:64, 2 * hh:2 * hh + 2], sig, sig[:, hh * 64:(hh + 1) * 64], ones_f, ones_f[:, :],
                                 acc=(hh > 0))
                        k.act(zs, zs[:, :], p_, p_[0:64, 0:32].rearrange("p (h t) -> p h t", t=2)[:, :, 0], AF.Exp,
                              scale=-C0)
                        for nm in ("al", "be", "kt", "rt"):
                            for hh in range(H):
                                k.tr(pdb, pdb[0:64, hh, :], tl[nm], tl[nm][:, hh * 64:(hh + 1) * 64], m.identb,
                                     m.identb[0:64, 0:64], partial=(hh > 0))
                            k.cp("act", tT[nm], tT[nm][:, :, :], pdb, pdb[0:64, :, :])
                        prods = (("NN", "al", "be", LT), ("NNT", "be", "al", LTo), ("CT", "kt", "be", LT),
                                 ("PmT", "al", "rt", LE), ("QmT", "kt", "rt", LE))
                        for (dn, ln, rn, mk) in prods:
                            for g in range(2):
                                p_ = m.pd[g]
                                for hh in range(8):
                                    h_ = g * 8 + hh
                                    k.mm(p_, p_[0:64, hh * 64:(hh + 1) * 64], tT[ln], tT[ln][:, h_, :], tT[rn],
                                         tT[rn][:, h_, :], acc=(hh > 0))
                                k.tt("dve", st_[dn], st_[dn][:, g * 8:(g + 1) * 8, :], p_,
                                     p_[0:64, :].rearrange("p (h t) -> p h t", t=64), mk,
                                     mk[0:64, 0:64].unsqueeze(1).to_broadcast([64, 8, 64]), ALU.mult, partial=(g > 0))
                        for g in range(2):
                            p_ = m.pd[g]
                            for hh in range(8):
                                h_ = g * 8 + hh
                                k.mm(p_, p_[0:64, hh * 64:(hh + 1) * 64], st_["CT"], st_["CT"][:, h_, :], tl["v"],
                                     tl["v"][:, h_ * 64:(h_ + 1) * 64], acc=(hh > 0))
                            k.cp("dve", tl["cv"], tl["cv"][:, g * 512:(g + 1) * 512], p_, p_[0:64, :], partial=(g > 0))

                def back(sq_, ci, nch, runs, bsi):
                    xv, bs = xv2[bsi], bs2[bsi]
                    S1, ssq = S2, ssqe

                    def rec(e, o=ssq[:, :]):
                        return e.reciprocal(o, o)
                    if ci == 0:
                        init_state(sq_)
                    if True:
                        o_t = core.step(dict(NN=st_["NN"], NNT=st_["NNT"], Bop=tl["be"], U0rhs=tl["cv"], RopT=tT["rt"],
                                             PmT=st_["PmT"], QmT=st_["QmT"], Vtok=tl["v"], Aop=tl["al"], Kop=tl["kt"],
                                             zs=zs))
                        if d == 0:
                            store_rows(k, m, R["orw"].ap, 0, 1024, runs, o_t, lambda a, c, t=o_t: t[a:c, :])
                            store_rows(k, m, R["bsum"].ap, 0, 16, runs, bs, lambda a, c, t=bs: t[a:c, :])
                        else:
                            load_rows(k, m, ofw, lambda a, c, t=ofw: t[a:c, :], R["orw"].ap, 0, 1024, runs)
                            load_rows(k, m, bs0, lambda a, c, t=bs0: t[a:c, :], R["bsum"].ap, 0, 16, runs)
                            load_rows(k, m, rz, lambda a, c, t=rz: t[a:c, :], proj, 6176, 7200, runs)
                            k.tt("dve", ofw, ofw[:, :], ofw, ofw[:, :], o_t, o_t[:, :], ALU.add)
                            k.tt("dve", bs0, bs0[:, :], bs0, bs0[:, :], bs, bs[:, :], ALU.add)
                            o3 = ofw[:, :].rearrange("p (h v) -> p h v", v=64)

                            def red3(e, o=mean[:, :], i=o3):
                                return e.tensor_reduce(o, i, AX.X, ALU.add)
                            k.s.op("dve", red3, reads=(ofw,), writes=(mean,))
                            k.tsc("dve", mean, mean[:, :], mean, mean[:, :], 1.0 / 64.0, None, ALU.mult)
                            k.tt("dve", ofw, o3, ofw, o3, mean, mean[:, :].unsqueeze(2).to_broadcast([64, 16, 64]),
                                 ALU.subtract)
                            k.tt("dve", S1, S1[:, :], ofw, ofw[:, :], ofw, ofw[:, :], ALU.mult)

                            def red4(e, o=ssq[:, :], i=S1[:, :].rearrange("p (h v) -> p h v", v=64)):
                                return e.tensor_reduce(o, i, AX.X, ALU.add)
                            k.s.op("dve", red4, reads=(S1,), writes=(ssq,))
                            k.tsc("dve", ssq, ssq[:, :], ssq, ssq[:, :], 1.0 / 64.0, GN_EPS, ALU.mult, ALU.add)
                            k.act(ssq, ssq[:, :], ssq, ssq[:, :], AF.Sqrt)
                            k.s.op("dve", rec, reads=(ssq,), writes=(ssq,))
                            k.tt("dve", ofw, o3, ofw, o3, ssq, ssq[:, :].unsqueeze(2).to_broadcast([64, 16, 64]), ALU.mult)
                            k.tt("dve", ofw, ofw[:, :], ofw, ofw[:, :], gng, gng[:, :], ALU.mult)
                            k.tt("dve", ofw, ofw[:, :], ofw, ofw[:, :], gnb, gnb[:, :], ALU.add)
                            v3 = xv[:, :].rearrange("p (h v) -> p h v", v=64)
                            k.tt("dve", S1, S1[:, :].rearrange("p (h v) -> p h v", v=64), xv, v3, bs0,
                                 bs0[:, :].unsqueeze(2).to_broadcast([64, 16, 64]), ALU.mult)
                            k.tt("dve", ofw, ofw[:, :], ofw, ofw[:, :], S1, S1[:, :], ALU.add)
                            k.act(S1, S1[:, :], rz, rz[:, :], AF.Sigmoid)
                            k.tt("dve", S1, S1[:, :], S1, S1[:, :], rz, rz[:, :], ALU.mult)
                            k.tt("dve", mo, mo[:, :], ofw, ofw[:, :], S1, S1[:, :], ALU.mult)
                            store_rows(k, m, R["mixd"].ap, 1024, 2048, runs, mo, lambda a, c, t=mo: t[a:c, :])
                    if sq_["kind"] == "prompt" and ci == nch - 1:
                        for g in range(2):
                            p_ = m.pd[g]
                            for hh in range(8):
                                k.tr(p_, p_[0:64, hh * 64:(hh + 1) * 64], core.Z, core.Z[:, g * 8 + hh, :], m.ident,
                                     m.ident[0:64, 0:64], partial=(hh > 0))
                            k.cp("dve", Sst, Sst[:, g * 8:(g + 1) * 8, :], p_,
                                 p_[0:64, :].rearrange("p (h v) -> p h v", v=64), partial=(g > 0))
                        k.ld(R["ns_rwkv"], R["ns_rwkv"].ap[sq_["idx"], j, d].rearrange("h v k -> v h k"), Sst,
                             Sst[:, :, :], partial=True)

                partA(work[0][3], 0)
                partB(work[0][3], 0)
                for wi, (sq_, ci, nch, runs) in enumerate(work):
                    bsi = wi % 2
                    nxt = work[wi + 1] if wi + 1 < len(work) else None
                    A = k.s.capture(lambda: partA(nxt[3], (wi + 1) % 2)) if nxt is not None else []
                    B = k.s.capture(lambda: back(sq_, ci, nch, runs, bsi))
                    k.s.append_merged(A, B)
                    if nxt is not None:
                        partB(nxt[3], (wi + 1) % 2)
                k.s.barrier()


def gdn_mixer(k, m, seqs):
    cfg, layer, R = m.cfg, m.layer, m.R
    j = layer // 2
    proj = R["proj"].ap
    HG = 8
    with ExitStack() as P:
        pdb = m.pdb
        ones_f = k.sb(P, [64, 128], F32, "g_ones", const=True)
        k.memset("dve", ones_f, ones_f[:, :], 1.0)
        gg = k.sb(P, [64, 128], F32, "gdng", const=True)
        k.ld(gg, gg[:, :], R["gdn_g"], R["gdn_g"].ap[j:j + 1, :].to_broadcast([64, 128]))
        core = ScanCore(k, m, P, 128, 128, HG, True, False, False, False, "gd")
        cur = [k.sb(P, [64, 1024], F32, "gcur%d" % i) for i in range(3)]
        pv_ = k.sb(P, [64, 1024], F32, "gpv")
        nx_ = k.sb(P, [64, 1024], F32, "gnx")
        S1 = k.sb(P, [64, 1024], F32, "gS1")
        ssq = k.sb(P, [64, 8], F32, "gssq")
        bl = k.sb(P, [64, 8], F32, "gbl")
        al = k.sb(P, [64, 8], F32, "gal")
        beta = k.sb(P, [64, 8], F32, "gbeta")
        gt = k.sb(P, [64, 8], F32, "ggt")
        gc = k.sb(P, [64, 8], F32, "ggc")
        egc = k.sb(P, [64, 8], F32, "gegc")
        coef = k.sb(P, [64, 8], F32, "gcoef")
        edec = k.sb(P, [64, 8], F32, "gedec")
        zs2 = [k.sb(P, [128, 8], F32, "gzs") for _ in range(2)]
        gLE = k.sb(P, [64, 8, 64], F32, "gLE")
        gLT = k.sb(P, [64, 8, 64], F32, "gLT")
        EMs = k.sb(P, [64, 8, 64], F32, "EMs")
        EMe = k.sb(P, [64, 8, 64], F32, "EMe")
        EMt = k.sb(P, [64, 8, 64], F32, "EMt")
        tl2 = [{nm: k.sb(P, [64, 1024], BF16, "gtl" + nm) for nm in ("k", "kb", "q", "qe", "bop", "bv", "aop")}
               for _ in range(2)]
        tT2 = [{nm: k.sb(P, [128, HG, 64], BF16, "gtT" + nm) for nm in ("k", "kb", "q", "qe")} for _ in range(2)]
        st2 = [{nm: k.sb(P, [64, HG, 64], F32 if nm in ("NN", "NNT") else BF16, "gst" + nm)
                for nm in ("NN", "NNT", "PmT")} for _ in range(2)]
        S1e = k.sb(P, [64, 1024], F32, "gS1e")
        ssqe = k.sb(P, [64, 8], F32, "gssqe")
        for d in range(2):
            LE, LT, LTo = m.masks["LE%d" % d], m.masks["LT%d" % d], m.masks["LT%d" % (1 - d)]
            LEf, LTof = m.masks["LEf%d" % d], m.masks["LTf%d" % (1 - d)]
            for g in range(2):
                with ExitStack() as PD:
                    cw = [[k.sb(PD, [64, 1024], F32, "cw%d%d" % (qi, tp), const=True) for tp in range(3)] for qi in range(3)]
                    for qi in range(3):
                        for tp in range(3):
                            c0 = qi * 2048 + g * 1024
                            k.ld(cw[qi][tp], cw[qi][tp][:, :], R["gdn_conv_w"],
                                 R["gdn_conv_w"].ap[j, tp:tp + 1, c0:c0 + 1024].to_broadcast([64, 1024]))
                    negA = k.sb(PD, [64, 8], F32, "negA", const=True)
                    dtb = k.sb(PD, [64, 8], F32, "dtb", const=True)
                    k.ld(negA, negA[:, :], R["gdn_A_log"], R["gdn_A_log"].ap[j, d:d + 1, g * 8:(g + 1) * 8].to_broadcast([64, 8]))
                    k.ld(dtb, dtb[:, :], R["gdn_dt_bias"],
                         R["gdn_dt_bias"].ap[j, d:d + 1, g * 8:(g + 1) * 8].to_broadcast([64, 8]))
                    k.act(negA, negA[:, :], negA, negA[:, :], AF.Exp)
                    k.tsc("dve", negA, negA[:, :], negA, negA[:, :], -1.0, None, ALU.mult)
                    if d == 1:
                        ofw = k.sb(PD, [64, 1024], F32, "gofw")
                        zt = k.sb(PD, [64, 1024], F32, "gz")
                        mo = k.sb(PD, [64, 1024], BF16, "gmo")
                    work = []
                    for sq_ in seqs:
                        chs = sq_["chunks"] if d == 0 else sq_["chunks"][::-1]
                        for ci, runs in enumerate(chs):
                            work.append((sq_, ci, len(chs), runs))
                    pend = [None]
                    nb_ = [0]

                    def front(runs, bs):
                        tl, tT, st_, zs = tl2[bs], tT2[bs], st2[bs], zs2[bs]
                        if True:
                            for qi in range(3):
                                c0 = qi * 2048 + g * 1024
                                X = cur[qi]
                                load_rows(k, m, X, lambda a, c, t=X: t[a:c, :], proj, c0, c0 + 1024, runs, 0)
                                load_rows(k, m, pv_, lambda a, c, t=pv_: t[a:c, :], proj, c0, c0 + 1024, runs, -1)
                                load_rows(k, m, nx_, lambda a, c, t=nx_: t[a:c, :], proj, c0, c0 + 1024, runs, +1)
                                k.tt("dve", X, X[:, :], X, X[:, :], cw[qi][1], cw[qi][1][:, :], ALU.mult)
                                k.tt("dve", pv_, pv_[:, :], pv_, pv_[:, :], cw[qi][0], cw[qi][0][:, :], ALU.mult)
                                k.tt("dve", nx_, nx_[:, :], nx_, nx_[:, :], cw[qi][2], cw[qi][2][:, :], ALU.mult)
                                k.tt("dve", X, X[:, :], X, X[:, :], pv_, pv_[:, :], ALU.add)
                                k.tt("dve", X, X[:, :], X, X[:, :], nx_, nx_[:, :], ALU.add)
                                k.act(S1, S1[:, :], X, X[:, :], AF.Sigmoid)
                                k.tt("dve", X, X[:, :], X, X[:, :], S1, S1[:, :], ALU.mult)
                            load_rows(k, m, bl, lambda a, c, t=bl: t[a:c, :], proj, 8192 + d * 16 + g * 8,
                                      8192 + d * 16 + g * 8 + 8, runs, 0)
                            load_rows(k, m, al, lambda a, c, t=al: t[a:c, :], proj, 8224 + d * 16 + g * 8,
                                      8224 + d * 16 + g * 8 + 8, runs, 0)
                            for qi, scl in ((0, 128.0 ** -0.5), (1, 1.0)):
                                X = cur[qi]
                                k.tt("dve", S1, S1[:, :], X, X[:, :], X, X[:, :], ALU.mult)

                                def red(e, o=ssq[:, :], i=S1[:, :].rearrange("p (h v) -> p h v", v=128)):
                                    return e.tensor_reduce(o, i, AX.X, ALU.add)
                                k.s.op("dve", red, reads=(S1,), writes=(ssq,))
                                k.tsc("dve", ssq, ssq[:, :], ssq, ssq[:, :], EPS, None, ALU.add)
                                k.act(ssq, ssq[:, :], ssq, ssq[:, :], AF.Sqrt)

                                def rec(e, o=ssq[:, :]):
                                    return e.reciprocal(o, o)
                                k.s.op("dve", rec, reads=(ssq,), writes=(ssq,))
                                if scl != 1.0:
                                    k.tsc("dve", ssq, ssq[:, :], ssq, ssq[:, :], scl, None, ALU.mult)
                                x3 = X[:, :].rearrange("p (h v) -> p h v", v=128)
                                k.tt("dve", X, x3, X, x3, ssq, ssq[:, :].unsqueeze(2).to_broadcast([64, 8, 128]), ALU.mult)
                            Q, Kk, Vv = cur[0], cur[1], cur[2]
                            k.act(beta, beta[:, :], bl, bl[:, :], AF.Sigmoid)
                            k.tt("dve", gt, gt[:, :], al, al[:, :], dtb, dtb[:, :], ALU.add)
                            k.act(gt, gt[:, :], gt, gt[:, :], AF.Exp)
                            k.act(gt, gt[:, :], gt, gt[:, :], AF.Ln, bias=1.0)
                            k.tt("dve", gt, gt[:, :], gt, gt[:, :], negA, negA[:, :], ALU.mult)
                            p0 = p1 = m.pf
                            k.mm(p0, p0[0:64, 0:8], LEf, LEf[:, :], gt, gt[:, :])
                            k.mm(p0, p0[0:64, 8:16], LTof, LTof[:, :], gt, gt[:, :], acc=True)
                            k.mm(p0, p0[:, 16:24], ones_f, ones_f[:, :], gt, gt[:, :], acc=True)
                            k.act(egc, egc[:, :], p0, p0[0:64, 0:8], AF.Exp)
                            k.act(edec, edec[:, :], p0, p0[0:64, 8:16], AF.Exp)
                            k.act(zs, zs[:, :], p0, p0[:, 16:24], AF.Exp)
                            k.tt("dve", gLE, gLE[:, :, :], m.masks["LEf%d" % d],
                                 m.masks["LEf%d" % d][0:64, 0:64].unsqueeze(1).to_broadcast([64, 8, 64]), gt,
                                 gt[:, :].unsqueeze(2).to_broadcast([64, 8, 64]), ALU.mult)
                            k.tt("dve", gLT, gLT[:, :, :], LTof, LTof[0:64, 0:64].unsqueeze(1).to_broadcast([64, 8, 64]), gt,
                                 gt[:, :].unsqueeze(2).to_broadcast([64, 8, 64]), ALU.mult)
                            k.mm(p1, p1[0:64, :], LTof, LTof[:, :], gLE, gLE[:, :, :].rearrange("p h t -> p (h t)"))
                            k.act(EMe, EMe[:, :, :], p1, p1[0:64, :].rearrange("p (h t) -> p h t", t=64), AF.Exp)
                            k.mm(p1, p1[0:64, :], LEf, LEf[:, :], gLT, gLT[:, :, :].rearrange("p h t -> p (h t)"))
                            k.act(EMt, EMt[:, :, :], p1, p1[0:64, :].rearrange("p (h t) -> p h t", t=64), AF.Exp)
                            k.tt("dve", EMs, EMs[:, :, :], EMe, EMe[:, :, :], m.masks["LTf%d" % d],
                                 m.masks["LTf%d" % d][0:64, 0:64].unsqueeze(1).to_broadcast([64, 8, 64]), ALU.mult)
                            k.tt("dve", EMe, EMe[:, :, :], EMe, EMe[:, :, :], LEf,
                                 LEf[0:64, 0:64].unsqueeze(1).to_broadcast([64, 8, 64]), ALU.mult)
                            k.tt("dve", EMt, EMt[:, :, :], EMt, EMt[:, :, :], LTof,
                                 LTof[0:64, 0:64].unsqueeze(1).to_broadcast([64, 8, 64]), ALU.mult)
                            k3 = Kk[:, :].rearrange("p (h v) -> p h v", v=128)
                            q3 = Q[:, :].rearrange("p (h v) -> p h v", v=128)
                            v3 = Vv[:, :].rearrange("p (h v) -> p h v", v=128)

                            def t3(nm):
                                return tl[nm][:, :].rearrange("p (h v) -> p h v", v=128)

                            def bcs(t_):
                                return t_[:, :].unsqueeze(2).to_broadcast([64, 8, 128])
                            k.cp("act", tl["k"], tl["k"][:, :], Kk, Kk[:, :])
                            k.cp("act", tl["q"], tl["q"][:, :], Q, Q[:, :])
                            k.tt("dve", tl["kb"], t3("kb"), Kk, k3, beta, bcs(beta), ALU.mult)
                            k.tt("dve", tl["bv"], t3("bv"), Vv, v3, beta, bcs(beta), ALU.mult)
                            k.tt("dve", tl["qe"], t3("qe"), Q, q3, egc, bcs(egc), ALU.mult)
                            k.tt("dve", tl["aop"], t3("aop"), Kk, k3, edec, bcs(edec), ALU.mult)
                            k.stt(coef, coef[:, :], beta, beta[:, :], -1.0, egc, egc[:, :], ALU.mult, ALU.mult)
                            k.tt("dve", tl["bop"], t3("bop"), Kk, k3, coef, bcs(coef), ALU.mult)
                            for ti_, nm in enumerate(("k", "kb", "q", "qe")):
                                for hh in range(HG):
                                    k.tr(pdb, pdb[:, (ti_ % 2) * 8 + hh, :], tl[nm], tl[nm][:, hh * 128:(hh + 1) * 128],
                                         m.identb, m.identb[0:64, 0:64], partial=not (hh == 0 and ti_ % 2 == 0))
                                k.cp("act", tT[nm], tT[nm][:, :, :], pdb, pdb[:, (ti_ % 2) * 8:(ti_ % 2) * 8 + 8, :])
                            for (dn, ln, rn, em, sgn) in (("NN", "k", "kb", EMs, -1.0), ("NNT", "kb", "k", EMt, -1.0),
                                                          ("PmT", "k", "q", EMe, 1.0)):
                                p_ = m.pf
                                for hh in range(HG):
                                    k.mm(p_, p_[0:64, hh * 64:(hh + 1) * 64], tT[ln], tT[ln][:, hh, :], tT[rn], tT[rn][:, hh, :],
                                         acc=(hh > 0))
                                k.stt(st_[dn], st_[dn][:, :, :], p_, p_[0:64, :].rearrange("p (h t) -> p h t", t=64), sgn,
                                      em, em[:, :, :], ALU.mult, ALU.mult)

                    def back(sq_, ci, nch, runs, bs):
                        tl, tT, st_, zs = tl2[bs], tT2[bs], st2[bs], zs2[bs]
                        S1, ssq = S1e, ssqe
                        if ci == 0:
                            if sq_["kind"] == "sample":
                                core.init_state(R["state_gdn"],
                                                R["state_gdn"].ap[j, d, g * 8:(g + 1) * 8].rearrange("h k v -> k h v"))
                            else:
                                core.init_state()
                        if True:
                            o_t = core.step(dict(NN=st_["NN"], NNT=st_["NNT"], Bop=tl["bop"], U0rhs=tl["bv"], RopT=tT["qe"],
                                                 PmT=st_["PmT"], Aop=tl["aop"], zs=zs))
                            oc0 = g * 1024
                            if d == 0:
                                store_rows(k, m, R["ogd"].ap, oc0, oc0 + 1024, runs, o_t, lambda a, c, t=o_t: t[a:c, :])
                            else:
                                load_rows(k, m, ofw, lambda a, c, t=ofw: t[a:c, :], R["ogd"].ap, oc0, oc0 + 1024, runs)
                                load_rows(k, m, zt, lambda a, c, t=zt: t[a:c, :], proj, 6144 + oc0, 6144 + oc0 + 1024, runs)
                                k.tt("dve", ofw, ofw[:, :], ofw, ofw[:, :], o_t, o_t[:, :], ALU.add)
                                k.tt("dve", S1, S1[:, :], ofw, ofw[:, :], ofw, ofw[:, :], ALU.mult)

                                def red5(e, o=ssq[:, :], i=S1[:, :].rearrange("p (h v) -> p h v", v=128)):
                                    return e.tensor_reduce(o, i, AX.X, ALU.add)
                                k.s.op("dve", red5, reads=(S1,), writes=(ssq,))
                                k.tsc("dve", ssq, ssq[:, :], ssq, ssq[:, :], 1.0 / 128.0, EPS, ALU.mult, ALU.add)
                                k.act(ssq, ssq[:, :], ssq, ssq[:, :], AF.Sqrt)

                                def rec2(e, o=ssq[:, :]):
                                    return e.reciprocal(o, o)
                                k.s.op("dve", rec2, reads=(ssq,), writes=(ssq,))
                                o3 = ofw[:, :].rearrange("p (h v) -> p h v", v=128)
                                k.tt("dve", ofw, o3, ofw, o3, ssq, ssq[:, :].unsqueeze(2).to_broadcast([64, 8, 128]), ALU.mult)
                                k.tt("dve", ofw, o3, ofw, o3, gg, gg[:, :].unsqueeze(1).to_broadcast([64, 8, 128]), ALU.mult)
                                k.act(S1, S1[:, :], zt, zt[:, :], AF.Sigmoid)
                                k.tt("dve", S1, S1[:, :], S1, S1[:, :], zt, zt[:, :], ALU.mult)
                                k.tt("dve", mo, mo[:, :], ofw, ofw[:, :], S1, S1[:, :], ALU.mult)
                                store_rows(k, m, R["mixd"].ap, oc0, oc0 + 1024, runs, mo, lambda a, c, t=mo: t[a:c, :])
                        if sq_["kind"] == "prompt" and ci == nch - 1:
                            core.store_state(R["ns_gdn"],
                                             R["ns_gdn"].ap[sq_["idx"], j, d, g * 8:(g + 1) * 8].rearrange("h k v -> k h v"))

                    prev = None
                    for wi, (sq_, ci, nch, runs) in enumerate(work):
                        bs = wi % 2
                        A = k.s.capture(lambda: front(runs, bs))
                        B = k.s.capture(lambda: back(*prev)) if prev is not None else []
                        k.s.append_merged(A, B)
                        prev = (sq_, ci, nch, runs, bs)
                    B = k.s.capture(lambda: back(*prev))
                    k.s.append_merged([], B)
                    k.s.barrier()


_CACHE = {}


def kernel(**inputs):
    cfg = Cfg(depth=4, ts=4096, np_=4, tp=256)
    n = 8
    if "kb" not in _CACHE:
        _CACHE["kb"] = build(cfg)
    kb = _CACHE["kb"]
    consts = make_consts()
    in_maps = [core_inputs(inputs, c, cfg, consts) for c in range(n)]
    res = run_bass_kernel_spmd(kb.nc, in_maps, core_ids=list(range(n))).results
    y_sample = np.stack([res[c]["y"][:cfg.ts] for c in range(n)]).astype(np.float32)
    y_prompt = np.concatenate([res[c]["y"][cfg.ts:].reshape(cfg.np, cfg.tp, D) for c in range(n)]).astype(np.float32)
    ns_gla = np.concatenate([res[c]["ns_gla"] for c in range(n)]).astype(np.float32)
    ns_rwkv = np.concatenate([res[c]["ns_rwkv"] for c in range(n)]).astype(np.float32)
    ns_gdn = np.concatenate([res[c]["ns_gdn"] for c in range(n)]).astype(np.float32)
    return (y_prompt, y_sample, ns_gla, ns_rwkv, ns_gdn)
```
